# Optimizing a Trainium2 kernel written in Bass

```python
import math
import jax
import jax.numpy as jnp
from jax import lax
import numpy as np

D_MODEL = 1024
BATCH = 2
SEQ = 8192
DEPTH = 4

D_MIX = D_MODEL
D_CONV = D_MIX // 2
CONV_GROUPS = 8
D_GLA = D_MIX - D_CONV
GLA_HEADS = 4
HEAD_V = D_GLA // GLA_HEADS
HEAD_K = HEAD_V // 2
D_GLA_K = GLA_HEADS * HEAD_K
GATE_RANK = 16
GATE_NORMALIZER = 16.0
CHUNK = 64
D_FF = 2816
CONV_WIDTH = 3
EPS = 1e-6
SPLIT_SIZES = (D_CONV, D_CONV, D_CONV,
               D_GLA_K, D_GLA_K, D_GLA, D_GLA,
               GATE_RANK, GATE_RANK)
D_IN = sum(SPLIT_SIZES)

kernel_name = "hybrid_shortconv_gla_convffn_encoder"


def rmsnorm(x, g):
    xf = x.astype(jnp.float32)
    y = xf * lax.rsqrt(jnp.mean(xf * xf, axis=-1, keepdims=True) + EPS)
    return (y * g.astype(jnp.float32)).astype(x.dtype)


def dwconv3(x, w):
    xp = jnp.pad(x, ((0, 0), (1, 1), (0, 0)))
    return w[0] * xp[:, :-2] + w[1] * xp[:, 1:-1] + w[2] * xp[:, 2:]


def split_cols(p):
    idx, acc = [], 0
    for s in SPLIT_SIZES[:-1]:
        acc += s
        idx.append(acc)
    return jnp.split(p, idx, axis=-1)


def gla_chunked(q, k, v, log_a):
    b_, h_, L, dk = q.shape
    dv = v.shape[-1]
    n = L // CHUNK
    q = q.reshape(b_, h_, n, CHUNK, dk)
    k = k.reshape(b_, h_, n, CHUNK, dk)
    v = v.reshape(b_, h_, n, CHUNK, dv)
    cum = jnp.cumsum(log_a.reshape(b_, h_, n, CHUNK, dk), axis=3)
    cum_last = cum[:, :, :, -1:, :]
    q_in = q * jnp.exp(cum)
    k_in = k * jnp.exp(-cum)
    k_out = k * jnp.exp(cum_last - cum)
    mask = jnp.tril(jnp.ones((CHUNK, CHUNK), dtype=bool))
    scores = jnp.einsum('bhnid,bhnjd->bhnij', q_in, k_in)
    scores = jnp.where(mask, scores, 0.0)
    o_intra = jnp.einsum('bhnij,bhnje->bhnie', scores, v)
    kv = jnp.einsum('bhnjd,bhnje->bhnde', k_out, v)
    decay = jnp.exp(cum_last[:, :, :, 0, :])

    def step(state, inp):
        d_n, kv_n = inp
        return d_n[..., None] * state + kv_n, state

    _, s_prev = lax.scan(step, jnp.zeros((b_, h_, dk, dv), jnp.float32),
                         (jnp.moveaxis(decay, 2, 0), jnp.moveaxis(kv, 2, 0)))
    s_prev = jnp.moveaxis(s_prev, 0, 2)
    o_inter = jnp.einsum('bhnid,bhnde->bhnie', q_in, s_prev)
    return (o_intra + o_inter).reshape(b_, h_, L, dv)


def to_heads(t, d_head):
    b_, L, _ = t.shape
    return t.reshape(b_, L, -1, d_head).transpose(0, 2, 1, 3)


def gate_log_decay(lr, w_up, bias):
    pre = (lr @ w_up + bias).astype(jnp.float32)
    return jax.nn.log_sigmoid(pre) / GATE_NORMALIZER


def mixer(h, w_in, conv_a, gate_up_fwd, gate_bias_fwd, gate_up_bwd, gate_bias_bwd,
          gla_head_norm, w_out):
    p = h @ w_in
    gb, gc, gv, q, k, v, go, lr_f, lr_b = split_cols(p)
    y_a = gb * dwconv3(gc * gv, conv_a)
    la_f = gate_log_decay(lr_f, gate_up_fwd, gate_bias_fwd)
    la_b = gate_log_decay(lr_b, gate_up_bwd, gate_bias_bwd)
    qh = to_heads(q, HEAD_K).astype(jnp.float32) * (HEAD_K ** -0.5)
    kh = to_heads(k, HEAD_K).astype(jnp.float32)
    vh = to_heads(v, HEAD_V).astype(jnp.float32)
    af = to_heads(la_f, HEAD_K)
    ab = to_heads(la_b, HEAD_K)
    flip = lambda t: jnp.flip(t, axis=2)
    o_f = gla_chunked(qh, kh, vh, af)
    o_b = flip(gla_chunked(flip(qh), flip(kh), flip(vh), flip(ab)))
    o = o_f + o_b
    o = o * lax.rsqrt(jnp.mean(o * o, axis=-1, keepdims=True) + EPS) * gla_head_norm.astype(jnp.float32)
    b_, _, L, _ = o.shape
    o = o.transpose(0, 2, 1, 3).reshape(b_, L, D_GLA).astype(h.dtype)
    y_b = jax.nn.silu(go) * o
    y = jnp.concatenate([y_a, y_b], axis=-1)
    return y @ w_out


def conv_mlp(h, w_up, conv_w, w_down):
    u = dwconv3(h @ w_up, conv_w)
    gate, val = jnp.split(u, 2, axis=-1)
    return (jax.nn.silu(gate) * val) @ w_down


def setup_inputs(seed: int = 0) -> dict:
    key = jax.random.key(seed)
    ks = jax.random.split(key, 20)
    f32 = jnp.float32
    nrm = lambda k, shape, scale: jax.random.normal(k, shape, f32) * scale
    gain = lambda k, shape: 1.0 + 0.05 * jax.random.normal(k, shape, f32)
    return {
        "x": jax.random.normal(ks[0], (BATCH, SEQ, D_MODEL), f32),
        "norm_mix_pre": gain(ks[1], (DEPTH, D_MODEL)),
        "norm_mix_post": gain(ks[2], (DEPTH, D_MODEL)),
        "norm_ffn_pre": gain(ks[3], (DEPTH, D_MODEL)),
        "norm_ffn_post": gain(ks[4], (DEPTH, D_MODEL)),
        "w_in": nrm(ks[5], (DEPTH, D_MODEL, D_IN), D_MODEL ** -0.5),
        "conv_a": nrm(ks[6], (DEPTH, CONV_WIDTH, D_CONV), CONV_WIDTH ** -0.5),
        "gate_up_fwd": nrm(ks[7], (DEPTH, GATE_RANK, D_GLA_K), GATE_RANK ** -0.5),
        "gate_bias_fwd": nrm(ks[8], (DEPTH, D_GLA_K), 0.1),
        "gate_up_bwd": nrm(ks[9], (DEPTH, GATE_RANK, D_GLA_K), GATE_RANK ** -0.5),
        "gate_bias_bwd": nrm(ks[10], (DEPTH, D_GLA_K), 0.1),
        "gla_head_norm": gain(ks[11], (DEPTH, HEAD_V)),
        "w_out": nrm(ks[12], (DEPTH, D_MIX, D_MODEL), D_MIX ** -0.5),
        "w_up": nrm(ks[13], (DEPTH, D_MODEL, 2 * D_FF), D_MODEL ** -0.5),
        "conv_ffn": nrm(ks[14], (DEPTH, CONV_WIDTH, 2 * D_FF), CONV_WIDTH ** -0.5),
        "w_down": nrm(ks[15], (DEPTH, D_FF, D_MODEL), D_FF ** -0.5),
    }


def reference(x, norm_mix_pre, norm_mix_post, norm_ffn_pre, norm_ffn_post, w_in, conv_a,
              gate_up_fwd, gate_bias_fwd, gate_up_bwd, gate_bias_bwd, gla_head_norm,
              w_out, w_up, conv_ffn, w_down):
    for l in range(DEPTH):
        h = rmsnorm(x, norm_mix_pre[l])
        y = mixer(h, w_in[l], conv_a[l], gate_up_fwd[l], gate_bias_fwd[l], gate_up_bwd[l],
                  gate_bias_bwd[l], gla_head_norm[l], w_out[l])
        x = x + rmsnorm(y, norm_mix_post[l])
        h = rmsnorm(x, norm_ffn_pre[l])
        y = conv_mlp(h, w_up[l], conv_ffn[l], w_down[l])
        x = x + rmsnorm(y, norm_ffn_post[l])
    return x
```

```python
import numpy as np
from contextlib import ExitStack
import concourse.bass as bass
import concourse.mybir as mybir
from concourse.bass_utils import run_bass_kernel_spmd

F32 = mybir.dt.float32
BF16 = mybir.dt.bfloat16
ALU = mybir.AluOpType
AF = mybir.ActivationFunctionType
AX = mybir.AxisListType

D = 1024
DEPTH = 4
BATCH = 2
SEQ = 8192
NCORES = 8
SEG = 2048
TW = SEG + 2
KC = D // 128
EPS = 1e-6
D_IN = 3104
D_FF = 2816
NCP = D_FF // 128
HEADS = 4
NT = SEG // 128
OFF_GB, OFF_GC, OFF_GV, OFF_Q, OFF_K, OFF_V, OFF_GO, OFF_LR = 0, 512, 1024, 1536, 1792, 2048, 2560, 3072
GROUPS = [[0, 1, 2, 3], [4, 5, 6, 7]]
ARENA_WORDS = 23360
FFN_SUPER = [(0, [342, 342]), (684, [341, 341]), (1366, [341, 341])]


class Tile:
    __slots__ = ("name", "w", "r")

    def __init__(self, name):
        self.name = name
        self.w = None
        self.r = {}


class Sched:
    ENG = ("pe", "act", "dve", "pool", "sp")

    def __init__(self):
        self.ops = {e: [] for e in self.ENG}
        self.cnt = {e: 0 for e in self.ENG}
        self.known = {e: {} for e in self.ENG}
        self.dma_cnt = {}
        self.snap = {}

    def op(self, eng, fn, reads=(), writes=(), dma=None, inc=True):
        deps = {}

        def add(ev):
            if ev is None:
                return
            k, v = ev
            if deps.get(k, 0) < v:
                deps[k] = v
        for t in reads:
            add(t.w)
        for t in writes:
            add(t.w)
            for k, v in t.r.items():
                add((k, v))
        kn = self.known[eng]
        need = [(k, v) for k, v in deps.items() if not (k == eng and eng == "pe") and kn.get(k, 0) < v]
        waits = list(need)
        for d_ in need:
            if any(o is not d_ and self.snap.get(o, {}).get(d_[0], 0) >= d_[1] for o in waits):
                waits.remove(d_)
        for k, v in need:
            if kn.get(k, 0) < v:
                kn[k] = v
        for o in waits:
            for k2, v2 in self.snap.get(o, {}).items():
                if kn.get(k2, 0) < v2:
                    kn[k2] = v2
        if dma is not None:
            self.dma_cnt[dma] = self.dma_cnt.get(dma, 0) + 16
            ev = (dma, self.dma_cnt[dma])
            incv = 16
        elif inc:
            self.cnt[eng] += 1
            ev = (eng, self.cnt[eng])
            incv = 1
        else:
            ev = (eng, self.cnt[eng] + 1)
            incv = 0
        if dma is not None or inc:
            self.snap[ev] = dict(kn)
        for t in reads:
            if t.r.get(ev[0], 0) < ev[1]:
                t.r[ev[0]] = ev[1]
        for t in writes:
            t.w = ev
            t.r = {}
        self.ops[eng].append((waits, fn, ev[0], incv))

    def fence(self):
        allv = [(k, v) for k, v in self.cnt.items() if v > 0] + list(self.dma_cnt.items())
        for e in self.ENG:
            kn = self.known[e]
            waits = []
            for k, v in allv:
                if k == e and e == "pe":
                    continue
                if kn.get(k, 0) >= v:
                    continue
                kn[k] = v
                waits.append((k, v))
            if waits:
                self.ops[e].append((waits, None, None, 0))

    def sem_names(self):
        return list(self.ENG) + list(self.dma_cnt.keys())

    def emit(self, block, sems):
        engs = {"pe": block.tensor, "act": block.scalar, "dve": block.vector,
                "pool": block.gpsimd, "sp": block.sync}
        for name in self.ENG:
            ops = self.ops[name]

            def body(engine, ops=ops):
                for waits, fn, evk, incv in ops:
                    for k, v in waits:
                        engine.wait_ge(sems[k], v)
                    if fn is None:
                        continue
                    ins = fn(engine)
                    if incv:
                        ins.then_inc(sems[evk], incv)
            engs[name](body)


def MM(out, lhsT, rhs, start, stop):
    return lambda e: e.matmul(out, lhsT=lhsT, rhs=rhs, start=start, stop=stop)


def ACT(out, in_, func, **kw):
    return lambda e: e.activation(out=out, in_=in_, func=func, **kw)


def TT(out, in0, in1, op):
    return lambda e: e.tensor_tensor(out=out, in0=in0, in1=in1, op=op)


def STT(out, in0, scalar, in1, op0, op1):
    return lambda e: e.scalar_tensor_tensor(out=out, in0=in0, scalar=scalar, in1=in1, op0=op0, op1=op1)


def TS(out, in0, s1, s2, op0, op1):
    return lambda e: e.tensor_scalar(out=out, in0=in0, scalar1=s1, scalar2=s2, op0=op0, op1=op1)


def RSUM(out, in_):
    return lambda e: e.reduce_sum(out=out, in_=in_, axis=AX.X)


def DMA(out, in_):
    return lambda e: e.dma_start(out=out, in_=in_)


def AG(in_ap, out_ap):
    return lambda e: e.collective_compute("AllGather", ALU.bypass, replica_groups=GROUPS, ins=[in_ap], outs=[out_ap])


class Arena:
    def __init__(self, ap, nwords, alias0=False):
        self.ap, self.n, self.off, self.alias0 = ap, nwords, 0, alias0

    def seek(self, off):
        self.off = off

    def _take(self, words):
        words = (words + 7) // 8 * 8
        a = self.off
        self.off += words
        assert self.off <= self.n, f"arena overflow {self.off} > {self.n}"
        if self.alias0:
            return self.ap[:, 0:words]
        return self.ap[:, a:a + words]

    def f32(self, cols):
        return self._take(cols)[:, 0:cols]

    def bf16(self, cols):
        return self._take((cols + 1) // 2).bitcast(BF16)[:, 0:cols]


def _col_tiles(total, width):
    out, c = [], 0
    while c < total:
        w = min(width, total - c)
        out.append((c, w))
        c += w
    return out


class _Cut(Exception):
    pass


def build_program(n_layers=DEPTH, stop=None, cut=None):
    nc = bass.Bass("TRN2", target_bir_lowering=False)

    def din(name, shape):
        return nc.dram_tensor(name, shape, F32, kind="ExternalInput").ap()
    xT_d = din("xT", [D, TW])
    gains_d = din("gains", [128, 4 * DEPTH * KC])
    ghn_d = din("ghn", [128, DEPTH])
    cva_d = din("cva", [128, DEPTH * 12])
    cvf_d = din("cvf", [128, DEPTH * 132])
    cst_d = din("consts", [128, 4])
    msk_d = din("masks", [128, 24])
    ones_d = din("ones", [128, 128])
    onesrow_d = din("ones_row", [1, SEG])
    tri_d = din("tri", [128, 512])
    m01_d = din("m01", [128, 256])
    gup_d = din("gup", [DEPTH * HEADS * 33, 128])
    win128_d = din("w_in128", [n_layers * 16 * 128, KC * 128])
    win64_d = din("w_in64", [n_layers * HEADS * 128, KC * 64])
    winkv_d = din("w_inkv", [n_layers * HEADS * 128, KC * 192])
    win32_d = din("w_in32", [n_layers * 128, KC * 32])
    wout_d = din("w_out", [n_layers * D, D])
    wup_d = din("w_up", [n_layers * 2 * NCP * 128, KC * 128])
    wdn_d = din("w_down", [n_layers * KC * 128, NCP * 128])

    def wblk(d, idx, c):
        return d[idx * 128:(idx + 1) * 128, :].rearrange("p (c n) -> p c n", c=c)
    out_d = nc.dram_tensor("outT", [D, SEG], F32, kind="ExternalOutput").ap()
    cc_in = nc.dram_tensor("cc_in", [64, 264], F32)
    cc_out = nc.dram_tensor("cc_out", [4 * 64, 264], F32)
    cx_in = nc.dram_tensor("cx_in", [128, 16], F32)
    cx_out = nc.dram_tensor("cx_out", [4 * 128, 16], F32)

    S = Sched()

    def stage(k):
        if cut is not None and k > cut:
            raise _Cut()

    with ExitStack() as es:
        def sb(name, shape, dt):
            return es.enter_context(nc.sbuf_tensor(name, shape, dt))

        xT = sb("xT_s", [128, KC, TW], F32)
        hT = sb("hT_s", [128, KC, TW], BF16)
        gains = sb("gains_s", [128, 4 * DEPTH * KC], F32)
        ghn = sb("ghn_s", [128, DEPTH], F32)
        cva = sb("cva_s", [128, DEPTH * 12], F32)
        cvf = sb("cvf_s", [128, DEPTH * 132], F32)
        cst = sb("cst_s", [128, 4], F32)
        msk = sb("msk_s", [128, 24], F32)
        ones = sb("ones_s", [128, 128], BF16)
        tri = sb("tri_s", [128, 512], BF16)
        m01 = sb("m01_s", [128, 256], F32)
        lrT = sb("lrT_s", [33, SEG], BF16)
        sqn = [sb(f"sqn{i}", [128, 512], BF16) for i in range(2)]
        rsn = [sb(f"rsn{i}", [128, 512], F32) for i in range(2)]
        import os
        _small = os.environ.get("KDBG_SMALLARENA") == "1"
        arena_t = sb("arena", [128, 8200 if _small else ARENA_WORDS], F32)
        AR = Arena(arena_t, ARENA_WORDS, alias0=_small)
        if os.environ.get("KDBG_PSUM2") == "1":
            _p2 = [es.enter_context(nc.psum_tensor(f"bank{i}", [128, 512], F32)) for i in range(2)]
            P = [_p2[i % 2] for i in range(8)]
        else:
            P = [es.enter_context(nc.psum_tensor(f"bank{i}", [128, 512], F32)) for i in range(8)]
        tP = [Tile(f"bank{i}") for i in range(8)]
        t_x, t_h, t_setup, t_lrT = Tile("x"), Tile("h"), Tile("setup"), Tile("lrT")
        t_sqn, t_rsn = [Tile("sqn0"), Tile("sqn1")], [Tile("rsn0"), Tile("rsn1")]
        eps_ap = cst[:, 0:1]

        def gain(kind, layer, c):
            i = (kind * DEPTH + layer) * KC + c
            return gains[:, i:i + 1]

        xT_v = xT_d.rearrange("(c p) t -> p c t", p=128)
        for c in range(KC):
            S.op("sp", DMA(xT[:, c, :], xT_v[:, c, :]), writes=[t_x], dma="ldx")
        for dst, src in [(gains, gains_d), (ghn, ghn_d), (cva, cva_d), (cvf, cvf_d), (cst, cst_d), (msk, msk_d), (m01, m01_d)]:
            S.op("sp", DMA(dst[:], src), writes=[t_setup], dma="setup")
        S.op("pool", DMA(ones[:], ones_d), writes=[t_setup], dma="setup_c")
        S.op("pool", DMA(tri[:], tri_d), writes=[t_setup], dma="setup_c")
        import os
        if os.environ.get("KDBG_NOROW") != "1":
            S.op("pool", DMA(lrT[32:33, :], onesrow_d), writes=[t_lrT], dma="setup_c")
        if os.environ.get("KDBG_TOUCH") == "1":
            scr = sb("scr", [128, 64], F32)
            t_scr = Tile("scr")
            for i_, src in enumerate([win128_d, wout_d, wup_d, wdn_d, gup_d]):
                S.op("sp", DMA(scr[:, 8 * i_:8 * i_ + 8], src[0:128, 0:8]), writes=[t_scr], dma="touch")
            S.op("sp", DMA(scr[0:1, 48:56], onesrow_d[0:1, 0:8]), writes=[t_scr], dma="touch")
        S.fence()

        AR.seek(0)
        yT = AR.bf16(KC * SEG).rearrange("p (c t) -> p c t", c=KC)
        HEAD0 = AR.off
        qin = [AR.bf16(SEG) for _ in range(2)]
        kin = [AR.bf16(SEG) for _ in range(2)]
        kout = [AR.bf16(NT * 64).rearrange("p (n d) -> p n d", n=NT) for _ in range(2)]
        vtok = AR.bf16(NT * 128).rearrange("p (n e) -> p n e", n=NT)
        Sbf = [AR.bf16(NT * 128).rearrange("p (n e) -> p n e", n=NT) for _ in range(2)]
        Wq = AR.bf16(KC * 64).rearrange("p (c n) -> p c n", c=KC)
        Wkv = AR.bf16(KC * 192).rearrange("p (c n) -> p c n", c=KC)
        Wgo = AR.bf16(KC * 128).rearrange("p (c n) -> p c n", c=KC)
        Wlr = AR.bf16(KC * 32).rearrange("p (c n) -> p c n", c=KC)
        gup = AR.bf16(128)
        sp32 = [AR.f32(128) for _ in range(2)]
        sp_hi = [AR.bf16(128) for _ in range(2)]
        sp_lo = [AR.bf16(128) for _ in range(2)]
        tmpE = [[AR.f32(128) for _ in range(4)] for _ in range(2)]
        ekd = [AR.f32(128) for _ in range(2)]
        cl = [AR.f32(NT) for _ in range(2)]
        dec = [AR.f32(NT) for _ in range(2)]
        clsum = [AR.f32(1) for _ in range(2)]
        pay = AR.f32(264)
        gbuf = AR.f32(4 * 264).rearrange("p (r n) -> p r n", r=4)
        Ain = [AR.f32(128) for _ in range(2)]
        Sx = [AR.f32(128) for _ in range(2)]
        tmpKV = AR.f32(128)
        Dm = AR.f32(8)
        sTb = [[AR.bf16(128) for _ in range(2)] for _ in range(2)]
        osq = [AR.bf16(128) for _ in range(2)]
        rs_o = [AR.f32(128) for _ in range(2)]
        o_tmp = [AR.f32(128) for _ in range(2)]
        AR.seek(HEAD0)
        Wb = [AR.bf16(KC * 128).rearrange("p (c n) -> p c n", c=KC) for _ in range(2)]
        Wc = [AR.bf16(KC * 128).rearrange("p (c n) -> p c n", c=KC) for _ in range(2)]
        Wvv = [AR.bf16(KC * 128).rearrange("p (c n) -> p c n", c=KC) for _ in range(2)]
        cv = AR.f32(TW)
        acc = AR.f32(SEG)
        gcs = AR.f32(512)
        Wout = AR.bf16(KC * D).rearrange("p (c n) -> p c n", c=KC)
        print("arena: conv scratch + Wout end at", AR.off)
        AR.seek(HEAD0)
        zbuf = [AR.f32(KC * 256).rearrange("p (m t) -> p m t", m=KC) for _ in range(2)]
        sqz = [AR.bf16(256) for _ in range(2)]
        rs_z = [AR.f32(256) for _ in range(2)]
        z_tmp = AR.f32(256)
        payx = AR.f32(16)
        gx = AR.f32(64).rearrange("p (r n) -> p r n", r=4)
        AR.seek(0)
        aT = AR.bf16(NCP * 688).rearrange("p (c t) -> p c t", c=NCP)
        zf = AR.f32(KC * 684).rearrange("p (m t) -> p m t", m=KC)
        Wg = [AR.bf16(KC * 128).rearrange("p (c n) -> p c n", c=KC) for _ in range(2)]
        Wv = [AR.bf16(KC * 128).rearrange("p (c n) -> p c n", c=KC) for _ in range(2)]
        Wd = [AR.bf16(NCP * 128).rearrange("p (c n) -> p c n", c=NCP) for _ in range(2)]
        accg = [AR.f32(344) for _ in range(2)]
        accv = [AR.f32(344) for _ in range(2)]
        sqf = [AR.bf16(344) for _ in range(2)]
        rs_f = [AR.f32(344) for _ in range(2)]
        f_tmp = AR.f32(344)
        payx2 = AR.f32(16)
        gx2 = AR.f32(64).rearrange("p (r n) -> p r n", r=4)

        def rstd_from(ps_ap, out_ap, n, reads, wtile):
            S.op("act", ACT(out_ap, ps_ap, AF.Ln, scale=1.0 / n, bias=eps_ap), reads=list(reads) + [t_setup], writes=[wtile])
            S.op("act", ACT(out_ap, out_ap, AF.Exp, scale=-0.5), reads=[wtile], writes=[wtile])

        def rmsnorm_to_h(kind, layer):
            for ti, (c0, w) in enumerate(_col_tiles(TW, 512)):
                b = ti % 2
                bank = 6 + b
                for c in range(KC):
                    S.op("act", ACT(sqn[c % 2][:, 0:w], xT[:, c, c0:c0 + w], AF.Square), reads=[t_x], writes=[t_sqn[c % 2]])
                    S.op("pe", MM(P[bank][:, 0:w], ones[:], sqn[c % 2][:, 0:w], c == 0, c == KC - 1),
                         reads=[t_sqn[c % 2], t_setup], writes=[tP[bank]])
                rstd_from(P[bank][:, 0:w], rsn[b][:, 0:w], D, [tP[bank]], t_rsn[b])
                for c in range(KC):
                    S.op("dve", STT(hT[:, c, c0:c0 + w], xT[:, c, c0:c0 + w], gain(kind, layer, c), rsn[b][:, 0:w], ALU.mult, ALU.mult),
                         reads=[t_x, t_setup, t_rsn[b]], writes=[t_h])

        def proj_fm(bank, Wt, m, c0, w, wtile):
            for c in range(KC):
                S.op("pe", MM(P[bank][0:m, 0:w], Wt[:, c, :], hT[:, c, c0:c0 + w], c == 0, c == KC - 1),
                     reads=[wtile, t_h], writes=[tP[bank]], inc=(c == KC - 1))

        def exchange_x(pay_ap, g_ap):
            t_px, t_cxi, t_cxo, t_gx = Tile("payx"), Tile("cx_in"), Tile("cx_out"), Tile("gx")
            pv = pay_ap.rearrange("p (c two) -> p c two", two=2)
            S.op("act", ACT(pv[:, :, 0], xT[:, :, 1], AF.Copy), reads=[t_x], writes=[t_px])
            S.op("act", ACT(pv[:, :, 1], xT[:, :, SEG], AF.Copy), reads=[t_x], writes=[t_px])
            S.op("sp", DMA(cx_in.ap(), pay_ap), reads=[t_px], writes=[t_cxi], dma="xch_o")
            S.op("pool", AG(cx_in.ap().opt(), cx_out.ap().opt()), reads=[t_cxi], writes=[t_cxo])
            S.op("sp", DMA(g_ap, cx_out.ap().rearrange("(r p) n -> p r n", p=128)), reads=[t_cxo], writes=[t_gx], dma="xch_i")
            for halo_col, src_two, mbase in [(0, 1, 16), (TW - 1, 0, 20)]:
                for r in range(4):
                    src = g_ap[:, r, :].rearrange("p (c two) -> p c two", two=2)[:, :, src_two]
                    m_ap = msk[:, mbase + r:mbase + r + 1]
                    if r == 0:
                        S.op("act", ACT(xT[:, :, halo_col], src, AF.Copy, scale=m_ap), reads=[t_gx, t_setup], writes=[t_x])
                    else:
                        S.op("dve", STT(xT[:, :, halo_col], src, m_ap, xT[:, :, halo_col], ALU.mult, ALU.add),
                             reads=[t_gx, t_setup, t_x], writes=[t_x])

        def gla_head(l, h, t_y):
            t_wq, t_wkv, t_wgo, t_gup = Tile("wq"), Tile("wkv"), Tile("wgo"), Tile("gup")
            t_qin, t_kin = [Tile("qinf"), Tile("qinb")], [Tile("kinf"), Tile("kinb")]
            t_kout, t_v, t_sbf = [Tile("koutf"), Tile("koutb")], Tile("vtok"), [Tile("sbff"), Tile("sbfb")]
            t_sp2, t_hi2, t_lo2, t_ekd2 = ([Tile(f"{nm}{i}") for i in range(2)] for nm in ("sp", "sphi", "splo", "ekd"))
            t_tmpE2 = [[Tile(f"tmpE{p}{i}") for i in range(4)] for p in range(2)]
            t_cl, t_dec, t_cs, t_pay = [Tile("clf"), Tile("clb")], [Tile("decf"), Tile("decb")], [Tile("csf"), Tile("csb")], Tile("pay")
            stage(3)
            S.op("pool", DMA(Wq, wblk(win64_d, l * HEADS + h, KC)), writes=[t_wq], dma="ld_wq")
            S.op("pool", DMA(Wkv, wblk(winkv_d, l * HEADS + h, KC)), writes=[t_wkv], dma="ld_wkv")
            S.op("pool", DMA(Wgo, wblk(win128_d, l * 16 + 12 + h, KC)), writes=[t_wgo], dma="ld_wgo")
            g0 = (l * HEADS + h) * 33
            S.op("pool", DMA(gup[0:33, :], gup_d[g0:g0 + 33, :]), writes=[t_gup], dma="ld_gup")
            for tg in range(4):
                bank = tg % 2
                proj_fm(bank, Wgo, 128, 1 + 512 * tg, 512, t_wgo)
                S.op("act", ACT(yT[:, 4 + h, 512 * tg:512 * tg + 512], P[bank][:, 0:512], AF.Silu), reads=[tP[bank]], writes=[t_y])
            for n in range(NT):
                par = n % 2
                pb = 4 * par
                bA, bB, bC, bD = pb, pb + 1, pb + 2, pb + 3
                sp32_, sp_hi_, sp_lo_, ekd_, tmpE_ = sp32[par], sp_hi[par], sp_lo[par], ekd[par], tmpE[par]
                t_sp, t_hi, t_lo, t_ekd, t_tmpE = t_sp2[par], t_hi2[par], t_lo2[par], t_ekd2[par], t_tmpE2[par]
                c0 = 1 + 128 * n
                tl = slice(128 * n, 128 * n + 128)
                stage(4 if n == 0 else 7)
                for c in range(KC):
                    S.op("pe", MM(P[bA][0:64, 0:128], Wq[:, c, :], hT[:, c, c0:c0 + 128], c == 0, c == KC - 1),
                         reads=[t_wq, t_h], writes=[tP[bA]], inc=False)
                for c in range(KC):
                    S.op("pe", MM(P[bA][0:64, 128:256], Wkv[:, c, 0:64], hT[:, c, c0:c0 + 128], c == 0, c == KC - 1),
                         reads=[t_wkv, t_h], writes=[tP[bA]], inc=(c == KC - 1))
                for c in range(KC):
                    S.op("pe", MM(P[bB][:, 0:192], hT[:, c, c0:c0 + 128], Wkv[:, c, :], c == 0, c == KC - 1),
                         reads=[t_wkv, t_h], writes=[tP[bB]], inc=(c == KC - 1))
                stage(5 if n == 0 else 7)
                S.op("pe", MM(P[bC][:, 0:128], lrT[0:33, tl], gup[0:33, :], True, True), reads=[t_lrT, t_gup], writes=[tP[bC]])
                S.op("act", ACT(sp32_, P[bC][:, 0:128], AF.Exp, scale=-1.0), reads=[tP[bC]], writes=[t_sp])
                S.op("act", ACT(sp32_, sp32_, AF.Ln, bias=1.0), reads=[t_sp], writes=[t_sp])
                S.op("act", ACT(sp_hi_, sp32_, AF.Copy), reads=[t_sp], writes=[t_hi])
                S.op("dve", TT(sp_lo_, sp32_, sp_hi_, ALU.subtract), reads=[t_sp, t_hi], writes=[t_lo])
                stage(6 if n == 0 else 7)
                for d_, (lo_c, trc) in enumerate([(0, 0), (64, 128)]):
                    oc = 128 * d_
                    S.op("pe", MM(P[bD][0:64, oc:oc + 128], sp_hi_[:, lo_c:lo_c + 64], tri[:, trc:trc + 128], True, False),
                         reads=[t_hi, t_setup], writes=[tP[bD]], inc=False)
                    S.op("pe", MM(P[bD][0:64, oc:oc + 128], sp_lo_[:, lo_c:lo_c + 64], tri[:, trc:trc + 128], False, True),
                         reads=[t_lo, t_setup], writes=[tP[bD]], inc=(d_ == 1))
                for d_, (lo_c, trc) in enumerate([(0, 256), (64, 384)]):
                    oc = 128 + 64 * d_
                    S.op("pe", MM(P[bC][:, oc:oc + 64], tri[:, trc:trc + 128], sp_hi_[:, lo_c:lo_c + 64], True, False),
                         reads=[t_hi, t_setup], writes=[tP[bC]], inc=False)
                    S.op("pe", MM(P[bC][:, oc:oc + 64], tri[:, trc:trc + 128], sp_lo_[:, lo_c:lo_c + 64], False, True),
                         reads=[t_lo, t_setup], writes=[tP[bC]], inc=(d_ == 1))
                for d_ in range(2):
                    cum = P[bD][0:64, 128 * d_:128 * d_ + 128]
                    eq, ek = tmpE_[2 * d_][0:64, :], tmpE_[2 * d_ + 1][0:64, :]
                    S.op("act", ACT(eq, cum, AF.Exp), reads=[tP[bD]], writes=[t_tmpE[2 * d_]])
                    S.op("act", ACT(ek, cum, AF.Exp, scale=-1.0), reads=[tP[bD]], writes=[t_tmpE[2 * d_ + 1]])
                    S.op("dve", STT(qin[d_][0:64, tl], P[bA][0:64, 0:128], 0.125, eq, ALU.mult, ALU.mult),
                         reads=[tP[bA], t_tmpE[2 * d_]], writes=[t_qin[d_]])
                    S.op("dve", TT(kin[d_][0:64, tl], P[bA][0:64, 128:256], ek, ALU.mult),
                         reads=[tP[bA], t_tmpE[2 * d_ + 1]], writes=[t_kin[d_]])
                    last = 127 if d_ == 0 else 0
                    S.op("act", ACT(cl[d_][0:64, n:n + 1], cum[:, last:last + 1], AF.Copy), reads=[tP[bD]], writes=[t_cl[d_]])
                S.op("act", ACT(ekd_, P[bC][:, 128:256], AF.Exp), reads=[tP[bC]], writes=[t_ekd])
                for d_ in range(2):
                    S.op("dve", TT(kout[d_][:, n, :], P[bB][:, 0:64], ekd_[:, 64 * d_:64 * d_ + 64], ALU.mult),
                         reads=[tP[bB], t_ekd], writes=[t_kout[d_]])
                S.op("act", ACT(vtok[:, n, :], P[bB][:, 64:192], AF.Copy), reads=[tP[bB]], writes=[t_v])
            stage(8)
            for d_ in range(2):
                S.op("act", ACT(dec[d_][0:64, :], cl[d_][0:64, :], AF.Exp), reads=[t_cl[d_]], writes=[t_dec[d_]])
                S.op("dve", RSUM(clsum[d_][0:64, :], cl[d_][0:64, :]), reads=[t_cl[d_]], writes=[t_cs[d_]])
                dcol = 128 if d_ == 0 else 257
                S.op("act", ACT(pay[0:64, dcol:dcol + 1], clsum[d_][0:64, :], AF.Exp), reads=[t_cs[d_]], writes=[t_pay])

            def order_of(d_, i):
                return i if d_ == 0 else NT - 1 - i
            stage(9)
            t_payS = [Tile("payf"), Tile("payb")]
            S.op("act", ACT(pay[0:64, 258:264], m01[0:64, 0:6], AF.Copy), reads=[t_setup], writes=[t_pay])
            payS = [pay[0:64, 0:128], pay[0:64, 129:257]]
            for i in range(NT):
                for d_ in range(2):
                    n = order_of(d_, i)
                    bank = (2 * i + d_) % 4
                    S.op("pe", MM(P[bank][0:64, 0:128], kout[d_][:, n, :], vtok[:, n, :], True, True),
                         reads=[t_kout[d_], t_v], writes=[tP[bank]])
                    if i == 0:
                        S.op("act", ACT(payS[d_], P[bank][0:64, 0:128], AF.Copy), reads=[tP[bank]], writes=[t_payS[d_]])
                    else:
                        S.op("dve", STT(payS[d_], payS[d_], dec[d_][0:64, n:n + 1], P[bank][0:64, 0:128], ALU.mult, ALU.add),
                             reads=[t_payS[d_], t_dec[d_], tP[bank]], writes=[t_payS[d_]])
            stage(10)
            t_cci, t_cco, t_g = Tile("cc_in"), Tile("cc_out"), Tile("gbuf")
            S.op("sp", DMA(cc_in.ap(), pay[0:64, :]), reads=[t_pay, t_payS[0], t_payS[1]], writes=[t_cci], dma="gx_o")
            S.op("pool", AG(cc_in.ap().opt(), cc_out.ap().opt()), reads=[t_cci], writes=[t_cco])
            S.op("sp", DMA(gbuf[0:64, :, :], cc_out.ap().rearrange("(r p) n -> p r n", p=64)), reads=[t_cco], writes=[t_g], dma="gx_i")
            t_A, t_tkv, t_dm = [Tile("Af"), Tile("Ab")], Tile("tmpKV"), Tile("Dm")
            for d_ in range(2):
                order = [0, 1, 2, 3] if d_ == 0 else [3, 2, 1, 0]
                mb, kv0, dcol = (0, 0, 128) if d_ == 0 else (8, 129, 257)
                A = Ain[d_][0:64, :]
                for i, r in enumerate(order):
                    m_ap, om_ap = msk[0:64, mb + r:mb + r + 1], msk[0:64, mb + 4 + r:mb + 5 + r]
                    if i == 0:
                        S.op("act", ACT(A, gbuf[0:64, r, kv0:kv0 + 128], AF.Copy, scale=m_ap), reads=[t_g, t_setup], writes=[t_A[d_]])
                        continue
                    S.op("dve", TS(Dm[0:64, 0:1], gbuf[0:64, r, dcol:dcol + 1], m_ap, om_ap, ALU.mult, ALU.add),
                         reads=[t_g, t_setup], writes=[t_dm])
                    S.op("act", ACT(tmpKV[0:64, :], gbuf[0:64, r, kv0:kv0 + 128], AF.Copy, scale=m_ap), reads=[t_g, t_setup], writes=[t_tkv])
                    S.op("dve", STT(A, A, Dm[0:64, 0:1], tmpKV[0:64, :], ALU.mult, ALU.add), reads=[t_A[d_], t_dm, t_tkv], writes=[t_A[d_]])
            stage(11)
            t_Sx = [Tile("Sxf"), Tile("Sxb")]
            for i in range(NT):
                for d_ in range(2):
                    n = order_of(d_, i)
                    bufs = [(Ain[d_][0:64, :], t_A[d_]), (Sx[d_][0:64, :], t_Sx[d_])]
                    (cur, t_cur), (nxt, t_nxt) = bufs[i % 2], bufs[(i + 1) % 2]
                    S.op("act", ACT(Sbf[d_][0:64, n, :], cur, AF.Copy), reads=[t_cur], writes=[t_sbf[d_]])
                    if i == NT - 1:
                        continue
                    bank = (2 * i + d_) % 4
                    S.op("pe", MM(P[bank][0:64, 0:128], kout[d_][:, n, :], vtok[:, n, :], True, True),
                         reads=[t_kout[d_], t_v], writes=[tP[bank]])
                    S.op("dve", STT(nxt, cur, dec[d_][0:64, n:n + 1], P[bank][0:64, 0:128], ALU.mult, ALU.add),
                         reads=[t_cur, t_dec[d_], tP[bank]], writes=[t_nxt])
            stage(12)
            t_sT = [[Tile("sTf0"), Tile("sTb0")], [Tile("sTf1"), Tile("sTb1")]]
            t_osq, t_rso, t_to = [Tile("osq0"), Tile("osq1")], [Tile("rso0"), Tile("rso1")], [Tile("to0"), Tile("to1")]
            for n in range(NT):
                k = n % 2
                pb = 4 * k
                tl = slice(128 * n, 128 * n + 128)
                for d_ in range(2):
                    S.op("pe", MM(P[pb + d_][:, 0:128], kin[d_][0:64, tl], qin[d_][0:64, tl], True, True),
                         reads=[t_kin[d_], t_qin[d_]], writes=[tP[pb + d_]])
                    S.op("dve", TT(sTb[k][d_], P[pb + d_][:, 0:128], m01[:, 128 * d_:128 * d_ + 128], ALU.mult),
                         reads=[tP[pb + d_], t_setup], writes=[t_sT[k][d_]])
                ob = pb + 2
                S.op("pe", MM(P[ob][:, 0:128], vtok[:, n, :], sTb[k][0], True, False), reads=[t_v, t_sT[k][0]], writes=[tP[ob]], inc=False)
                S.op("pe", MM(P[ob][:, 0:128], vtok[:, n, :], sTb[k][1], False, False), reads=[t_v, t_sT[k][1]], writes=[tP[ob]], inc=False)
                S.op("pe", MM(P[ob][:, 0:128], Sbf[0][0:64, n, :], qin[0][0:64, tl], False, False), reads=[t_sbf[0], t_qin[0]], writes=[tP[ob]], inc=False)
                S.op("pe", MM(P[ob][:, 0:128], Sbf[1][0:64, n, :], qin[1][0:64, tl], False, True), reads=[t_sbf[1], t_qin[1]], writes=[tP[ob]])
                S.op("act", ACT(osq[k], P[ob][:, 0:128], AF.Square), reads=[tP[ob]], writes=[t_osq[k]])
                S.op("pe", MM(P[pb + 3][:, 0:128], ones[:], osq[k], True, True), reads=[t_osq[k], t_setup], writes=[tP[pb + 3]])
                rstd_from(P[pb + 3][:, 0:128], rs_o[k], 128, [tP[pb + 3]], t_rso[k])
                S.op("dve", TT(o_tmp[k], P[ob][:, 0:128], rs_o[k], ALU.mult), reads=[tP[ob], t_rso[k]], writes=[t_to[k]])
                S.op("dve", STT(yT[:, 4 + h, tl], o_tmp[k], ghn[:, l:l + 1], yT[:, 4 + h, tl], ALU.mult, ALU.mult),
                     reads=[t_to[k], t_setup, t_y], writes=[t_y])

        def mixer(l):
            S.fence()
            stage(1)
            rmsnorm_to_h(0, l)
            stage(2)
            t_wlr, t_y = Tile("wlr"), Tile("yT")
            S.op("pool", DMA(Wlr, wblk(win32_d, l, KC)), writes=[t_wlr], dma="ld_wlr")
            for tg in range(4):
                bank = tg % 2
                proj_fm(bank, Wlr, 32, 1 + 512 * tg, 512, t_wlr)
                S.op("act", ACT(lrT[0:32, 512 * tg:512 * tg + 512], P[bank][0:32, 0:512], AF.Copy), reads=[tP[bank]], writes=[t_lrT])
            for h in range(HEADS):
                gla_head(l, h, t_y)
            S.fence()
            stage(13)
            t_wb2, t_wc2, t_wv2 = ([Tile(f"{nm}{i}") for i in range(2)] for nm in ("wb", "wc", "wvv"))
            t_cv, t_acc, t_gcs = Tile("cv"), Tile("acc"), Tile("gcs")
            t_wo = Tile("wout")
            wo = wout_d[l * D:(l + 1) * D, :].rearrange("(c p) n -> p c n", p=128)

            def load_conv_w(cc):
                q_ = cc % 2
                S.op("pool", DMA(Wb[q_], wblk(win128_d, l * 16 + cc, KC)), writes=[t_wb2[q_]], dma=f"ld_wb{q_}")
                S.op("pool", DMA(Wc[q_], wblk(win128_d, l * 16 + 4 + cc, KC)), writes=[t_wc2[q_]], dma=f"ld_wc{q_}")
                S.op("pool", DMA(Wvv[q_], wblk(win128_d, l * 16 + 8 + cc, KC)), writes=[t_wv2[q_]], dma=f"ld_wv{q_}")
            load_conv_w(0)
            load_conv_w(1)
            for c in range(KC):
                S.op("pool", DMA(Wout[:, c, :], wo[:, c, :]), writes=[t_wo], dma="ld_wo")
            for cc in range(4):
                q_ = cc % 2
                Wb_, Wc_, Wvv_, t_wb, t_wc, t_wv = Wb[q_], Wc[q_], Wvv[q_], t_wb2[q_], t_wc2[q_], t_wv2[q_]
                for ti, (c0, w) in enumerate(_col_tiles(TW, 512)):
                    b0 = 2 * (ti % 2)
                    proj_fm(b0, Wc_, 128, c0, w, t_wc)
                    proj_fm(b0 + 1, Wvv_, 128, c0, w, t_wv)
                    S.op("act", ACT(gcs[:, 0:w], P[b0][:, 0:w], AF.Copy), reads=[tP[b0]], writes=[t_gcs])
                    S.op("dve", TT(cv[:, c0:c0 + w], P[b0 + 1][:, 0:w], gcs[:, 0:w], ALU.mult), reads=[tP[b0 + 1], t_gcs], writes=[t_cv])
                wi = (l * 4 + cc) * 3
                S.op("act", ACT(acc, cv[:, 1:1 + SEG], AF.Copy, scale=cva[:, wi + 1:wi + 2]), reads=[t_cv, t_setup], writes=[t_acc])
                S.op("dve", STT(acc, cv[:, 0:SEG], cva[:, wi:wi + 1], acc, ALU.mult, ALU.add), reads=[t_cv, t_setup, t_acc], writes=[t_acc])
                S.op("dve", STT(acc, cv[:, 2:2 + SEG], cva[:, wi + 2:wi + 3], acc, ALU.mult, ALU.add), reads=[t_cv, t_setup, t_acc], writes=[t_acc])
                for tg in range(4):
                    bank = 4 + tg % 2
                    proj_fm(bank, Wb_, 128, 1 + 512 * tg, 512, t_wb)
                    S.op("dve", TT(yT[:, cc, 512 * tg:512 * tg + 512], P[bank][:, 0:512], acc[:, 512 * tg:512 * tg + 512], ALU.mult),
                         reads=[tP[bank], t_acc], writes=[t_y])
                if cc + 2 < 4:
                    load_conv_w(cc + 2)
            S.fence()
            stage(14)
            t_zb2, t_sqz, t_rsz2, t_zt = [Tile("zbuf0"), Tile("zbuf1")], [Tile("sqz0"), Tile("sqz1")], [Tile("rsz0"), Tile("rsz1")], Tile("ztmp")
            for tg in range(8):
                zb, t_zb, rsz, t_rsz = zbuf[tg % 2], t_zb2[tg % 2], rs_z[tg % 2], t_rsz2[tg % 2]
                sbank = 6 + tg % 2
                cs = slice(256 * tg, 256 * tg + 256)
                xs = slice(1 + 256 * tg, 1 + 256 * tg + 256)
                for m in range(KC):
                    bank = m % 6
                    for c in range(KC):
                        S.op("pe", MM(P[bank][:, 0:256], Wout[:, c, 128 * m:128 * m + 128], yT[:, c, cs], c == 0, c == KC - 1),
                             reads=[t_wo, t_y], writes=[tP[bank]], inc=(c == KC - 1))
                    S.op("act", ACT(zb[:, m, :], P[bank][:, 0:256], AF.Copy), reads=[tP[bank]], writes=[t_zb])
                    S.op("act", ACT(sqz[m % 2], P[bank][:, 0:256], AF.Square), reads=[tP[bank]], writes=[t_sqz[m % 2]])
                    S.op("pe", MM(P[sbank][:, 0:256], ones[:], sqz[m % 2], m == 0, m == KC - 1),
                         reads=[t_sqz[m % 2], t_setup], writes=[tP[sbank]], inc=(m == KC - 1))
                rstd_from(P[sbank][:, 0:256], rsz, D, [tP[sbank]], t_rsz)
                for m in range(KC):
                    S.op("dve", STT(z_tmp, zb[:, m, :], gain(1, l, m), rsz, ALU.mult, ALU.mult), reads=[t_zb, t_setup, t_rsz], writes=[t_zt])
                    S.op("dve", TT(xT[:, m, xs], xT[:, m, xs], z_tmp, ALU.add), reads=[t_zt, t_x], writes=[t_x])
            S.fence()
            stage(15)
            exchange_x(payx, gx)

        def ffn(l, last):
            S.fence()
            rmsnorm_to_h(2, l)
            t_wg, t_wv = [Tile("wg0"), Tile("wg1")], [Tile("wv0"), Tile("wv1")]
            t_wd = [Tile("wd0"), Tile("wd1")]
            t_ag, t_av = [Tile("accg0"), Tile("accg1")], [Tile("accv0"), Tile("accv1")]
            t_a, t_zf, t_sqf, t_rsf, t_ft = Tile("aT"), Tile("zf"), [Tile("sqf0"), Tile("sqf1")], [Tile("rsf0"), Tile("rsf1")], Tile("ftmp")
            it = 0
            for (tok0, subs) in FFN_SUPER:
                for cp in range(NCP):
                    sl = cp % 2
                    S.op("pool", DMA(Wg[sl], wblk(wup_d, l * 2 * NCP + cp, KC)), writes=[t_wg[sl]], dma=f"ld_wg{sl}")
                    S.op("pool", DMA(Wv[sl], wblk(wup_d, l * 2 * NCP + NCP + cp, KC)), writes=[t_wv[sl]], dma=f"ld_wv{sl}")
                    a0 = 0
                    for si, wo_ in enumerate(subs):
                        t0 = tok0 + a0
                        u0, uw = t0, wo_ + 2
                        pb = 2 * (it % 4)
                        par = it % 2
                        it += 1
                        for c in range(KC):
                            S.op("pe", MM(P[pb][:, 0:uw], Wg[sl][:, c, :], hT[:, c, u0:u0 + uw], c == 0, c == KC - 1),
                                 reads=[t_wg[sl], t_h], writes=[tP[pb]], inc=(c == KC - 1))
                        for c in range(KC):
                            S.op("pe", MM(P[pb + 1][:, 0:uw], Wv[sl][:, c, :], hT[:, c, u0:u0 + uw], c == 0, c == KC - 1),
                                 reads=[t_wv[sl], t_h], writes=[tP[pb + 1]], inc=(c == KC - 1))
                        for (bank, accb, t_acc_, chunk) in [(pb, accg[par], t_ag[par], cp), (pb + 1, accv[par], t_av[par], NCP + cp)]:
                            wi = (l * 2 * NCP + chunk) * 3
                            ao = accb[:, 0:wo_]
                            S.op("act", ACT(ao, P[bank][:, 1:1 + wo_], AF.Copy, scale=cvf[:, wi + 1:wi + 2]), reads=[tP[bank], t_setup], writes=[t_acc_])
                            S.op("dve", STT(ao, P[bank][:, 0:wo_], cvf[:, wi:wi + 1], ao, ALU.mult, ALU.add), reads=[tP[bank], t_setup, t_acc_], writes=[t_acc_])
                            S.op("dve", STT(ao, P[bank][:, 2:2 + wo_], cvf[:, wi + 2:wi + 3], ao, ALU.mult, ALU.add), reads=[tP[bank], t_setup, t_acc_], writes=[t_acc_])
                        S.op("act", ACT(accg[par][:, 0:wo_], accg[par][:, 0:wo_], AF.Silu), reads=[t_ag[par]], writes=[t_ag[par]])
                        S.op("dve", TT(aT[:, cp, a0:a0 + wo_], accg[par][:, 0:wo_], accv[par][:, 0:wo_], ALU.mult),
                             reads=[t_ag[par], t_av[par]], writes=[t_a])
                        a0 += wo_
                for m in range(KC):
                    sl = m % 2
                    S.op("pool", DMA(Wd[sl], wblk(wdn_d, l * KC + m, NCP)), writes=[t_wd[sl]], dma=f"ld_wd{sl}")
                    a0 = 0
                    for si, wo_ in enumerate(subs):
                        bank = 4 + (2 * m + si) % 2
                        for cp in range(NCP):
                            S.op("pe", MM(P[bank][:, 0:wo_], Wd[sl][:, cp, :], aT[:, cp, a0:a0 + wo_], cp == 0, cp == NCP - 1),
                                 reads=[t_wd[sl], t_a], writes=[tP[bank]], inc=(cp == NCP - 1))
                        S.op("act", ACT(zf[:, m, a0:a0 + wo_], P[bank][:, 0:wo_], AF.Copy), reads=[tP[bank]], writes=[t_zf])
                        S.op("act", ACT(sqf[si][:, 0:wo_], P[bank][:, 0:wo_], AF.Square), reads=[tP[bank]], writes=[t_sqf[si]])
                        S.op("pe", MM(P[6 + si][:, 0:wo_], ones[:], sqf[si][:, 0:wo_], m == 0, m == KC - 1),
                             reads=[t_sqf[si], t_setup], writes=[tP[6 + si]], inc=(m == KC - 1))
                        a0 += wo_
                a0 = 0
                for si, wo_ in enumerate(subs):
                    rstd_from(P[6 + si][:, 0:wo_], rs_f[si][:, 0:wo_], D, [tP[6 + si]], t_rsf[si])
                    xs = slice(1 + tok0 + a0, 1 + tok0 + a0 + wo_)
                    for m in range(KC):
                        S.op("dve", STT(f_tmp[:, 0:wo_], zf[:, m, a0:a0 + wo_], gain(3, l, m), rs_f[si][:, 0:wo_], ALU.mult, ALU.mult),
                             reads=[t_zf, t_setup, t_rsf[si]], writes=[t_ft])
                        S.op("dve", TT(xT[:, m, xs], xT[:, m, xs], f_tmp[:, 0:wo_], ALU.add), reads=[t_ft, t_x], writes=[t_x])
                    a0 += wo_
            S.fence()
            if not last:
                exchange_x(payx2, gx2)

        try:
            for l in range(n_layers):
                mixer(l)
                if stop == "mixer" and l == n_layers - 1:
                    break
                ffn(l, last=(l == n_layers - 1))
        except _Cut:
            pass

        S.fence()
        out_v = out_d.rearrange("(c p) t -> p c t", p=128)
        for c in range(KC):
            S.op("sp", DMA(out_v[:, c, :], xT[:, c, 1:1 + SEG]), reads=[t_x], dma="sto")
        S.fence()
        sems = {n: es.enter_context(nc.semaphore(n)) for n in S.sem_names()}
        with nc.Block() as block:
            S.emit(block, sems)
    return nc


def _layout_inputs(x, norm_mix_pre, norm_mix_post, norm_ffn_pre, norm_ffn_post, w_in, conv_a,
                   gate_up_fwd, gate_bias_fwd, gate_up_bwd, gate_bias_bwd, gla_head_norm,
                   w_out, w_up, conv_ffn, w_down, n_layers=DEPTH):
    f = lambda a: np.ascontiguousarray(np.asarray(a, np.float32))
    x = f(x)
    gains = np.stack([f(norm_mix_pre), f(norm_mix_post), f(norm_ffn_pre), f(norm_ffn_post)])
    gains = f(gains.reshape(4 * DEPTH * KC, 128).T)
    ghn = f(f(gla_head_norm).T)
    cva = f(f(conv_a).reshape(DEPTH, 3, 4, 128).transpose(3, 0, 2, 1).reshape(128, DEPTH * 12))
    cvf = f(f(conv_ffn).reshape(DEPTH, 3, 2 * NCP, 128).transpose(3, 0, 2, 1).reshape(128, DEPTH * 132))
    consts = np.zeros((128, 4), np.float32)
    consts[:, 0] = EPS
    ones = np.ones((128, 128), np.float32)
    ones_row = np.ones((1, SEG), np.float32)
    j = np.arange(128)[:, None]
    i = np.arange(128)[None, :]
    sc = np.float32(-1.0 / 16.0)
    tri = np.concatenate([(j <= i) * sc, (j >= i) * sc, (j > i) * sc, (j < i) * sc], axis=1).astype(np.float32)
    m01 = np.concatenate([(j <= i), (j >= i)], axis=1).astype(np.float32)
    gup = np.zeros((DEPTH, HEADS, 33, 128), np.float32)
    guf, gub, gbf, gbb = f(gate_up_fwd), f(gate_up_bwd), f(gate_bias_fwd), f(gate_bias_bwd)
    for l in range(DEPTH):
        for h in range(HEADS):
            hs = slice(64 * h, 64 * h + 64)
            gup[l, h, 0:16, 0:64] = guf[l][:, hs]
            gup[l, h, 16:32, 64:128] = gub[l][:, hs]
            gup[l, h, 32, 0:64] = gbf[l][hs]
            gup[l, h, 32, 64:128] = gbb[l][hs]
    gup = gup.reshape(DEPTH * HEADS * 33, 128)
    L_ = n_layers
    wi = f(w_in)[:L_]

    def blocks(w, nb, bw):
        return f(w.reshape(L_, KC, 128, nb, bw).transpose(0, 3, 2, 1, 4).reshape(L_ * nb * 128, KC * bw))
    w_in128 = blocks(np.concatenate([wi[:, :, 0:OFF_Q], wi[:, :, OFF_GO:OFF_LR]], axis=2), 16, 128)
    w_in64 = blocks(wi[:, :, OFF_Q:OFF_K], HEADS, 64)
    wkv = np.concatenate([wi[:, :, OFF_K:OFF_V].reshape(L_, D, HEADS, 64),
                          wi[:, :, OFF_V:OFF_GO].reshape(L_, D, HEADS, 128)], axis=3).reshape(L_, D, HEADS * 192)
    w_inkv = blocks(wkv, HEADS, 192)
    w_in32 = blocks(wi[:, :, OFF_LR:OFF_LR + 32], 1, 32)
    w_out2 = f(w_out)[:L_].reshape(L_ * D, D)
    w_up2 = blocks(f(w_up)[:L_], 2 * NCP, 128)
    w_dn2 = f(f(w_down)[:L_].reshape(L_, NCP, 128, KC, 128).transpose(0, 3, 2, 1, 4).reshape(L_ * KC * 128, NCP * 128))
    maps = []
    for c in range(NCORES):
        b, s = c // 4, c % 4
        xt = np.zeros((D, TW), np.float32)
        lo, hi = s * SEG - 1, (s + 1) * SEG + 1
        clo, chi = max(lo, 0), min(hi, SEQ)
        xt[:, clo - lo:clo - lo + (chi - clo)] = x[b, clo:chi, :].T
        mk = np.zeros((128, 24), np.float32)
        for r in range(4):
            mk[:, 0 + r] = 1.0 if r < s else 0.0
            mk[:, 4 + r] = 1.0 - mk[:, 0 + r]
            mk[:, 8 + r] = 1.0 if r > s else 0.0
            mk[:, 12 + r] = 1.0 - mk[:, 8 + r]
            mk[:, 16 + r] = 1.0 if r == s - 1 else 0.0
            mk[:, 20 + r] = 1.0 if r == s + 1 else 0.0
        maps.append({"xT": xt, "gains": gains, "ghn": ghn, "cva": cva, "cvf": cvf, "consts": consts, "masks": mk,
                     "ones": ones, "ones_row": ones_row, "tri": tri, "m01": m01, "gup": gup,
                     "w_in128": w_in128, "w_in64": w_in64, "w_inkv": w_inkv, "w_in32": w_in32,
                     "w_out": w_out2, "w_up": w_up2, "w_down": w_dn2})
    return maps


def _gather(res):
    out = np.empty((BATCH, SEQ, D), np.float32)
    for c in range(NCORES):
        b, s = c // 4, c % 4
        out[b, s * SEG:(s + 1) * SEG, :] = res.results[c]["outT"].T
    return out


def kernel(x, norm_mix_pre, norm_mix_post, norm_ffn_pre, norm_ffn_post, w_in, conv_a,
           gate_up_fwd, gate_bias_fwd, gate_up_bwd, gate_bias_bwd, gla_head_norm,
           w_out, w_up, conv_ffn, w_down):
    nc = build_program()
    maps = _layout_inputs(x, norm_mix_pre, norm_mix_post, norm_ffn_pre, norm_ffn_post, w_in, conv_a,
                          gate_up_fwd, gate_bias_fwd, gate_up_bwd, gate_bias_bwd, gla_head_norm,
                          w_out, w_up, conv_ffn, w_down)
    res = run_bass_kernel_spmd(nc, maps, core_ids=list(range(NCORES)))
    return _gather(res)
```

```python
import numpy as np
from contextlib import ExitStack
import concourse.bass as bass
import concourse.mybir as mybir
from concourse.bass_utils import run_bass_kernel_spmd

F32 = mybir.dt.float32
BF16 = mybir.dt.bfloat16
ALU = mybir.AluOpType
AF = mybir.ActivationFunctionType
AX = mybir.AxisListType

D = 1024
DEPTH = 4
BATCH = 2
SEQ = 8192
NCORES = 8
SEG = 2048
TW = SEG + 2
KC = D // 128
EPS = 1e-6
D_IN = 3104
D_FF = 2816
NCP = D_FF // 128
HEADS = 4
NT = SEG // 128
OFF_GB, OFF_GC, OFF_GV, OFF_Q, OFF_K, OFF_V, OFF_GO, OFF_LR = 0, 512, 1024, 1536, 1792, 2048, 2560, 3072
GROUPS = [[0, 1, 2, 3], [4, 5, 6, 7]]
ARENA_WORDS = 23360
FFN_SUPER = [(0, [342, 342]), (684, [341, 341]), (1366, [341, 341])]


class Tile:
    __slots__ = ("name", "w", "r")

    def __init__(self, name):
        self.name = name
        self.w = None
        self.r = {}


class Sched:
    ENG = ("pe", "act", "dve", "pool", "sp")

    def __init__(self):
        self.ops = {e: [] for e in self.ENG}
        self.cnt = {e: 0 for e in self.ENG}
        self.known = {e: {} for e in self.ENG}
        self.dma_cnt = {}
        self.snap = {}

    def op(self, eng, fn, reads=(), writes=(), dma=None, inc=True):
        deps = {}

        def add(ev):
            if ev is None:
                return
            k, v = ev
            if deps.get(k, 0) < v:
                deps[k] = v
        for t in reads:
            add(t.w)
        for t in writes:
            add(t.w)
            for k, v in t.r.items():
                add((k, v))
        kn = self.known[eng]
        need = [(k, v) for k, v in deps.items() if not (k == eng and eng == "pe") and kn.get(k, 0) < v]
        waits = list(need)
        for d_ in need:
            if any(o is not d_ and self.snap.get(o, {}).get(d_[0], 0) >= d_[1] for o in waits):
                waits.remove(d_)
        for k, v in need:
            if kn.get(k, 0) < v:
                kn[k] = v
        for o in waits:
            for k2, v2 in self.snap.get(o, {}).items():
                if kn.get(k2, 0) < v2:
                    kn[k2] = v2
        if dma is not None:
            self.dma_cnt[dma] = self.dma_cnt.get(dma, 0) + 16
            ev = (dma, self.dma_cnt[dma])
            incv = 16
        elif inc:
            self.cnt[eng] += 1
            ev = (eng, self.cnt[eng])
            incv = 1
        else:
            ev = (eng, self.cnt[eng] + 1)
            incv = 0
        if dma is not None or inc:
            self.snap[ev] = dict(kn)
        for t in reads:
            if t.r.get(ev[0], 0) < ev[1]:
                t.r[ev[0]] = ev[1]
        for t in writes:
            t.w = ev
            t.r = {}
        self.ops[eng].append((waits, fn, ev[0], incv))

    def fence(self):
        allv = [(k, v) for k, v in self.cnt.items() if v > 0] + list(self.dma_cnt.items())
        for e in self.ENG:
            kn = self.known[e]
            waits = []
            for k, v in allv:
                if k == e and e == "pe":
                    continue
                if kn.get(k, 0) >= v:
                    continue
                kn[k] = v
                waits.append((k, v))
            if waits:
                self.ops[e].append((waits, None, None, 0))

    def sem_names(self):
        return list(self.ENG) + list(self.dma_cnt.keys())

    def emit(self, block, sems):
        engs = {"pe": block.tensor, "act": block.scalar, "dve": block.vector,
                "pool": block.gpsimd, "sp": block.sync}
        for name in self.ENG:
            ops = self.ops[name]

            def body(engine, ops=ops):
                for waits, fn, evk, incv in ops:
                    for k, v in waits:
                        engine.wait_ge(sems[k], v)
                    if fn is None:
                        continue
                    ins = fn(engine)
                    if incv:
                        ins.then_inc(sems[evk], incv)
            engs[name](body)


def MM(out, lhsT, rhs, start, stop):
    return lambda e: e.matmul(out, lhsT=lhsT, rhs=rhs, start=start, stop=stop)


def ACT(out, in_, func, **kw):
    return lambda e: e.activation(out=out, in_=in_, func=func, **kw)


def TT(out, in0, in1, op):
    return lambda e: e.tensor_tensor(out=out, in0=in0, in1=in1, op=op)


def STT(out, in0, scalar, in1, op0, op1):
    return lambda e: e.scalar_tensor_tensor(out=out, in0=in0, scalar=scalar, in1=in1, op0=op0, op1=op1)


def TS(out, in0, s1, s2, op0, op1):
    return lambda e: e.tensor_scalar(out=out, in0=in0, scalar1=s1, scalar2=s2, op0=op0, op1=op1)


def RSUM(out, in_):
    return lambda e: e.reduce_sum(out=out, in_=in_, axis=AX.X)


def DMA(out, in_):
    return lambda e: e.dma_start(out=out, in_=in_)


def AG(in_ap, out_ap):
    return lambda e: e.collective_compute("AllGather", ALU.bypass, replica_groups=GROUPS, ins=[in_ap], outs=[out_ap])


class Arena:
    def __init__(self, ap, nwords, alias0=False):
        self.ap, self.n, self.off, self.alias0 = ap, nwords, 0, alias0

    def seek(self, off):
        self.off = off

    def _take(self, words):
        words = (words + 7) // 8 * 8
        a = self.off
        self.off += words
        assert self.off <= self.n, f"arena overflow {self.off} > {self.n}"
        if self.alias0:
            return self.ap[:, 0:words]
        return self.ap[:, a:a + words]

    def f32(self, cols):
        return self._take(cols)[:, 0:cols]

    def bf16(self, cols):
        return self._take((cols + 1) // 2).bitcast(BF16)[:, 0:cols]


def _col_tiles(total, width):
    out, c = [], 0
    while c < total:
        w = min(width, total - c)
        out.append((c, w))
        c += w
    return out


class _Cut(Exception):
    pass


def build_program(n_layers=DEPTH, stop=None, cut=None):
    nc = bass.Bass("TRN2", target_bir_lowering=False)

    def din(name, shape):
        return nc.dram_tensor(name, shape, F32, kind="ExternalInput").ap()
    xT_d = din("xT", [D, TW])
    gains_d = din("gains", [128, 4 * DEPTH * KC])
    ghn_d = din("ghn", [128, DEPTH])
    cva_d = din("cva", [128, DEPTH * 12])
    cvf_d = din("cvf", [128, DEPTH * 132])
    cst_d = din("consts", [128, 4])
    msk_d = din("masks", [128, 24])
    ones_d = din("ones", [128, 128])
    onesrow_d = din("ones_row", [1, SEG])
    tri_d = din("tri", [128, 512])
    m01_d = din("m01", [128, 256])
    gup_d = din("gup", [DEPTH * HEADS * 33, 128])
    win128_d = din("w_in128", [n_layers * 16 * 128, KC * 128])
    win64_d = din("w_in64", [n_layers * HEADS * 128, KC * 64])
    winkv_d = din("w_inkv", [n_layers * HEADS * 128, KC * 192])
    win32_d = din("w_in32", [n_layers * 128, KC * 32])
    wout_d = din("w_out", [n_layers * D, D])
    wup_d = din("w_up", [n_layers * 2 * NCP * 128, KC * 128])
    wdn_d = din("w_down", [n_layers * KC * 128, NCP * 128])

    def wblk(d, idx, c):
        return d[idx * 128:(idx + 1) * 128, :].rearrange("p (c n) -> p c n", c=c)
    out_d = nc.dram_tensor("outT", [D, SEG], F32, kind="ExternalOutput").ap()
    cc_in = nc.dram_tensor("cc_in", [64, 264], F32)
    cc_out = nc.dram_tensor("cc_out", [4 * 64, 264], F32)
    cx_in = nc.dram_tensor("cx_in", [128, 16], F32)
    cx_out = nc.dram_tensor("cx_out", [4 * 128, 16], F32)

    S = Sched()

    def stage(k):
        if cut is not None and k > cut:
            raise _Cut()

    with ExitStack() as es:
        def sb(name, shape, dt):
            return es.enter_context(nc.sbuf_tensor(name, shape, dt))

        xT = sb("xT_s", [128, KC, TW], F32)
        hT = sb("hT_s", [128, KC, TW], BF16)
        gains = sb("gains_s", [128, 4 * DEPTH * KC], F32)
        ghn = sb("ghn_s", [128, DEPTH], F32)
        cva = sb("cva_s", [128, DEPTH * 12], F32)
        cvf = sb("cvf_s", [128, DEPTH * 132], F32)
        cst = sb("cst_s", [128, 4], F32)
        msk = sb("msk_s", [128, 24], F32)
        ones = sb("ones_s", [128, 128], BF16)
        tri = sb("tri_s", [128, 512], BF16)
        m01 = sb("m01_s", [128, 256], F32)
        lrT = sb("lrT_s", [33, SEG], BF16)
        sqn = [sb(f"sqn{i}", [128, 512], BF16) for i in range(2)]
        rsn = [sb(f"rsn{i}", [128, 512], F32) for i in range(2)]
        import os
        _small = os.environ.get("KDBG_SMALLARENA") == "1"
        arena_t = sb("arena", [128, 8200 if _small else ARENA_WORDS], F32)
        AR = Arena(arena_t, ARENA_WORDS, alias0=_small)
        if os.environ.get("KDBG_PSUM2") == "1":
            _p2 = [es.enter_context(nc.psum_tensor(f"bank{i}", [128, 512], F32)) for i in range(2)]
            P = [_p2[i % 2] for i in range(8)]
        else:
            P = [es.enter_context(nc.psum_tensor(f"bank{i}", [128, 512], F32)) for i in range(8)]
        tP = [Tile(f"bank{i}") for i in range(8)]
        t_x, t_h, t_setup, t_lrT = Tile("x"), Tile("h"), Tile("setup"), Tile("lrT")
        t_sqn, t_rsn = [Tile("sqn0"), Tile("sqn1")], [Tile("rsn0"), Tile("rsn1")]
        eps_ap = cst[:, 0:1]

        def gain(kind, layer, c):
            i = (kind * DEPTH + layer) * KC + c
            return gains[:, i:i + 1]

        xT_v = xT_d.rearrange("(c p) t -> p c t", p=128)
        for c in range(KC):
            S.op("sp", DMA(xT[:, c, :], xT_v[:, c, :]), writes=[t_x], dma="ldx")
        for dst, src in [(gains, gains_d), (ghn, ghn_d), (cva, cva_d), (cvf, cvf_d), (cst, cst_d), (msk, msk_d), (m01, m01_d)]:
            S.op("sp", DMA(dst[:], src), writes=[t_setup], dma="setup")
        S.op("pool", DMA(ones[:], ones_d), writes=[t_setup], dma="setup_c")
        S.op("pool", DMA(tri[:], tri_d), writes=[t_setup], dma="setup_c")
        import os
        if os.environ.get("KDBG_NOROW") != "1":
            S.op("pool", DMA(lrT[32:33, :], onesrow_d), writes=[t_lrT], dma="setup_c")
        if os.environ.get("KDBG_TOUCH") == "1":
            scr = sb("scr", [128, 64], F32)
            t_scr = Tile("scr")
            for i_, src in enumerate([win128_d, wout_d, wup_d, wdn_d, gup_d]):
                S.op("sp", DMA(scr[:, 8 * i_:8 * i_ + 8], src[0:128, 0:8]), writes=[t_scr], dma="touch")
            S.op("sp", DMA(scr[0:1, 48:56], onesrow_d[0:1, 0:8]), writes=[t_scr], dma="touch")
        S.fence()

        AR.seek(0)
        yT = AR.bf16(KC * SEG).rearrange("p (c t) -> p c t", c=KC)
        HEAD0 = AR.off
        qin = [AR.bf16(SEG) for _ in range(2)]
        kin = [AR.bf16(SEG) for _ in range(2)]
        kout = [AR.bf16(NT * 64).rearrange("p (n d) -> p n d", n=NT) for _ in range(2)]
        vtok = AR.bf16(NT * 128).rearrange("p (n e) -> p n e", n=NT)
        Sbf = [AR.bf16(NT * 128).rearrange("p (n e) -> p n e", n=NT) for _ in range(2)]
        Wq = AR.bf16(KC * 64).rearrange("p (c n) -> p c n", c=KC)
        Wkv = AR.bf16(KC * 192).rearrange("p (c n) -> p c n", c=KC)
        Wgo = AR.bf16(KC * 128).rearrange("p (c n) -> p c n", c=KC)
        Wlr = AR.bf16(KC * 32).rearrange("p (c n) -> p c n", c=KC)
        gup = AR.bf16(128)
        sp32 = [AR.f32(128) for _ in range(2)]
        sp_hi = [AR.bf16(128) for _ in range(2)]
        sp_lo = [AR.bf16(128) for _ in range(2)]
        tmpE = [[AR.f32(128) for _ in range(4)] for _ in range(2)]
        ekd = [AR.f32(128) for _ in range(2)]
        cl = [AR.f32(NT) for _ in range(2)]
        dec = [AR.f32(NT) for _ in range(2)]
        clsum = [AR.f32(1) for _ in range(2)]
        pay = AR.f32(264)
        gbuf = AR.f32(4 * 264).rearrange("p (r n) -> p r n", r=4)
        Ain = [AR.f32(128) for _ in range(2)]
        Sx = [AR.f32(128) for _ in range(2)]
        tmpKV = AR.f32(128)
        Dm = AR.f32(8)
        sTb = [[AR.bf16(128) for _ in range(2)] for _ in range(2)]
        osq = [AR.bf16(128) for _ in range(2)]
        rs_o = [AR.f32(128) for _ in range(2)]
        o_tmp = [AR.f32(128) for _ in range(2)]
        AR.seek(HEAD0)
        Wb = [AR.bf16(KC * 128).rearrange("p (c n) -> p c n", c=KC) for _ in range(2)]
        Wc = [AR.bf16(KC * 128).rearrange("p (c n) -> p c n", c=KC) for _ in range(2)]
        Wvv = [AR.bf16(KC * 128).rearrange("p (c n) -> p c n", c=KC) for _ in range(2)]
        cv = AR.f32(TW)
        acc = AR.f32(SEG)
        gcs = AR.f32(512)
        Wout = AR.bf16(KC * D).rearrange("p (c n) -> p c n", c=KC)
        print("arena: conv scratch + Wout end at", AR.off)
        AR.seek(HEAD0)
        zbuf = [AR.f32(KC * 256).rearrange("p (m t) -> p m t", m=KC) for _ in range(2)]
        sqz = [AR.bf16(256) for _ in range(2)]
        rs_z = [AR.f32(256) for _ in range(2)]
        z_tmp = AR.f32(256)
        payx = AR.f32(16)
        gx = AR.f32(64).rearrange("p (r n) -> p r n", r=4)
        AR.seek(0)
        aT = AR.bf16(NCP * 688).rearrange("p (c t) -> p c t", c=NCP)
        zf = AR.f32(KC * 684).rearrange("p (m t) -> p m t", m=KC)
        Wg = [AR.bf16(KC * 128).rearrange("p (c n) -> p c n", c=KC) for _ in range(2)]
        Wv = [AR.bf16(KC * 128).rearrange("p (c n) -> p c n", c=KC) for _ in range(2)]
        Wd = [AR.bf16(NCP * 128).rearrange("p (c n) -> p c n", c=NCP) for _ in range(2)]
        accg = [AR.f32(344) for _ in range(2)]
        accv = [AR.f32(344) for _ in range(2)]
        sqf = [AR.bf16(344) for _ in range(2)]
        rs_f = [AR.f32(344) for _ in range(2)]
        f_tmp = AR.f32(344)
        payx2 = AR.f32(16)
        gx2 = AR.f32(64).rearrange("p (r n) -> p r n", r=4)

        def rstd_from(ps_ap, out_ap, n, reads, wtile):
            S.op("act", ACT(out_ap, ps_ap, AF.Ln, scale=1.0 / n, bias=eps_ap), reads=list(reads) + [t_setup], writes=[wtile])
            S.op("act", ACT(out_ap, out_ap, AF.Exp, scale=-0.5), reads=[wtile], writes=[wtile])

        def rmsnorm_to_h(kind, layer):
            for ti, (c0, w) in enumerate(_col_tiles(TW, 512)):
                b = ti % 2
                bank = 6 + b
                for c in range(KC):
                    S.op("act", ACT(sqn[c % 2][:, 0:w], xT[:, c, c0:c0 + w], AF.Square), reads=[t_x], writes=[t_sqn[c % 2]])
                    S.op("pe", MM(P[bank][:, 0:w], ones[:], sqn[c % 2][:, 0:w], c == 0, c == KC - 1),
                         reads=[t_sqn[c % 2], t_setup], writes=[tP[bank]])
                rstd_from(P[bank][:, 0:w], rsn[b][:, 0:w], D, [tP[bank]], t_rsn[b])
                for c in range(KC):
                    S.op("dve", STT(hT[:, c, c0:c0 + w], xT[:, c, c0:c0 + w], gain(kind, layer, c), rsn[b][:, 0:w], ALU.mult, ALU.mult),
                         reads=[t_x, t_setup, t_rsn[b]], writes=[t_h])

        def proj_fm(bank, Wt, m, c0, w, wtile):
            for c in range(KC):
                S.op("pe", MM(P[bank][0:m, 0:w], Wt[:, c, :], hT[:, c, c0:c0 + w], c == 0, c == KC - 1),
                     reads=[wtile, t_h], writes=[tP[bank]], inc=(c == KC - 1))

        def exchange_x(pay_ap, g_ap):
            t_px, t_cxi, t_cxo, t_gx = Tile("payx"), Tile("cx_in"), Tile("cx_out"), Tile("gx")
            pv = pay_ap.rearrange("p (c two) -> p c two", two=2)
            S.op("act", ACT(pv[:, :, 0], xT[:, :, 1], AF.Copy), reads=[t_x], writes=[t_px])
            S.op("act", ACT(pv[:, :, 1], xT[:, :, SEG], AF.Copy), reads=[t_x], writes=[t_px])
            S.op("sp", DMA(cx_in.ap(), pay_ap), reads=[t_px], writes=[t_cxi], dma="xch_o")
            S.op("pool", AG(cx_in.ap().opt(), cx_out.ap().opt()), reads=[t_cxi], writes=[t_cxo])
            S.op("sp", DMA(g_ap, cx_out.ap().rearrange("(r p) n -> p r n", p=128)), reads=[t_cxo], writes=[t_gx], dma="xch_i")
            for halo_col, src_two, mbase in [(0, 1, 16), (TW - 1, 0, 20)]:
                for r in range(4):
                    src = g_ap[:, r, :].rearrange("p (c two) -> p c two", two=2)[:, :, src_two]
                    m_ap = msk[:, mbase + r:mbase + r + 1]
                    if r == 0:
                        S.op("act", ACT(xT[:, :, halo_col], src, AF.Copy, scale=m_ap), reads=[t_gx, t_setup], writes=[t_x])
                    else:
                        S.op("dve", STT(xT[:, :, halo_col], src, m_ap, xT[:, :, halo_col], ALU.mult, ALU.add),
                             reads=[t_gx, t_setup, t_x], writes=[t_x])

        def gla_head(l, h, t_y):
            t_wq, t_wkv, t_wgo, t_gup = Tile("wq"), Tile("wkv"), Tile("wgo"), Tile("gup")
            t_qin, t_kin = [Tile("qinf"), Tile("qinb")], [Tile("kinf"), Tile("kinb")]
            t_kout, t_v, t_sbf = [Tile("koutf"), Tile("koutb")], Tile("vtok"), [Tile("sbff"), Tile("sbfb")]
            t_sp2, t_hi2, t_lo2, t_ekd2 = ([Tile(f"{nm}{i}") for i in range(2)] for nm in ("sp", "sphi", "splo", "ekd"))
            t_tmpE2 = [[Tile(f"tmpE{p}{i}") for i in range(4)] for p in range(2)]
            t_cl, t_dec, t_cs, t_pay = [Tile("clf"), Tile("clb")], [Tile("decf"), Tile("decb")], [Tile("csf"), Tile("csb")], Tile("pay")
            stage(3)
            S.op("pool", DMA(Wq, wblk(win64_d, l * HEADS + h, KC)), writes=[t_wq], dma="ld_wq")
            S.op("pool", DMA(Wkv, wblk(winkv_d, l * HEADS + h, KC)), writes=[t_wkv], dma="ld_wkv")
            S.op("pool", DMA(Wgo, wblk(win128_d, l * 16 + 12 + h, KC)), writes=[t_wgo], dma="ld_wgo")
            g0 = (l * HEADS + h) * 33
            S.op("pool", DMA(gup[0:33, :], gup_d[g0:g0 + 33, :]), writes=[t_gup], dma="ld_gup")
            for tg in range(4):
                bank = tg % 2
                proj_fm(bank, Wgo, 128, 1 + 512 * tg, 512, t_wgo)
                S.op("act", ACT(yT[:, 4 + h, 512 * tg:512 * tg + 512], P[bank][:, 0:512], AF.Silu), reads=[tP[bank]], writes=[t_y])
            for n in range(NT):
                par = n % 2
                pb = 4 * par
                bA, bB, bC, bD = pb, pb + 1, pb + 2, pb + 3
                sp32_, sp_hi_, sp_lo_, ekd_, tmpE_ = sp32[par], sp_hi[par], sp_lo[par], ekd[par], tmpE[par]
                t_sp, t_hi, t_lo, t_ekd, t_tmpE = t_sp2[par], t_hi2[par], t_lo2[par], t_ekd2[par], t_tmpE2[par]
                c0 = 1 + 128 * n
                tl = slice(128 * n, 128 * n + 128)
                stage(4 if n == 0 else 7)
                for c in range(KC):
                    S.op("pe", MM(P[bA][0:64, 0:128], Wq[:, c, :], hT[:, c, c0:c0 + 128], c == 0, c == KC - 1),
                         reads=[t_wq, t_h], writes=[tP[bA]], inc=False)
                for c in range(KC):
                    S.op("pe", MM(P[bA][0:64, 128:256], Wkv[:, c, 0:64], hT[:, c, c0:c0 + 128], c == 0, c == KC - 1),
                         reads=[t_wkv, t_h], writes=[tP[bA]], inc=(c == KC - 1))
                for c in range(KC):
                    S.op("pe", MM(P[bB][:, 0:192], hT[:, c, c0:c0 + 128], Wkv[:, c, :], c == 0, c == KC - 1),
                         reads=[t_wkv, t_h], writes=[tP[bB]], inc=(c == KC - 1))
                stage(5 if n == 0 else 7)
                S.op("pe", MM(P[bC][:, 0:128], lrT[0:33, tl], gup[0:33, :], True, True), reads=[t_lrT, t_gup], writes=[tP[bC]])
                S.op("act", ACT(sp32_, P[bC][:, 0:128], AF.Exp, scale=-1.0), reads=[tP[bC]], writes=[t_sp])
                S.op("act", ACT(sp32_, sp32_, AF.Ln, bias=1.0), reads=[t_sp], writes=[t_sp])
                S.op("act", ACT(sp_hi_, sp32_, AF.Copy), reads=[t_sp], writes=[t_hi])
                S.op("dve", TT(sp_lo_, sp32_, sp_hi_, ALU.subtract), reads=[t_sp, t_hi], writes=[t_lo])
                stage(6 if n == 0 else 7)
                for d_, (lo_c, trc) in enumerate([(0, 0), (64, 128)]):
                    oc = 128 * d_
                    S.op("pe", MM(P[bD][0:64, oc:oc + 128], sp_hi_[:, lo_c:lo_c + 64], tri[:, trc:trc + 128], True, False),
                         reads=[t_hi, t_setup], writes=[tP[bD]], inc=False)
                    S.op("pe", MM(P[bD][0:64, oc:oc + 128], sp_lo_[:, lo_c:lo_c + 64], tri[:, trc:trc + 128], False, True),
                         reads=[t_lo, t_setup], writes=[tP[bD]], inc=(d_ == 1))
                for d_, (lo_c, trc) in enumerate([(0, 256), (64, 384)]):
                    oc = 128 + 64 * d_
                    S.op("pe", MM(P[bC][:, oc:oc + 64], tri[:, trc:trc + 128], sp_hi_[:, lo_c:lo_c + 64], True, False),
                         reads=[t_hi, t_setup], writes=[tP[bC]], inc=False)
                    S.op("pe", MM(P[bC][:, oc:oc + 64], tri[:, trc:trc + 128], sp_lo_[:, lo_c:lo_c + 64], False, True),
                         reads=[t_lo, t_setup], writes=[tP[bC]], inc=(d_ == 1))
                for d_ in range(2):
                    cum = P[bD][0:64, 128 * d_:128 * d_ + 128]
                    eq, ek = tmpE_[2 * d_][0:64, :], tmpE_[2 * d_ + 1][0:64, :]
                    S.op("act", ACT(eq, cum, AF.Exp), reads=[tP[bD]], writes=[t_tmpE[2 * d_]])
                    S.op("act", ACT(ek, cum, AF.Exp, scale=-1.0), reads=[tP[bD]], writes=[t_tmpE[2 * d_ + 1]])
                    S.op("dve", STT(qin[d_][0:64, tl], P[bA][0:64, 0:128], 0.125, eq, ALU.mult, ALU.mult),
                         reads=[tP[bA], t_tmpE[2 * d_]], writes=[t_qin[d_]])
                    S.op("dve", TT(kin[d_][0:64, tl], P[bA][0:64, 128:256], ek, ALU.mult),
                         reads=[tP[bA], t_tmpE[2 * d_ + 1]], writes=[t_kin[d_]])
                    last = 127 if d_ == 0 else 0
                    S.op("act", ACT(cl[d_][0:64, n:n + 1], cum[:, last:last + 1], AF.Copy), reads=[tP[bD]], writes=[t_cl[d_]])
                S.op("act", ACT(ekd_, P[bC][:, 128:256], AF.Exp), reads=[tP[bC]], writes=[t_ekd])
                for d_ in range(2):
                    S.op("dve", TT(kout[d_][:, n, :], P[bB][:, 0:64], ekd_[:, 64 * d_:64 * d_ + 64], ALU.mult),
                         reads=[tP[bB], t_ekd], writes=[t_kout[d_]])
                S.op("act", ACT(vtok[:, n, :], P[bB][:, 64:192], AF.Copy), reads=[tP[bB]], writes=[t_v])
            stage(8)
            for d_ in range(2):
                S.op("act", ACT(dec[d_][0:64, :], cl[d_][0:64, :], AF.Exp), reads=[t_cl[d_]], writes=[t_dec[d_]])
                S.op("dve", RSUM(clsum[d_][0:64, :], cl[d_][0:64, :]), reads=[t_cl[d_]], writes=[t_cs[d_]])
                dcol = 128 if d_ == 0 else 257
                S.op("act", ACT(pay[0:64, dcol:dcol + 1], clsum[d_][0:64, :], AF.Exp), reads=[t_cs[d_]], writes=[t_pay])

            def order_of(d_, i):
                return i if d_ == 0 else NT - 1 - i
            stage(9)
            t_payS = [Tile("payf"), Tile("payb")]
            S.op("act", ACT(pay[0:64, 258:264], m01[0:64, 0:6], AF.Copy), reads=[t_setup], writes=[t_pay])
            payS = [pay[0:64, 0:128], pay[0:64, 129:257]]
            for i in range(NT):
                for d_ in range(2):
                    n = order_of(d_, i)
                    bank = (2 * i + d_) % 4
                    S.op("pe", MM(P[bank][0:64, 0:128], kout[d_][:, n, :], vtok[:, n, :], True, True),
                         reads=[t_kout[d_], t_v], writes=[tP[bank]])
                    if i == 0:
                        S.op("act", ACT(payS[d_], P[bank][0:64, 0:128], AF.Copy), reads=[tP[bank]], writes=[t_payS[d_]])
                    else:
                        S.op("dve", STT(payS[d_], payS[d_], dec[d_][0:64, n:n + 1], P[bank][0:64, 0:128], ALU.mult, ALU.add),
                             reads=[t_payS[d_], t_dec[d_], tP[bank]], writes=[t_payS[d_]])
            stage(10)
            t_cci, t_cco, t_g = Tile("cc_in"), Tile("cc_out"), Tile("gbuf")
            S.op("sp", DMA(cc_in.ap(), pay[0:64, :]), reads=[t_pay, t_payS[0], t_payS[1]], writes=[t_cci], dma="gx_o")
            S.op("pool", AG(cc_in.ap().opt(), cc_out.ap().opt()), reads=[t_cci], writes=[t_cco])
            S.op("sp", DMA(gbuf[0:64, :, :], cc_out.ap().rearrange("(r p) n -> p r n", p=64)), reads=[t_cco], writes=[t_g], dma="gx_i")
            t_A, t_tkv, t_dm = [Tile("Af"), Tile("Ab")], Tile("tmpKV"), Tile("Dm")
            for d_ in range(2):
                order = [0, 1, 2, 3] if d_ == 0 else [3, 2, 1, 0]
                mb, kv0, dcol = (0, 0, 128) if d_ == 0 else (8, 129, 257)
                A = Ain[d_][0:64, :]
                for i, r in enumerate(order):
                    m_ap, om_ap = msk[0:64, mb + r:mb + r + 1], msk[0:64, mb + 4 + r:mb + 5 + r]
                    if i == 0:
                        S.op("act", ACT(A, gbuf[0:64, r, kv0:kv0 + 128], AF.Copy, scale=m_ap), reads=[t_g, t_setup], writes=[t_A[d_]])
                        continue
                    S.op("dve", TS(Dm[0:64, 0:1], gbuf[0:64, r, dcol:dcol + 1], m_ap, om_ap, ALU.mult, ALU.add),
                         reads=[t_g, t_setup], writes=[t_dm])
                    S.op("act", ACT(tmpKV[0:64, :], gbuf[0:64, r, kv0:kv0 + 128], AF.Copy, scale=m_ap), reads=[t_g, t_setup], writes=[t_tkv])
                    S.op("dve", STT(A, A, Dm[0:64, 0:1], tmpKV[0:64, :], ALU.mult, ALU.add), reads=[t_A[d_], t_dm, t_tkv], writes=[t_A[d_]])
            stage(11)
            t_Sx = [Tile("Sxf"), Tile("Sxb")]
            for i in range(NT):
                for d_ in range(2):
                    n = order_of(d_, i)
                    bufs = [(Ain[d_][0:64, :], t_A[d_]), (Sx[d_][0:64, :], t_Sx[d_])]
                    (cur, t_cur), (nxt, t_nxt) = bufs[i % 2], bufs[(i + 1) % 2]
                    S.op("act", ACT(Sbf[d_][0:64, n, :], cur, AF.Copy), reads=[t_cur], writes=[t_sbf[d_]])
                    if i == NT - 1:
                        continue
                    bank = (2 * i + d_) % 4
                    S.op("pe", MM(P[bank][0:64, 0:128], kout[d_][:, n, :], vtok[:, n, :], True, True),
                         reads=[t_kout[d_], t_v], writes=[tP[bank]])
                    S.op("dve", STT(nxt, cur, dec[d_][0:64, n:n + 1], P[bank][0:64, 0:128], ALU.mult, ALU.add),
                         reads=[t_cur, t_dec[d_], tP[bank]], writes=[t_nxt])
            stage(12)
            t_sT = [[Tile("sTf0"), Tile("sTb0")], [Tile("sTf1"), Tile("sTb1")]]
            t_osq, t_rso, t_to = [Tile("osq0"), Tile("osq1")], [Tile("rso0"), Tile("rso1")], [Tile("to0"), Tile("to1")]
            for n in range(NT):
                k = n % 2
                pb = 4 * k
                tl = slice(128 * n, 128 * n + 128)
                for d_ in range(2):
                    S.op("pe", MM(P[pb + d_][:, 0:128], kin[d_][0:64, tl], qin[d_][0:64, tl], True, True),
                         reads=[t_kin[d_], t_qin[d_]], writes=[tP[pb + d_]])
                    S.op("dve", TT(sTb[k][d_], P[pb + d_][:, 0:128], m01[:, 128 * d_:128 * d_ + 128], ALU.mult),
                         reads=[tP[pb + d_], t_setup], writes=[t_sT[k][d_]])
                ob = pb + 2
                S.op("pe", MM(P[ob][:, 0:128], vtok[:, n, :], sTb[k][0], True, False), reads=[t_v, t_sT[k][0]], writes=[tP[ob]], inc=False)
                S.op("pe", MM(P[ob][:, 0:128], vtok[:, n, :], sTb[k][1], False, False), reads=[t_v, t_sT[k][1]], writes=[tP[ob]], inc=False)
                S.op("pe", MM(P[ob][:, 0:128], Sbf[0][0:64, n, :], qin[0][0:64, tl], False, False), reads=[t_sbf[0], t_qin[0]], writes=[tP[ob]], inc=False)
                S.op("pe", MM(P[ob][:, 0:128], Sbf[1][0:64, n, :], qin[1][0:64, tl], False, True), reads=[t_sbf[1], t_qin[1]], writes=[tP[ob]])
                S.op("act", ACT(osq[k], P[ob][:, 0:128], AF.Square), reads=[tP[ob]], writes=[t_osq[k]])
                S.op("pe", MM(P[pb + 3][:, 0:128], ones[:], osq[k], True, True), reads=[t_osq[k], t_setup], writes=[tP[pb + 3]])
                rstd_from(P[pb + 3][:, 0:128], rs_o[k], 128, [tP[pb + 3]], t_rso[k])
                S.op("dve", TT(o_tmp[k], P[ob][:, 0:128], rs_o[k], ALU.mult), reads=[tP[ob], t_rso[k]], writes=[t_to[k]])
                S.op("dve", STT(yT[:, 4 + h, tl], o_tmp[k], ghn[:, l:l + 1], yT[:, 4 + h, tl], ALU.mult, ALU.mult),
                     reads=[t_to[k], t_setup, t_y], writes=[t_y])

        def mixer(l):
            S.fence()
            stage(1)
            rmsnorm_to_h(0, l)
            stage(2)
            t_wlr, t_y = Tile("wlr"), Tile("yT")
            S.op("pool", DMA(Wlr, wblk(win32_d, l, KC)), writes=[t_wlr], dma="ld_wlr")
            for tg in range(4):
                bank = tg % 2
                proj_fm(bank, Wlr, 32, 1 + 512 * tg, 512, t_wlr)
                S.op("act", ACT(lrT[0:32, 512 * tg:512 * tg + 512], P[bank][0:32, 0:512], AF.Copy), reads=[tP[bank]], writes=[t_lrT])
            for h in range(HEADS):
                gla_head(l, h, t_y)
            S.fence()
            stage(13)
            t_wb2, t_wc2, t_wv2 = ([Tile(f"{nm}{i}") for i in range(2)] for nm in ("wb", "wc", "wvv"))
            t_cv, t_acc, t_gcs = Tile("cv"), Tile("acc"), Tile("gcs")
            t_wo = Tile("wout")
            wo = wout_d[l * D:(l + 1) * D, :].rearrange("(c p) n -> p c n", p=128)

            def load_conv_w(cc):
                q_ = cc % 2
                S.op("pool", DMA(Wb[q_], wblk(win128_d, l * 16 + cc, KC)), writes=[t_wb2[q_]], dma=f"ld_wb{q_}")
                S.op("pool", DMA(Wc[q_], wblk(win128_d, l * 16 + 4 + cc, KC)), writes=[t_wc2[q_]], dma=f"ld_wc{q_}")
                S.op("pool", DMA(Wvv[q_], wblk(win128_d, l * 16 + 8 + cc, KC)), writes=[t_wv2[q_]], dma=f"ld_wv{q_}")
            load_conv_w(0)
            load_conv_w(1)
            for c in range(KC):
                S.op("pool", DMA(Wout[:, c, :], wo[:, c, :]), writes=[t_wo], dma="ld_wo")
            for cc in range(4):
                q_ = cc % 2
                Wb_, Wc_, Wvv_, t_wb, t_wc, t_wv = Wb[q_], Wc[q_], Wvv[q_], t_wb2[q_], t_wc2[q_], t_wv2[q_]
                for ti, (c0, w) in enumerate(_col_tiles(TW, 512)):
                    b0 = 2 * (ti % 2)
                    proj_fm(b0, Wc_, 128, c0, w, t_wc)
                    proj_fm(b0 + 1, Wvv_, 128, c0, w, t_wv)
                    S.op("act", ACT(gcs[:, 0:w], P[b0][:, 0:w], AF.Copy), reads=[tP[b0]], writes=[t_gcs])
                    S.op("dve", TT(cv[:, c0:c0 + w], P[b0 + 1][:, 0:w], gcs[:, 0:w], ALU.mult), reads=[tP[b0 + 1], t_gcs], writes=[t_cv])
                wi = (l * 4 + cc) * 3
                S.op("act", ACT(acc, cv[:, 1:1 + SEG], AF.Copy, scale=cva[:, wi + 1:wi + 2]), reads=[t_cv, t_setup], writes=[t_acc])
                S.op("dve", STT(acc, cv[:, 0:SEG], cva[:, wi:wi + 1], acc, ALU.mult, ALU.add), reads=[t_cv, t_setup, t_acc], writes=[t_acc])
                S.op("dve", STT(acc, cv[:, 2:2 + SEG], cva[:, wi + 2:wi + 3], acc, ALU.mult, ALU.add), reads=[t_cv, t_setup, t_acc], writes=[t_acc])
                for tg in range(4):
                    bank = 4 + tg % 2
                    proj_fm(bank, Wb_, 128, 1 + 512 * tg, 512, t_wb)
                    S.op("dve", TT(yT[:, cc, 512 * tg:512 * tg + 512], P[bank][:, 0:512], acc[:, 512 * tg:512 * tg + 512], ALU.mult),
                         reads=[tP[bank], t_acc], writes=[t_y])
                if cc + 2 < 4:
                    load_conv_w(cc + 2)
            S.fence()
            stage(14)
            t_zb2, t_sqz, t_rsz2, t_zt = [Tile("zbuf0"), Tile("zbuf1")], [Tile("sqz0"), Tile("sqz1")], [Tile("rsz0"), Tile("rsz1")], Tile("ztmp")
            for tg in range(8):
                zb, t_zb, rsz, t_rsz = zbuf[tg % 2], t_zb2[tg % 2], rs_z[tg % 2], t_rsz2[tg % 2]
                sbank = 6 + tg % 2
                cs = slice(256 * tg, 256 * tg + 256)
                xs = slice(1 + 256 * tg, 1 + 256 * tg + 256)
                for m in range(KC):
                    bank = m % 6
                    for c in range(KC):
                        S.op("pe", MM(P[bank][:, 0:256], Wout[:, c, 128 * m:128 * m + 128], yT[:, c, cs], c == 0, c == KC - 1),
                             reads=[t_wo, t_y], writes=[tP[bank]], inc=(c == KC - 1))
                    S.op("act", ACT(zb[:, m, :], P[bank][:, 0:256], AF.Copy), reads=[tP[bank]], writes=[t_zb])
                    S.op("act", ACT(sqz[m % 2], P[bank][:, 0:256], AF.Square), reads=[tP[bank]], writes=[t_sqz[m % 2]])
                    for j in ([m - 1] if m >= 1 else []) + ([m] if m == KC - 1 else []):
                        S.op("pe", MM(P[sbank][:, 0:256], ones[:], sqz[j % 2], j == 0, j == KC - 1),
                             reads=[t_sqz[j % 2], t_setup], writes=[tP[sbank]], inc=(j == KC - 1))
                rstd_from(P[sbank][:, 0:256], rsz, D, [tP[sbank]], t_rsz)
                for m in range(KC):
                    S.op("dve", STT(z_tmp, zb[:, m, :], gain(1, l, m), rsz, ALU.mult, ALU.mult), reads=[t_zb, t_setup, t_rsz], writes=[t_zt])
                    S.op("dve", TT(xT[:, m, xs], xT[:, m, xs], z_tmp, ALU.add), reads=[t_zt, t_x], writes=[t_x])
            S.fence()
            stage(15)
            exchange_x(payx, gx)

        def ffn(l, last):
            S.fence()
            rmsnorm_to_h(2, l)
            t_wg, t_wv = [Tile("wg0"), Tile("wg1")], [Tile("wv0"), Tile("wv1")]
            t_wd = [Tile("wd0"), Tile("wd1")]
            t_ag, t_av = [Tile("accg0"), Tile("accg1")], [Tile("accv0"), Tile("accv1")]
            t_a, t_zf, t_sqf, t_rsf, t_ft = Tile("aT"), Tile("zf"), [Tile("sqf0"), Tile("sqf1")], [Tile("rsf0"), Tile("rsf1")], Tile("ftmp")
            it = 0
            for (tok0, subs) in FFN_SUPER:
                for cp in range(NCP):
                    sl = cp % 2
                    S.op("pool", DMA(Wg[sl], wblk(wup_d, l * 2 * NCP + cp, KC)), writes=[t_wg[sl]], dma=f"ld_wg{sl}")
                    S.op("pool", DMA(Wv[sl], wblk(wup_d, l * 2 * NCP + NCP + cp, KC)), writes=[t_wv[sl]], dma=f"ld_wv{sl}")
                    a0 = 0
                    for si, wo_ in enumerate(subs):
                        t0 = tok0 + a0
                        u0, uw = t0, wo_ + 2
                        pb = 2 * (it % 4)
                        par = it % 2
                        it += 1
                        for c in range(KC):
                            S.op("pe", MM(P[pb][:, 0:uw], Wg[sl][:, c, :], hT[:, c, u0:u0 + uw], c == 0, c == KC - 1),
                                 reads=[t_wg[sl], t_h], writes=[tP[pb]], inc=(c == KC - 1))
                        for c in range(KC):
                            S.op("pe", MM(P[pb + 1][:, 0:uw], Wv[sl][:, c, :], hT[:, c, u0:u0 + uw], c == 0, c == KC - 1),
                                 reads=[t_wv[sl], t_h], writes=[tP[pb + 1]], inc=(c == KC - 1))
                        for (bank, accb, t_acc_, chunk) in [(pb, accg[par], t_ag[par], cp), (pb + 1, accv[par], t_av[par], NCP + cp)]:
                            wi = (l * 2 * NCP + chunk) * 3
                            ao = accb[:, 0:wo_]
                            S.op("act", ACT(ao, P[bank][:, 1:1 + wo_], AF.Copy, scale=cvf[:, wi + 1:wi + 2]), reads=[tP[bank], t_setup], writes=[t_acc_])
                            S.op("dve", STT(ao, P[bank][:, 0:wo_], cvf[:, wi:wi + 1], ao, ALU.mult, ALU.add), reads=[tP[bank], t_setup, t_acc_], writes=[t_acc_])
                            S.op("dve", STT(ao, P[bank][:, 2:2 + wo_], cvf[:, wi + 2:wi + 3], ao, ALU.mult, ALU.add), reads=[tP[bank], t_setup, t_acc_], writes=[t_acc_])
                        S.op("act", ACT(accg[par][:, 0:wo_], accg[par][:, 0:wo_], AF.Silu), reads=[t_ag[par]], writes=[t_ag[par]])
                        S.op("dve", TT(aT[:, cp, a0:a0 + wo_], accg[par][:, 0:wo_], accv[par][:, 0:wo_], ALU.mult),
                             reads=[t_ag[par], t_av[par]], writes=[t_a])
                        a0 += wo_
                pend = None

                def stat_mm(m, si, wo_):
                    S.op("pe", MM(P[6 + si][:, 0:wo_], ones[:], sqf[si][:, 0:wo_], m == 0, m == KC - 1),
                         reads=[t_sqf[si], t_setup], writes=[tP[6 + si]], inc=(m == KC - 1))
                for m in range(KC):
                    sl = m % 2
                    S.op("pool", DMA(Wd[sl], wblk(wdn_d, l * KC + m, NCP)), writes=[t_wd[sl]], dma=f"ld_wd{sl}")
                    a0 = 0
                    for si, wo_ in enumerate(subs):
                        bank = 4 + (2 * m + si) % 2
                        for cp in range(NCP):
                            S.op("pe", MM(P[bank][:, 0:wo_], Wd[sl][:, cp, :], aT[:, cp, a0:a0 + wo_], cp == 0, cp == NCP - 1),
                                 reads=[t_wd[sl], t_a], writes=[tP[bank]], inc=(cp == NCP - 1))
                        S.op("act", ACT(zf[:, m, a0:a0 + wo_], P[bank][:, 0:wo_], AF.Copy), reads=[tP[bank]], writes=[t_zf])
                        S.op("act", ACT(sqf[si][:, 0:wo_], P[bank][:, 0:wo_], AF.Square), reads=[tP[bank]], writes=[t_sqf[si]])
                        if pend is not None:
                            stat_mm(*pend)
                        pend = (m, si, wo_)
                        a0 += wo_
                stat_mm(*pend)
                a0 = 0
                for si, wo_ in enumerate(subs):
                    rstd_from(P[6 + si][:, 0:wo_], rs_f[si][:, 0:wo_], D, [tP[6 + si]], t_rsf[si])
                    xs = slice(1 + tok0 + a0, 1 + tok0 + a0 + wo_)
                    for m in range(KC):
                        S.op("dve", STT(f_tmp[:, 0:wo_], zf[:, m, a0:a0 + wo_], gain(3, l, m), rs_f[si][:, 0:wo_], ALU.mult, ALU.mult),
                             reads=[t_zf, t_setup, t_rsf[si]], writes=[t_ft])
                        S.op("dve", TT(xT[:, m, xs], xT[:, m, xs], f_tmp[:, 0:wo_], ALU.add), reads=[t_ft, t_x], writes=[t_x])
                    a0 += wo_
            S.fence()
            if not last:
                exchange_x(payx2, gx2)

        try:
            for l in range(n_layers):
                mixer(l)
                if stop == "mixer" and l == n_layers - 1:
                    break
                ffn(l, last=(l == n_layers - 1))
        except _Cut:
            pass

        S.fence()
        out_v = out_d.rearrange("(c p) t -> p c t", p=128)
        for c in range(KC):
            S.op("sp", DMA(out_v[:, c, :], xT[:, c, 1:1 + SEG]), reads=[t_x], dma="sto")
        S.fence()
        sems = {n: es.enter_context(nc.semaphore(n)) for n in S.sem_names()}
        with nc.Block() as block:
            S.emit(block, sems)
    return nc


def _layout_inputs(x, norm_mix_pre, norm_mix_post, norm_ffn_pre, norm_ffn_post, w_in, conv_a,
                   gate_up_fwd, gate_bias_fwd, gate_up_bwd, gate_bias_bwd, gla_head_norm,
                   w_out, w_up, conv_ffn, w_down, n_layers=DEPTH):
    f = lambda a: np.ascontiguousarray(np.asarray(a, np.float32))
    x = f(x)
    gains = np.stack([f(norm_mix_pre), f(norm_mix_post), f(norm_ffn_pre), f(norm_ffn_post)])
    gains = f(gains.reshape(4 * DEPTH * KC, 128).T)
    ghn = f(f(gla_head_norm).T)
    cva = f(f(conv_a).reshape(DEPTH, 3, 4, 128).transpose(3, 0, 2, 1).reshape(128, DEPTH * 12))
    cvf = f(f(conv_ffn).reshape(DEPTH, 3, 2 * NCP, 128).transpose(3, 0, 2, 1).reshape(128, DEPTH * 132))
    consts = np.zeros((128, 4), np.float32)
    consts[:, 0] = EPS
    ones = np.ones((128, 128), np.float32)
    ones_row = np.ones((1, SEG), np.float32)
    j = np.arange(128)[:, None]
    i = np.arange(128)[None, :]
    sc = np.float32(-1.0 / 16.0)
    tri = np.concatenate([(j <= i) * sc, (j >= i) * sc, (j > i) * sc, (j < i) * sc], axis=1).astype(np.float32)
    m01 = np.concatenate([(j <= i), (j >= i)], axis=1).astype(np.float32)
    gup = np.zeros((DEPTH, HEADS, 33, 128), np.float32)
    guf, gub, gbf, gbb = f(gate_up_fwd), f(gate_up_bwd), f(gate_bias_fwd), f(gate_bias_bwd)
    for l in range(DEPTH):
        for h in range(HEADS):
            hs = slice(64 * h, 64 * h + 64)
            gup[l, h, 0:16, 0:64] = guf[l][:, hs]
            gup[l, h, 16:32, 64:128] = gub[l][:, hs]
            gup[l, h, 32, 0:64] = gbf[l][hs]
            gup[l, h, 32, 64:128] = gbb[l][hs]
    gup = gup.reshape(DEPTH * HEADS * 33, 128)
    L_ = n_layers
    wi = f(w_in)[:L_]

    def blocks(w, nb, bw):
        return f(w.reshape(L_, KC, 128, nb, bw).transpose(0, 3, 2, 1, 4).reshape(L_ * nb * 128, KC * bw))
    w_in128 = blocks(np.concatenate([wi[:, :, 0:OFF_Q], wi[:, :, OFF_GO:OFF_LR]], axis=2), 16, 128)
    w_in64 = blocks(wi[:, :, OFF_Q:OFF_K], HEADS, 64)
    wkv = np.concatenate([wi[:, :, OFF_K:OFF_V].reshape(L_, D, HEADS, 64),
                          wi[:, :, OFF_V:OFF_GO].reshape(L_, D, HEADS, 128)], axis=3).reshape(L_, D, HEADS * 192)
    w_inkv = blocks(wkv, HEADS, 192)
    w_in32 = blocks(wi[:, :, OFF_LR:OFF_LR + 32], 1, 32)
    w_out2 = f(w_out)[:L_].reshape(L_ * D, D)
    w_up2 = blocks(f(w_up)[:L_], 2 * NCP, 128)
    w_dn2 = f(f(w_down)[:L_].reshape(L_, NCP, 128, KC, 128).transpose(0, 3, 2, 1, 4).reshape(L_ * KC * 128, NCP * 128))
    maps = []
    for c in range(NCORES):
        b, s = c // 4, c % 4
        xt = np.zeros((D, TW), np.float32)
        lo, hi = s * SEG - 1, (s + 1) * SEG + 1
        clo, chi = max(lo, 0), min(hi, SEQ)
        xt[:, clo - lo:clo - lo + (chi - clo)] = x[b, clo:chi, :].T
        mk = np.zeros((128, 24), np.float32)
        for r in range(4):
            mk[:, 0 + r] = 1.0 if r < s else 0.0
            mk[:, 4 + r] = 1.0 - mk[:, 0 + r]
            mk[:, 8 + r] = 1.0 if r > s else 0.0
            mk[:, 12 + r] = 1.0 - mk[:, 8 + r]
            mk[:, 16 + r] = 1.0 if r == s - 1 else 0.0
            mk[:, 20 + r] = 1.0 if r == s + 1 else 0.0
        maps.append({"xT": xt, "gains": gains, "ghn": ghn, "cva": cva, "cvf": cvf, "consts": consts, "masks": mk,
                     "ones": ones, "ones_row": ones_row, "tri": tri, "m01": m01, "gup": gup,
                     "w_in128": w_in128, "w_in64": w_in64, "w_inkv": w_inkv, "w_in32": w_in32,
                     "w_out": w_out2, "w_up": w_up2, "w_down": w_dn2})
    return maps


def _gather(res):
    out = np.empty((BATCH, SEQ, D), np.float32)
    for c in range(NCORES):
        b, s = c // 4, c % 4
        out[b, s * SEG:(s + 1) * SEG, :] = res.results[c]["outT"].T
    return out


def kernel(x, norm_mix_pre, norm_mix_post, norm_ffn_pre, norm_ffn_post, w_in, conv_a,
           gate_up_fwd, gate_bias_fwd, gate_up_bwd, gate_bias_bwd, gla_head_norm,
           w_out, w_up, conv_ffn, w_down):
    nc = build_program()
    maps = _layout_inputs(x, norm_mix_pre, norm_mix_post, norm_ffn_pre, norm_ffn_post, w_in, conv_a,
                          gate_up_fwd, gate_bias_fwd, gate_up_bwd, gate_bias_bwd, gla_head_norm,
                          w_out, w_up, conv_ffn, w_down)
    res = run_bass_kernel_spmd(nc, maps, core_ids=list(range(NCORES)))
    return _gather(res)
```

```python
import numpy as np
from contextlib import ExitStack
import concourse.bass as bass
import concourse.mybir as mybir
from concourse.bass_utils import run_bass_kernel_spmd

F32 = mybir.dt.float32
BF16 = mybir.dt.bfloat16
ALU = mybir.AluOpType
AF = mybir.ActivationFunctionType
AX = mybir.AxisListType

D = 1024
DEPTH = 4
BATCH = 2
SEQ = 8192
NCORES = 8
SEG = 2048
TW = SEG + 2
KC = D // 128
EPS = 1e-6
D_IN = 3104
D_FF = 2816
NCP = D_FF // 128
HEADS = 4
NT = SEG // 128
OFF_GB, OFF_GC, OFF_GV, OFF_Q, OFF_K, OFF_V, OFF_GO, OFF_LR = 0, 512, 1024, 1536, 1792, 2048, 2560, 3072
GROUPS = [[0, 1, 2, 3], [4, 5, 6, 7]]
ARENA_WORDS = 23360
FFN_SUPER = [(0, [342, 342]), (684, [341, 341]), (1366, [341, 341])]


class Tile:
    __slots__ = ("name", "w", "r")

    def __init__(self, name):
        self.name = name
        self.w = None
        self.r = {}


class Sched:
    ENG = ("pe", "act", "dve", "pool", "sp")

    def __init__(self):
        self.ops = {e: [] for e in self.ENG}
        self.cnt = {e: 0 for e in self.ENG}
        self.known = {e: {} for e in self.ENG}
        self.dma_cnt = {}
        self.snap = {}

    def op(self, eng, fn, reads=(), writes=(), dma=None, inc=True):
        deps = {}

        def add(ev):
            if ev is None:
                return
            k, v = ev
            if deps.get(k, 0) < v:
                deps[k] = v
        for t in reads:
            add(t.w)
        for t in writes:
            add(t.w)
            for k, v in t.r.items():
                add((k, v))
        kn = self.known[eng]
        need = [(k, v) for k, v in deps.items() if not (k == eng and eng == "pe") and kn.get(k, 0) < v]
        waits = list(need)
        for d_ in need:
            if any(o is not d_ and self.snap.get(o, {}).get(d_[0], 0) >= d_[1] for o in waits):
                waits.remove(d_)
        for k, v in need:
            if kn.get(k, 0) < v:
                kn[k] = v
        for o in waits:
            for k2, v2 in self.snap.get(o, {}).items():
                if kn.get(k2, 0) < v2:
                    kn[k2] = v2
        if dma is not None:
            self.dma_cnt[dma] = self.dma_cnt.get(dma, 0) + 16
            ev = (dma, self.dma_cnt[dma])
            incv = 16
        elif inc:
            self.cnt[eng] += 1
            ev = (eng, self.cnt[eng])
            incv = 1
        else:
            ev = (eng, self.cnt[eng] + 1)
            incv = 0
        if dma is not None or inc:
            self.snap[ev] = dict(kn)
        for t in reads:
            if t.r.get(ev[0], 0) < ev[1]:
                t.r[ev[0]] = ev[1]
        for t in writes:
            t.w = ev
            t.r = {}
        self.ops[eng].append((waits, fn, ev[0], incv))

    def fence(self):
        allv = [(k, v) for k, v in self.cnt.items() if v > 0] + list(self.dma_cnt.items())
        for e in self.ENG:
            kn = self.known[e]
            waits = []
            for k, v in allv:
                if k == e and e == "pe":
                    continue
                if kn.get(k, 0) >= v:
                    continue
                kn[k] = v
                waits.append((k, v))
            if waits:
                self.ops[e].append((waits, None, None, 0))

    def sem_names(self):
        return list(self.ENG) + list(self.dma_cnt.keys())

    def emit(self, block, sems):
        engs = {"pe": block.tensor, "act": block.scalar, "dve": block.vector,
                "pool": block.gpsimd, "sp": block.sync}
        for name in self.ENG:
            ops = self.ops[name]

            def body(engine, ops=ops):
                for waits, fn, evk, incv in ops:
                    for k, v in waits:
                        engine.wait_ge(sems[k], v)
                    if fn is None:
                        continue
                    ins = fn(engine)
                    if incv:
                        ins.then_inc(sems[evk], incv)
            engs[name](body)


def MM(out, lhsT, rhs, start, stop):
    return lambda e: e.matmul(out, lhsT=lhsT, rhs=rhs, start=start, stop=stop)


def ACT(out, in_, func, **kw):
    return lambda e: e.activation(out=out, in_=in_, func=func, **kw)


def TT(out, in0, in1, op):
    return lambda e: e.tensor_tensor(out=out, in0=in0, in1=in1, op=op)


def STT(out, in0, scalar, in1, op0, op1):
    return lambda e: e.scalar_tensor_tensor(out=out, in0=in0, scalar=scalar, in1=in1, op0=op0, op1=op1)


def TS(out, in0, s1, s2, op0, op1):
    return lambda e: e.tensor_scalar(out=out, in0=in0, scalar1=s1, scalar2=s2, op0=op0, op1=op1)


def RSUM(out, in_):
    return lambda e: e.reduce_sum(out=out, in_=in_, axis=AX.X)


def DMA(out, in_):
    return lambda e: e.dma_start(out=out, in_=in_)


def AG(in_ap, out_ap):
    return lambda e: e.collective_compute("AllGather", ALU.bypass, replica_groups=GROUPS, ins=[in_ap], outs=[out_ap])


class Arena:
    def __init__(self, ap, nwords, alias0=False):
        self.ap, self.n, self.off, self.alias0 = ap, nwords, 0, alias0

    def seek(self, off):
        self.off = off

    def _take(self, words):
        words = (words + 7) // 8 * 8
        a = self.off
        self.off += words
        assert self.off <= self.n, f"arena overflow {self.off} > {self.n}"
        if self.alias0:
            return self.ap[:, 0:words]
        return self.ap[:, a:a + words]

    def f32(self, cols):
        return self._take(cols)[:, 0:cols]

    def bf16(self, cols):
        return self._take((cols + 1) // 2).bitcast(BF16)[:, 0:cols]


def _col_tiles(total, width):
    out, c = [], 0
    while c < total:
        w = min(width, total - c)
        out.append((c, w))
        c += w
    return out


class _Cut(Exception):
    pass


def build_program(n_layers=DEPTH, stop=None, cut=None):
    nc = bass.Bass("TRN2", target_bir_lowering=False)

    def din(name, shape):
        return nc.dram_tensor(name, shape, F32, kind="ExternalInput").ap()
    xT_d = din("xT", [D, TW])
    gains_d = din("gains", [128, 4 * DEPTH * KC])
    ghn_d = din("ghn", [128, DEPTH])
    cva_d = din("cva", [128, DEPTH * 12])
    cvf_d = din("cvf", [128, DEPTH * 132])
    cst_d = din("consts", [128, 4])
    msk_d = din("masks", [128, 24])
    ones_d = din("ones", [128, 128])
    onesrow_d = din("ones_row", [1, SEG])
    tri_d = din("tri", [128, 512])
    m01_d = din("m01", [128, 256])
    gup_d = din("gup", [DEPTH * HEADS * 33, 128])
    win128_d = din("w_in128", [n_layers * 16 * 128, KC * 128])
    win64_d = din("w_in64", [n_layers * HEADS * 128, KC * 64])
    winkv_d = din("w_inkv", [n_layers * HEADS * 128, KC * 192])
    win32_d = din("w_in32", [n_layers * 128, KC * 32])
    wout_d = din("w_out", [n_layers * D, D])
    wup_d = din("w_up", [n_layers * 2 * NCP * 128, KC * 128])
    wdn_d = din("w_down", [n_layers * KC * 128, NCP * 128])

    def wblk(d, idx, c):
        return d[idx * 128:(idx + 1) * 128, :].rearrange("p (c n) -> p c n", c=c)
    out_d = nc.dram_tensor("outT", [D, SEG], F32, kind="ExternalOutput").ap()
    cc_in = nc.dram_tensor("cc_in", [64, 264], F32)
    cc_out = nc.dram_tensor("cc_out", [4 * 64, 264], F32)
    cx_in = nc.dram_tensor("cx_in", [128, 16], F32)
    cx_out = nc.dram_tensor("cx_out", [4 * 128, 16], F32)

    S = Sched()

    def stage(k):
        if cut is not None and k > cut:
            raise _Cut()

    with ExitStack() as es:
        def sb(name, shape, dt):
            return es.enter_context(nc.sbuf_tensor(name, shape, dt))

        xT = sb("xT_s", [128, KC, TW], F32)
        hT = sb("hT_s", [128, KC, TW], BF16)
        gains = sb("gains_s", [128, 4 * DEPTH * KC], F32)
        ghn = sb("ghn_s", [128, DEPTH], F32)
        cva = sb("cva_s", [128, DEPTH * 12], F32)
        cvf = sb("cvf_s", [128, DEPTH * 132], F32)
        cst = sb("cst_s", [128, 4], F32)
        msk = sb("msk_s", [128, 24], F32)
        ones = sb("ones_s", [128, 128], BF16)
        tri = sb("tri_s", [128, 512], BF16)
        m01 = sb("m01_s", [128, 256], F32)
        lrT = sb("lrT_s", [33, SEG], BF16)
        sqn = [sb(f"sqn{i}", [128, 512], BF16) for i in range(2)]
        rsn = [sb(f"rsn{i}", [128, 512], F32) for i in range(2)]
        import os
        _small = os.environ.get("KDBG_SMALLARENA") == "1"
        arena_t = sb("arena", [128, 8200 if _small else ARENA_WORDS], F32)
        AR = Arena(arena_t, ARENA_WORDS, alias0=_small)
        if os.environ.get("KDBG_PSUM2") == "1":
            _p2 = [es.enter_context(nc.psum_tensor(f"bank{i}", [128, 512], F32)) for i in range(2)]
            P = [_p2[i % 2] for i in range(8)]
        else:
            P = [es.enter_context(nc.psum_tensor(f"bank{i}", [128, 512], F32)) for i in range(8)]
        tP = [Tile(f"bank{i}") for i in range(8)]
        t_x, t_h, t_setup, t_lrT = Tile("x"), Tile("h"), Tile("setup"), Tile("lrT")
        t_sqn, t_rsn = [Tile("sqn0"), Tile("sqn1")], [Tile("rsn0"), Tile("rsn1")]
        eps_ap = cst[:, 0:1]

        def gain(kind, layer, c):
            i = (kind * DEPTH + layer) * KC + c
            return gains[:, i:i + 1]

        xT_v = xT_d.rearrange("(c p) t -> p c t", p=128)
        for c in range(KC):
            S.op("sp", DMA(xT[:, c, :], xT_v[:, c, :]), writes=[t_x], dma="ldx")
        for dst, src in [(gains, gains_d), (ghn, ghn_d), (cva, cva_d), (cvf, cvf_d), (cst, cst_d), (msk, msk_d), (m01, m01_d)]:
            S.op("sp", DMA(dst[:], src), writes=[t_setup], dma="setup")
        S.op("pool", DMA(ones[:], ones_d), writes=[t_setup], dma="setup_c")
        S.op("pool", DMA(tri[:], tri_d), writes=[t_setup], dma="setup_c")
        import os
        if os.environ.get("KDBG_NOROW") != "1":
            S.op("pool", DMA(lrT[32:33, :], onesrow_d), writes=[t_lrT], dma="setup_c")
        if os.environ.get("KDBG_TOUCH") == "1":
            scr = sb("scr", [128, 64], F32)
            t_scr = Tile("scr")
            for i_, src in enumerate([win128_d, wout_d, wup_d, wdn_d, gup_d]):
                S.op("sp", DMA(scr[:, 8 * i_:8 * i_ + 8], src[0:128, 0:8]), writes=[t_scr], dma="touch")
            S.op("sp", DMA(scr[0:1, 48:56], onesrow_d[0:1, 0:8]), writes=[t_scr], dma="touch")
        S.fence()

        AR.seek(0)
        yT = AR.bf16(KC * SEG).rearrange("p (c t) -> p c t", c=KC)
        HEAD0 = AR.off
        qin = [AR.bf16(SEG) for _ in range(2)]
        kin = [AR.bf16(SEG) for _ in range(2)]
        kout = [AR.bf16(NT * 64).rearrange("p (n d) -> p n d", n=NT) for _ in range(2)]
        vtok = AR.bf16(NT * 128).rearrange("p (n e) -> p n e", n=NT)
        Sbf = [AR.bf16(NT * 128).rearrange("p (n e) -> p n e", n=NT) for _ in range(2)]
        Wq = AR.bf16(KC * 64).rearrange("p (c n) -> p c n", c=KC)
        Wkv = AR.bf16(KC * 192).rearrange("p (c n) -> p c n", c=KC)
        Wgo = AR.bf16(KC * 128).rearrange("p (c n) -> p c n", c=KC)
        Wlr = AR.bf16(KC * 32).rearrange("p (c n) -> p c n", c=KC)
        gup = AR.bf16(128)
        sp32 = [AR.f32(128) for _ in range(2)]
        sp_hi = [AR.bf16(128) for _ in range(2)]
        sp_lo = [AR.bf16(128) for _ in range(2)]
        tmpE = [[AR.f32(128) for _ in range(4)] for _ in range(2)]
        ekd = [AR.f32(128) for _ in range(2)]
        cl = [AR.f32(NT) for _ in range(2)]
        dec = [AR.f32(NT) for _ in range(2)]
        clsum = [AR.f32(1) for _ in range(2)]
        pay = AR.f32(264)
        gbuf = AR.f32(4 * 264).rearrange("p (r n) -> p r n", r=4)
        Ain = [AR.f32(128) for _ in range(2)]
        Sx = [AR.f32(128) for _ in range(2)]
        tmpKV = AR.f32(128)
        Dm = AR.f32(8)
        sTb = [[AR.bf16(128) for _ in range(2)] for _ in range(2)]
        osq = [AR.bf16(128) for _ in range(2)]
        rs_o = [AR.f32(128) for _ in range(2)]
        o_tmp = [AR.f32(128) for _ in range(2)]
        AR.seek(HEAD0)
        Wb = [AR.bf16(KC * 128).rearrange("p (c n) -> p c n", c=KC) for _ in range(2)]
        Wc = [AR.bf16(KC * 128).rearrange("p (c n) -> p c n", c=KC) for _ in range(2)]
        Wvv = [AR.bf16(KC * 128).rearrange("p (c n) -> p c n", c=KC) for _ in range(2)]
        cv = AR.f32(TW)
        acc = AR.f32(SEG)
        gcs = AR.f32(512)
        Wout = AR.bf16(KC * D).rearrange("p (c n) -> p c n", c=KC)
        print("arena: conv scratch + Wout end at", AR.off)
        AR.seek(HEAD0)
        zbuf = [AR.f32(KC * 256).rearrange("p (m t) -> p m t", m=KC) for _ in range(2)]
        sqz = [AR.bf16(256) for _ in range(2)]
        rs_z = [AR.f32(256) for _ in range(2)]
        z_tmp = AR.f32(256)
        payx = AR.f32(16)
        gx = AR.f32(64).rearrange("p (r n) -> p r n", r=4)
        AR.seek(0)
        aT = AR.bf16(NCP * 688).rearrange("p (c t) -> p c t", c=NCP)
        zf = AR.f32(KC * 684).rearrange("p (m t) -> p m t", m=KC)
        Wg = [AR.bf16(KC * 128).rearrange("p (c n) -> p c n", c=KC) for _ in range(2)]
        Wv = [AR.bf16(KC * 128).rearrange("p (c n) -> p c n", c=KC) for _ in range(2)]
        Wd = [AR.bf16(NCP * 128).rearrange("p (c n) -> p c n", c=NCP) for _ in range(2)]
        accg = [AR.f32(344) for _ in range(2)]
        accv = [AR.f32(344) for _ in range(2)]
        sqf = [AR.bf16(344) for _ in range(2)]
        rs_f = [AR.f32(344) for _ in range(2)]
        f_tmp = AR.f32(344)
        payx2 = AR.f32(16)
        gx2 = AR.f32(64).rearrange("p (r n) -> p r n", r=4)

        def rstd_from(ps_ap, out_ap, n, reads, wtile):
            S.op("act", ACT(out_ap, ps_ap, AF.Ln, scale=1.0 / n, bias=eps_ap), reads=list(reads) + [t_setup], writes=[wtile])
            S.op("act", ACT(out_ap, out_ap, AF.Exp, scale=-0.5), reads=[wtile], writes=[wtile])

        def rmsnorm_to_h(kind, layer):
            for ti, (c0, w) in enumerate(_col_tiles(TW, 512)):
                b = ti % 2
                bank = 6 + b
                for c in range(KC):
                    S.op("act", ACT(sqn[c % 2][:, 0:w], xT[:, c, c0:c0 + w], AF.Square), reads=[t_x], writes=[t_sqn[c % 2]])
                    S.op("pe", MM(P[bank][:, 0:w], ones[:], sqn[c % 2][:, 0:w], c == 0, c == KC - 1),
                         reads=[t_sqn[c % 2], t_setup], writes=[tP[bank]])
                rstd_from(P[bank][:, 0:w], rsn[b][:, 0:w], D, [tP[bank]], t_rsn[b])
                for c in range(KC):
                    S.op("dve", STT(hT[:, c, c0:c0 + w], xT[:, c, c0:c0 + w], gain(kind, layer, c), rsn[b][:, 0:w], ALU.mult, ALU.mult),
                         reads=[t_x, t_setup, t_rsn[b]], writes=[t_h])

        def proj_fm(bank, Wt, m, c0, w, wtile):
            for c in range(KC):
                S.op("pe", MM(P[bank][0:m, 0:w], Wt[:, c, :], hT[:, c, c0:c0 + w], c == 0, c == KC - 1),
                     reads=[wtile, t_h], writes=[tP[bank]], inc=(c == KC - 1))

        def exchange_x(pay_ap, g_ap):
            t_px, t_cxi, t_cxo, t_gx = Tile("payx"), Tile("cx_in"), Tile("cx_out"), Tile("gx")
            pv = pay_ap.rearrange("p (c two) -> p c two", two=2)
            S.op("act", ACT(pv[:, :, 0], xT[:, :, 1], AF.Copy), reads=[t_x], writes=[t_px])
            S.op("act", ACT(pv[:, :, 1], xT[:, :, SEG], AF.Copy), reads=[t_x], writes=[t_px])
            S.op("sp", DMA(cx_in.ap(), pay_ap), reads=[t_px], writes=[t_cxi], dma="xch_o")
            S.op("pool", AG(cx_in.ap().opt(), cx_out.ap().opt()), reads=[t_cxi], writes=[t_cxo])
            S.op("sp", DMA(g_ap, cx_out.ap().rearrange("(r p) n -> p r n", p=128)), reads=[t_cxo], writes=[t_gx], dma="xch_i")
            for halo_col, src_two, mbase in [(0, 1, 16), (TW - 1, 0, 20)]:
                for r in range(4):
                    src = g_ap[:, r, :].rearrange("p (c two) -> p c two", two=2)[:, :, src_two]
                    m_ap = msk[:, mbase + r:mbase + r + 1]
                    if r == 0:
                        S.op("act", ACT(xT[:, :, halo_col], src, AF.Copy, scale=m_ap), reads=[t_gx, t_setup], writes=[t_x])
                    else:
                        S.op("dve", STT(xT[:, :, halo_col], src, m_ap, xT[:, :, halo_col], ALU.mult, ALU.add),
                             reads=[t_gx, t_setup, t_x], writes=[t_x])

        def gla_head(l, h, t_y):
            t_wq, t_wkv, t_wgo, t_gup = Tile("wq"), Tile("wkv"), Tile("wgo"), Tile("gup")
            t_qin, t_kin = [Tile("qinf"), Tile("qinb")], [Tile("kinf"), Tile("kinb")]
            t_kout, t_v, t_sbf = [Tile("koutf"), Tile("koutb")], Tile("vtok"), [Tile("sbff"), Tile("sbfb")]
            t_sp2, t_hi2, t_lo2, t_ekd2 = ([Tile(f"{nm}{i}") for i in range(2)] for nm in ("sp", "sphi", "splo", "ekd"))
            t_tmpE2 = [[Tile(f"tmpE{p}{i}") for i in range(4)] for p in range(2)]
            t_cl, t_dec, t_cs, t_pay = [Tile("clf"), Tile("clb")], [Tile("decf"), Tile("decb")], [Tile("csf"), Tile("csb")], Tile("pay")
            stage(3)
            S.op("pool", DMA(Wq, wblk(win64_d, l * HEADS + h, KC)), writes=[t_wq], dma="ld_wq")
            S.op("pool", DMA(Wkv, wblk(winkv_d, l * HEADS + h, KC)), writes=[t_wkv], dma="ld_wkv")
            S.op("pool", DMA(Wgo, wblk(win128_d, l * 16 + 12 + h, KC)), writes=[t_wgo], dma="ld_wgo")
            g0 = (l * HEADS + h) * 33
            S.op("pool", DMA(gup[0:33, :], gup_d[g0:g0 + 33, :]), writes=[t_gup], dma="ld_gup")
            for tg in range(4):
                bank = tg % 2
                proj_fm(bank, Wgo, 128, 1 + 512 * tg, 512, t_wgo)
                S.op("act", ACT(yT[:, 4 + h, 512 * tg:512 * tg + 512], P[bank][:, 0:512], AF.Silu), reads=[tP[bank]], writes=[t_y])
            def gate_part(n):
                par = n % 2
                bC = 4 * par + 2
                sp32_, sp_hi_, sp_lo_ = sp32[par], sp_hi[par], sp_lo[par]
                t_sp, t_hi, t_lo = t_sp2[par], t_hi2[par], t_lo2[par]
                tl = slice(128 * n, 128 * n + 128)
                S.op("pe", MM(P[bC][:, 0:128], lrT[0:33, tl], gup[0:33, :], True, True), reads=[t_lrT, t_gup], writes=[tP[bC]])
                S.op("act", ACT(sp32_, P[bC][:, 0:128], AF.Exp, scale=-1.0), reads=[tP[bC]], writes=[t_sp])
                S.op("act", ACT(sp32_, sp32_, AF.Ln, bias=1.0), reads=[t_sp], writes=[t_sp])
                S.op("act", ACT(sp_hi_, sp32_, AF.Copy), reads=[t_sp], writes=[t_hi])
                S.op("dve", TT(sp_lo_, sp32_, sp_hi_, ALU.subtract), reads=[t_sp, t_hi], writes=[t_lo])
            gate_part(0)
            for n in range(NT):
                par = n % 2
                pb = 4 * par
                bA, bB, bC, bD = pb, pb + 1, pb + 2, pb + 3
                sp32_, sp_hi_, sp_lo_, ekd_, tmpE_ = sp32[par], sp_hi[par], sp_lo[par], ekd[par], tmpE[par]
                t_sp, t_hi, t_lo, t_ekd, t_tmpE = t_sp2[par], t_hi2[par], t_lo2[par], t_ekd2[par], t_tmpE2[par]
                c0 = 1 + 128 * n
                tl = slice(128 * n, 128 * n + 128)
                stage(4 if n == 0 else 7)
                for c in range(KC):
                    S.op("pe", MM(P[bA][0:64, 0:128], Wq[:, c, :], hT[:, c, c0:c0 + 128], c == 0, c == KC - 1),
                         reads=[t_wq, t_h], writes=[tP[bA]], inc=False)
                for c in range(KC):
                    S.op("pe", MM(P[bA][0:64, 128:256], Wkv[:, c, 0:64], hT[:, c, c0:c0 + 128], c == 0, c == KC - 1),
                         reads=[t_wkv, t_h], writes=[tP[bA]], inc=(c == KC - 1))
                for c in range(KC):
                    S.op("pe", MM(P[bB][:, 0:192], hT[:, c, c0:c0 + 128], Wkv[:, c, :], c == 0, c == KC - 1),
                         reads=[t_wkv, t_h], writes=[tP[bB]], inc=(c == KC - 1))
                stage(5 if n == 0 else 7)
                if n + 1 < NT:
                    gate_part(n + 1)
                stage(6 if n == 0 else 7)
                for d_, (lo_c, trc) in enumerate([(0, 0), (64, 128)]):
                    oc = 128 * d_
                    S.op("pe", MM(P[bD][0:64, oc:oc + 128], sp_hi_[:, lo_c:lo_c + 64], tri[:, trc:trc + 128], True, False),
                         reads=[t_hi, t_setup], writes=[tP[bD]], inc=False)
                    S.op("pe", MM(P[bD][0:64, oc:oc + 128], sp_lo_[:, lo_c:lo_c + 64], tri[:, trc:trc + 128], False, True),
                         reads=[t_lo, t_setup], writes=[tP[bD]], inc=(d_ == 1))
                for d_, (lo_c, trc) in enumerate([(0, 256), (64, 384)]):
                    oc = 128 + 64 * d_
                    S.op("pe", MM(P[bC][:, oc:oc + 64], tri[:, trc:trc + 128], sp_hi_[:, lo_c:lo_c + 64], True, False),
                         reads=[t_hi, t_setup], writes=[tP[bC]], inc=False)
                    S.op("pe", MM(P[bC][:, oc:oc + 64], tri[:, trc:trc + 128], sp_lo_[:, lo_c:lo_c + 64], False, True),
                         reads=[t_lo, t_setup], writes=[tP[bC]], inc=(d_ == 1))
                for d_ in range(2):
                    cum = P[bD][0:64, 128 * d_:128 * d_ + 128]
                    eq, ek = tmpE_[2 * d_][0:64, :], tmpE_[2 * d_ + 1][0:64, :]
                    S.op("act", ACT(eq, cum, AF.Exp), reads=[tP[bD]], writes=[t_tmpE[2 * d_]])
                    S.op("act", ACT(ek, cum, AF.Exp, scale=-1.0), reads=[tP[bD]], writes=[t_tmpE[2 * d_ + 1]])
                    S.op("dve", STT(qin[d_][0:64, tl], P[bA][0:64, 0:128], 0.125, eq, ALU.mult, ALU.mult),
                         reads=[tP[bA], t_tmpE[2 * d_]], writes=[t_qin[d_]])
                    S.op("dve", TT(kin[d_][0:64, tl], P[bA][0:64, 128:256], ek, ALU.mult),
                         reads=[tP[bA], t_tmpE[2 * d_ + 1]], writes=[t_kin[d_]])
                    last = 127 if d_ == 0 else 0
                    S.op("act", ACT(cl[d_][0:64, n:n + 1], cum[:, last:last + 1], AF.Copy), reads=[tP[bD]], writes=[t_cl[d_]])
                S.op("act", ACT(ekd_, P[bC][:, 128:256], AF.Exp), reads=[tP[bC]], writes=[t_ekd])
                for d_ in range(2):
                    S.op("dve", TT(kout[d_][:, n, :], P[bB][:, 0:64], ekd_[:, 64 * d_:64 * d_ + 64], ALU.mult),
                         reads=[tP[bB], t_ekd], writes=[t_kout[d_]])
                S.op("act", ACT(vtok[:, n, :], P[bB][:, 64:192], AF.Copy), reads=[tP[bB]], writes=[t_v])
            stage(8)
            for d_ in range(2):
                S.op("act", ACT(dec[d_][0:64, :], cl[d_][0:64, :], AF.Exp), reads=[t_cl[d_]], writes=[t_dec[d_]])
                S.op("dve", RSUM(clsum[d_][0:64, :], cl[d_][0:64, :]), reads=[t_cl[d_]], writes=[t_cs[d_]])
                dcol = 128 if d_ == 0 else 257
                S.op("act", ACT(pay[0:64, dcol:dcol + 1], clsum[d_][0:64, :], AF.Exp), reads=[t_cs[d_]], writes=[t_pay])

            def order_of(d_, i):
                return i if d_ == 0 else NT - 1 - i
            stage(9)
            t_payS = [Tile("payf"), Tile("payb")]
            S.op("act", ACT(pay[0:64, 258:264], m01[0:64, 0:6], AF.Copy), reads=[t_setup], writes=[t_pay])
            payS = [pay[0:64, 0:128], pay[0:64, 129:257]]
            for i in range(NT):
                for d_ in range(2):
                    n = order_of(d_, i)
                    bank = (2 * i + d_) % 4
                    S.op("pe", MM(P[bank][0:64, 0:128], kout[d_][:, n, :], vtok[:, n, :], True, True),
                         reads=[t_kout[d_], t_v], writes=[tP[bank]])
                    if i == 0:
                        S.op("act", ACT(payS[d_], P[bank][0:64, 0:128], AF.Copy), reads=[tP[bank]], writes=[t_payS[d_]])
                    else:
                        S.op("dve", STT(payS[d_], payS[d_], dec[d_][0:64, n:n + 1], P[bank][0:64, 0:128], ALU.mult, ALU.add),
                             reads=[t_payS[d_], t_dec[d_], tP[bank]], writes=[t_payS[d_]])
            stage(10)
            t_cci, t_cco, t_g = Tile("cc_in"), Tile("cc_out"), Tile("gbuf")
            S.op("sp", DMA(cc_in.ap(), pay[0:64, :]), reads=[t_pay, t_payS[0], t_payS[1]], writes=[t_cci], dma="gx_o")
            S.op("pool", AG(cc_in.ap().opt(), cc_out.ap().opt()), reads=[t_cci], writes=[t_cco])
            S.op("sp", DMA(gbuf[0:64, :, :], cc_out.ap().rearrange("(r p) n -> p r n", p=64)), reads=[t_cco], writes=[t_g], dma="gx_i")
            t_A, t_tkv, t_dm = [Tile("Af"), Tile("Ab")], Tile("tmpKV"), Tile("Dm")
            for d_ in range(2):
                order = [0, 1, 2, 3] if d_ == 0 else [3, 2, 1, 0]
                mb, kv0, dcol = (0, 0, 128) if d_ == 0 else (8, 129, 257)
                A = Ain[d_][0:64, :]
                for i, r in enumerate(order):
                    m_ap, om_ap = msk[0:64, mb + r:mb + r + 1], msk[0:64, mb + 4 + r:mb + 5 + r]
                    if i == 0:
                        S.op("act", ACT(A, gbuf[0:64, r, kv0:kv0 + 128], AF.Copy, scale=m_ap), reads=[t_g, t_setup], writes=[t_A[d_]])
                        continue
                    S.op("dve", TS(Dm[0:64, 0:1], gbuf[0:64, r, dcol:dcol + 1], m_ap, om_ap, ALU.mult, ALU.add),
                         reads=[t_g, t_setup], writes=[t_dm])
                    S.op("act", ACT(tmpKV[0:64, :], gbuf[0:64, r, kv0:kv0 + 128], AF.Copy, scale=m_ap), reads=[t_g, t_setup], writes=[t_tkv])
                    S.op("dve", STT(A, A, Dm[0:64, 0:1], tmpKV[0:64, :], ALU.mult, ALU.add), reads=[t_A[d_], t_dm, t_tkv], writes=[t_A[d_]])
            stage(11)
            t_Sx = [Tile("Sxf"), Tile("Sxb")]
            for i in range(NT):
                for d_ in range(2):
                    n = order_of(d_, i)
                    bufs = [(Ain[d_][0:64, :], t_A[d_]), (Sx[d_][0:64, :], t_Sx[d_])]
                    (cur, t_cur), (nxt, t_nxt) = bufs[i % 2], bufs[(i + 1) % 2]
                    S.op("act", ACT(Sbf[d_][0:64, n, :], cur, AF.Copy), reads=[t_cur], writes=[t_sbf[d_]])
                    if i == NT - 1:
                        continue
                    bank = (2 * i + d_) % 4
                    S.op("pe", MM(P[bank][0:64, 0:128], kout[d_][:, n, :], vtok[:, n, :], True, True),
                         reads=[t_kout[d_], t_v], writes=[tP[bank]])
                    S.op("dve", STT(nxt, cur, dec[d_][0:64, n:n + 1], P[bank][0:64, 0:128], ALU.mult, ALU.add),
                         reads=[t_cur, t_dec[d_], tP[bank]], writes=[t_nxt])
            stage(12)
            t_sT = [[Tile("sTf0"), Tile("sTb0")], [Tile("sTf1"), Tile("sTb1")]]
            t_osq, t_rso, t_to = [Tile("osq0"), Tile("osq1")], [Tile("rso0"), Tile("rso1")], [Tile("to0"), Tile("to1")]
            def scores_part(n):
                k = n % 2
                pb = 4 * k
                tl = slice(128 * n, 128 * n + 128)
                for d_ in range(2):
                    S.op("pe", MM(P[pb + d_][:, 0:128], kin[d_][0:64, tl], qin[d_][0:64, tl], True, True),
                         reads=[t_kin[d_], t_qin[d_]], writes=[tP[pb + d_]])
                    S.op("dve", TT(sTb[k][d_], P[pb + d_][:, 0:128], m01[:, 128 * d_:128 * d_ + 128], ALU.mult),
                         reads=[tP[pb + d_], t_setup], writes=[t_sT[k][d_]])

            def norm_part(n):
                k = n % 2
                pb = 4 * k
                ob = pb + 2
                tl = slice(128 * n, 128 * n + 128)
                S.op("pe", MM(P[pb + 3][:, 0:128], ones[:], osq[k], True, True), reads=[t_osq[k], t_setup], writes=[tP[pb + 3]])
                rstd_from(P[pb + 3][:, 0:128], rs_o[k], 128, [tP[pb + 3]], t_rso[k])
                S.op("dve", TT(o_tmp[k], P[ob][:, 0:128], rs_o[k], ALU.mult), reads=[tP[ob], t_rso[k]], writes=[t_to[k]])
                S.op("dve", STT(yT[:, 4 + h, tl], o_tmp[k], ghn[:, l:l + 1], yT[:, 4 + h, tl], ALU.mult, ALU.mult),
                     reads=[t_to[k], t_setup, t_y], writes=[t_y])
            scores_part(0)
            for n in range(NT):
                k = n % 2
                pb = 4 * k
                tl = slice(128 * n, 128 * n + 128)
                if n + 1 < NT:
                    scores_part(n + 1)
                ob = pb + 2
                S.op("pe", MM(P[ob][:, 0:128], vtok[:, n, :], sTb[k][0], True, False), reads=[t_v, t_sT[k][0]], writes=[tP[ob]], inc=False)
                S.op("pe", MM(P[ob][:, 0:128], vtok[:, n, :], sTb[k][1], False, False), reads=[t_v, t_sT[k][1]], writes=[tP[ob]], inc=False)
                S.op("pe", MM(P[ob][:, 0:128], Sbf[0][0:64, n, :], qin[0][0:64, tl], False, False), reads=[t_sbf[0], t_qin[0]], writes=[tP[ob]], inc=False)
                S.op("pe", MM(P[ob][:, 0:128], Sbf[1][0:64, n, :], qin[1][0:64, tl], False, True), reads=[t_sbf[1], t_qin[1]], writes=[tP[ob]])
                S.op("act", ACT(osq[k], P[ob][:, 0:128], AF.Square), reads=[tP[ob]], writes=[t_osq[k]])
                if n >= 1:
                    norm_part(n - 1)
            norm_part(NT - 1)

        def mixer(l):
            S.fence()
            stage(1)
            rmsnorm_to_h(0, l)
            stage(2)
            t_wlr, t_y = Tile("wlr"), Tile("yT")
            S.op("pool", DMA(Wlr, wblk(win32_d, l, KC)), writes=[t_wlr], dma="ld_wlr")
            for tg in range(4):
                bank = tg % 2
                proj_fm(bank, Wlr, 32, 1 + 512 * tg, 512, t_wlr)
                S.op("act", ACT(lrT[0:32, 512 * tg:512 * tg + 512], P[bank][0:32, 0:512], AF.Copy), reads=[tP[bank]], writes=[t_lrT])
            for h in range(HEADS):
                gla_head(l, h, t_y)
            S.fence()
            stage(13)
            t_wb2, t_wc2, t_wv2 = ([Tile(f"{nm}{i}") for i in range(2)] for nm in ("wb", "wc", "wvv"))
            t_cv, t_acc, t_gcs = Tile("cv"), Tile("acc"), Tile("gcs")
            t_wo = Tile("wout")
            wo = wout_d[l * D:(l + 1) * D, :].rearrange("(c p) n -> p c n", p=128)

            def load_conv_w(cc):
                q_ = cc % 2
                S.op("pool", DMA(Wb[q_], wblk(win128_d, l * 16 + cc, KC)), writes=[t_wb2[q_]], dma=f"ld_wb{q_}")
                S.op("pool", DMA(Wc[q_], wblk(win128_d, l * 16 + 4 + cc, KC)), writes=[t_wc2[q_]], dma=f"ld_wc{q_}")
                S.op("pool", DMA(Wvv[q_], wblk(win128_d, l * 16 + 8 + cc, KC)), writes=[t_wv2[q_]], dma=f"ld_wv{q_}")
            load_conv_w(0)
            load_conv_w(1)
            for c in range(KC):
                S.op("pool", DMA(Wout[:, c, :], wo[:, c, :]), writes=[t_wo], dma="ld_wo")
            for cc in range(4):
                q_ = cc % 2
                Wb_, Wc_, Wvv_, t_wb, t_wc, t_wv = Wb[q_], Wc[q_], Wvv[q_], t_wb2[q_], t_wc2[q_], t_wv2[q_]
                for ti, (c0, w) in enumerate(_col_tiles(TW, 512)):
                    b0 = 2 * (ti % 2)
                    proj_fm(b0, Wc_, 128, c0, w, t_wc)
                    proj_fm(b0 + 1, Wvv_, 128, c0, w, t_wv)
                    S.op("act", ACT(gcs[:, 0:w], P[b0][:, 0:w], AF.Copy), reads=[tP[b0]], writes=[t_gcs])
                    S.op("dve", TT(cv[:, c0:c0 + w], P[b0 + 1][:, 0:w], gcs[:, 0:w], ALU.mult), reads=[tP[b0 + 1], t_gcs], writes=[t_cv])
                wi = (l * 4 + cc) * 3
                S.op("act", ACT(acc, cv[:, 1:1 + SEG], AF.Copy, scale=cva[:, wi + 1:wi + 2]), reads=[t_cv, t_setup], writes=[t_acc])
                S.op("dve", STT(acc, cv[:, 0:SEG], cva[:, wi:wi + 1], acc, ALU.mult, ALU.add), reads=[t_cv, t_setup, t_acc], writes=[t_acc])
                S.op("dve", STT(acc, cv[:, 2:2 + SEG], cva[:, wi + 2:wi + 3], acc, ALU.mult, ALU.add), reads=[t_cv, t_setup, t_acc], writes=[t_acc])
                for tg in range(4):
                    bank = 4 + tg % 2
                    proj_fm(bank, Wb_, 128, 1 + 512 * tg, 512, t_wb)
                    S.op("dve", TT(yT[:, cc, 512 * tg:512 * tg + 512], P[bank][:, 0:512], acc[:, 512 * tg:512 * tg + 512], ALU.mult),
                         reads=[tP[bank], t_acc], writes=[t_y])
                if cc + 2 < 4:
                    load_conv_w(cc + 2)
            S.fence()
            stage(14)
            t_zb2, t_sqz, t_rsz2, t_zt = [Tile("zbuf0"), Tile("zbuf1")], [Tile("sqz0"), Tile("sqz1")], [Tile("rsz0"), Tile("rsz1")], Tile("ztmp")
            for tg in range(8):
                zb, t_zb, rsz, t_rsz = zbuf[tg % 2], t_zb2[tg % 2], rs_z[tg % 2], t_rsz2[tg % 2]
                sbank = 6 + tg % 2
                cs = slice(256 * tg, 256 * tg + 256)
                xs = slice(1 + 256 * tg, 1 + 256 * tg + 256)
                for m in range(KC):
                    bank = m % 6
                    for c in range(KC):
                        S.op("pe", MM(P[bank][:, 0:256], Wout[:, c, 128 * m:128 * m + 128], yT[:, c, cs], c == 0, c == KC - 1),
                             reads=[t_wo, t_y], writes=[tP[bank]], inc=(c == KC - 1))
                    S.op("act", ACT(zb[:, m, :], P[bank][:, 0:256], AF.Copy), reads=[tP[bank]], writes=[t_zb])
                    S.op("act", ACT(sqz[m % 2], P[bank][:, 0:256], AF.Square), reads=[tP[bank]], writes=[t_sqz[m % 2]])
                    for j in ([m - 1] if m >= 1 else []) + ([m] if m == KC - 1 else []):
                        S.op("pe", MM(P[sbank][:, 0:256], ones[:], sqz[j % 2], j == 0, j == KC - 1),
                             reads=[t_sqz[j % 2], t_setup], writes=[tP[sbank]], inc=(j == KC - 1))
                rstd_from(P[sbank][:, 0:256], rsz, D, [tP[sbank]], t_rsz)
                for m in range(KC):
                    S.op("dve", STT(z_tmp, zb[:, m, :], gain(1, l, m), rsz, ALU.mult, ALU.mult), reads=[t_zb, t_setup, t_rsz], writes=[t_zt])
                    S.op("dve", TT(xT[:, m, xs], xT[:, m, xs], z_tmp, ALU.add), reads=[t_zt, t_x], writes=[t_x])
            S.fence()
            stage(15)
            exchange_x(payx, gx)

        def ffn(l, last):
            S.fence()
            rmsnorm_to_h(2, l)
            t_wg, t_wv = [Tile("wg0"), Tile("wg1")], [Tile("wv0"), Tile("wv1")]
            t_wd = [Tile("wd0"), Tile("wd1")]
            t_ag, t_av = [Tile("accg0"), Tile("accg1")], [Tile("accv0"), Tile("accv1")]
            t_a, t_zf, t_sqf, t_rsf, t_ft = Tile("aT"), Tile("zf"), [Tile("sqf0"), Tile("sqf1")], [Tile("rsf0"), Tile("rsf1")], Tile("ftmp")
            it = 0
            for (tok0, subs) in FFN_SUPER:
                for cp in range(NCP):
                    sl = cp % 2
                    S.op("pool", DMA(Wg[sl], wblk(wup_d, l * 2 * NCP + cp, KC)), writes=[t_wg[sl]], dma=f"ld_wg{sl}")
                    S.op("pool", DMA(Wv[sl], wblk(wup_d, l * 2 * NCP + NCP + cp, KC)), writes=[t_wv[sl]], dma=f"ld_wv{sl}")
                    a0 = 0
                    for si, wo_ in enumerate(subs):
                        t0 = tok0 + a0
                        u0, uw = t0, wo_ + 2
                        pb = 2 * (it % 4)
                        par = it % 2
                        it += 1
                        for c in range(KC):
                            S.op("pe", MM(P[pb][:, 0:uw], Wg[sl][:, c, :], hT[:, c, u0:u0 + uw], c == 0, c == KC - 1),
                                 reads=[t_wg[sl], t_h], writes=[tP[pb]], inc=(c == KC - 1))
                        for c in range(KC):
                            S.op("pe", MM(P[pb + 1][:, 0:uw], Wv[sl][:, c, :], hT[:, c, u0:u0 + uw], c == 0, c == KC - 1),
                                 reads=[t_wv[sl], t_h], writes=[tP[pb + 1]], inc=(c == KC - 1))
                        for (bank, accb, t_acc_, chunk) in [(pb, accg[par], t_ag[par], cp), (pb + 1, accv[par], t_av[par], NCP + cp)]:
                            wi = (l * 2 * NCP + chunk) * 3
                            ao = accb[:, 0:wo_]
                            S.op("act", ACT(ao, P[bank][:, 1:1 + wo_], AF.Copy, scale=cvf[:, wi + 1:wi + 2]), reads=[tP[bank], t_setup], writes=[t_acc_])
                            S.op("dve", STT(ao, P[bank][:, 0:wo_], cvf[:, wi:wi + 1], ao, ALU.mult, ALU.add), reads=[tP[bank], t_setup, t_acc_], writes=[t_acc_])
                            S.op("dve", STT(ao, P[bank][:, 2:2 + wo_], cvf[:, wi + 2:wi + 3], ao, ALU.mult, ALU.add), reads=[tP[bank], t_setup, t_acc_], writes=[t_acc_])
                        S.op("act", ACT(accg[par][:, 0:wo_], accg[par][:, 0:wo_], AF.Silu), reads=[t_ag[par]], writes=[t_ag[par]])
                        S.op("dve", TT(aT[:, cp, a0:a0 + wo_], accg[par][:, 0:wo_], accv[par][:, 0:wo_], ALU.mult),
                             reads=[t_ag[par], t_av[par]], writes=[t_a])
                        a0 += wo_
                pend = None

                def stat_mm(m, si, wo_):
                    S.op("pe", MM(P[6 + si][:, 0:wo_], ones[:], sqf[si][:, 0:wo_], m == 0, m == KC - 1),
                         reads=[t_sqf[si], t_setup], writes=[tP[6 + si]], inc=(m == KC - 1))
                for m in range(KC):
                    sl = m % 2
                    S.op("pool", DMA(Wd[sl], wblk(wdn_d, l * KC + m, NCP)), writes=[t_wd[sl]], dma=f"ld_wd{sl}")
                    a0 = 0
                    for si, wo_ in enumerate(subs):
                        bank = 4 + (2 * m + si) % 2
                        for cp in range(NCP):
                            S.op("pe", MM(P[bank][:, 0:wo_], Wd[sl][:, cp, :], aT[:, cp, a0:a0 + wo_], cp == 0, cp == NCP - 1),
                                 reads=[t_wd[sl], t_a], writes=[tP[bank]], inc=(cp == NCP - 1))
                        S.op("act", ACT(zf[:, m, a0:a0 + wo_], P[bank][:, 0:wo_], AF.Copy), reads=[tP[bank]], writes=[t_zf])
                        S.op("act", ACT(sqf[si][:, 0:wo_], P[bank][:, 0:wo_], AF.Square), reads=[tP[bank]], writes=[t_sqf[si]])
                        if pend is not None:
                            stat_mm(*pend)
                        pend = (m, si, wo_)
                        a0 += wo_
                stat_mm(*pend)
                a0 = 0
                for si, wo_ in enumerate(subs):
                    rstd_from(P[6 + si][:, 0:wo_], rs_f[si][:, 0:wo_], D, [tP[6 + si]], t_rsf[si])
                    xs = slice(1 + tok0 + a0, 1 + tok0 + a0 + wo_)
                    for m in range(KC):
                        S.op("dve", STT(f_tmp[:, 0:wo_], zf[:, m, a0:a0 + wo_], gain(3, l, m), rs_f[si][:, 0:wo_], ALU.mult, ALU.mult),
                             reads=[t_zf, t_setup, t_rsf[si]], writes=[t_ft])
                        S.op("dve", TT(xT[:, m, xs], xT[:, m, xs], f_tmp[:, 0:wo_], ALU.add), reads=[t_ft, t_x], writes=[t_x])
                    a0 += wo_
            S.fence()
            if not last:
                exchange_x(payx2, gx2)

        try:
            for l in range(n_layers):
                mixer(l)
                if stop == "mixer" and l == n_layers - 1:
                    break
                ffn(l, last=(l == n_layers - 1))
        except _Cut:
            pass

        S.fence()
        out_v = out_d.rearrange("(c p) t -> p c t", p=128)
        for c in range(KC):
            S.op("sp", DMA(out_v[:, c, :], xT[:, c, 1:1 + SEG]), reads=[t_x], dma="sto")
        S.fence()
        sems = {n: es.enter_context(nc.semaphore(n)) for n in S.sem_names()}
        with nc.Block() as block:
            S.emit(block, sems)
    return nc


def _layout_inputs(x, norm_mix_pre, norm_mix_post, norm_ffn_pre, norm_ffn_post, w_in, conv_a,
                   gate_up_fwd, gate_bias_fwd, gate_up_bwd, gate_bias_bwd, gla_head_norm,
                   w_out, w_up, conv_ffn, w_down, n_layers=DEPTH):
    f = lambda a: np.ascontiguousarray(np.asarray(a, np.float32))
    x = f(x)
    gains = np.stack([f(norm_mix_pre), f(norm_mix_post), f(norm_ffn_pre), f(norm_ffn_post)])
    gains = f(gains.reshape(4 * DEPTH * KC, 128).T)
    ghn = f(f(gla_head_norm).T)
    cva = f(f(conv_a).reshape(DEPTH, 3, 4, 128).transpose(3, 0, 2, 1).reshape(128, DEPTH * 12))
    cvf = f(f(conv_ffn).reshape(DEPTH, 3, 2 * NCP, 128).transpose(3, 0, 2, 1).reshape(128, DEPTH * 132))
    consts = np.zeros((128, 4), np.float32)
    consts[:, 0] = EPS
    ones = np.ones((128, 128), np.float32)
    ones_row = np.ones((1, SEG), np.float32)
    j = np.arange(128)[:, None]
    i = np.arange(128)[None, :]
    sc = np.float32(-1.0 / 16.0)
    tri = np.concatenate([(j <= i) * sc, (j >= i) * sc, (j > i) * sc, (j < i) * sc], axis=1).astype(np.float32)
    m01 = np.concatenate([(j <= i), (j >= i)], axis=1).astype(np.float32)
    gup = np.zeros((DEPTH, HEADS, 33, 128), np.float32)
    guf, gub, gbf, gbb = f(gate_up_fwd), f(gate_up_bwd), f(gate_bias_fwd), f(gate_bias_bwd)
    for l in range(DEPTH):
        for h in range(HEADS):
            hs = slice(64 * h, 64 * h + 64)
            gup[l, h, 0:16, 0:64] = guf[l][:, hs]
            gup[l, h, 16:32, 64:128] = gub[l][:, hs]
            gup[l, h, 32, 0:64] = gbf[l][hs]
            gup[l, h, 32, 64:128] = gbb[l][hs]
    gup = gup.reshape(DEPTH * HEADS * 33, 128)
    L_ = n_layers
    wi = f(w_in)[:L_]

    def blocks(w, nb, bw):
        return f(w.reshape(L_, KC, 128, nb, bw).transpose(0, 3, 2, 1, 4).reshape(L_ * nb * 128, KC * bw))
    w_in128 = blocks(np.concatenate([wi[:, :, 0:OFF_Q], wi[:, :, OFF_GO:OFF_LR]], axis=2), 16, 128)
    w_in64 = blocks(wi[:, :, OFF_Q:OFF_K], HEADS, 64)
    wkv = np.concatenate([wi[:, :, OFF_K:OFF_V].reshape(L_, D, HEADS, 64),
                          wi[:, :, OFF_V:OFF_GO].reshape(L_, D, HEADS, 128)], axis=3).reshape(L_, D, HEADS * 192)
    w_inkv = blocks(wkv, HEADS, 192)
    w_in32 = blocks(wi[:, :, OFF_LR:OFF_LR + 32], 1, 32)
    w_out2 = f(w_out)[:L_].reshape(L_ * D, D)
    w_up2 = blocks(f(w_up)[:L_], 2 * NCP, 128)
    w_dn2 = f(f(w_down)[:L_].reshape(L_, NCP, 128, KC, 128).transpose(0, 3, 2, 1, 4).reshape(L_ * KC * 128, NCP * 128))
    maps = []
    for c in range(NCORES):
        b, s = c // 4, c % 4
        xt = np.zeros((D, TW), np.float32)
        lo, hi = s * SEG - 1, (s + 1) * SEG + 1
        clo, chi = max(lo, 0), min(hi, SEQ)
        xt[:, clo - lo:clo - lo + (chi - clo)] = x[b, clo:chi, :].T
        mk = np.zeros((128, 24), np.float32)
        for r in range(4):
            mk[:, 0 + r] = 1.0 if r < s else 0.0
            mk[:, 4 + r] = 1.0 - mk[:, 0 + r]
            mk[:, 8 + r] = 1.0 if r > s else 0.0
            mk[:, 12 + r] = 1.0 - mk[:, 8 + r]
            mk[:, 16 + r] = 1.0 if r == s - 1 else 0.0
            mk[:, 20 + r] = 1.0 if r == s + 1 else 0.0
        maps.append({"xT": xt, "gains": gains, "ghn": ghn, "cva": cva, "cvf": cvf, "consts": consts, "masks": mk,
                     "ones": ones, "ones_row": ones_row, "tri": tri, "m01": m01, "gup": gup,
                     "w_in128": w_in128, "w_in64": w_in64, "w_inkv": w_inkv, "w_in32": w_in32,
                     "w_out": w_out2, "w_up": w_up2, "w_down": w_dn2})
    return maps


def _gather(res):
    out = np.empty((BATCH, SEQ, D), np.float32)
    for c in range(NCORES):
        b, s = c // 4, c % 4
        out[b, s * SEG:(s + 1) * SEG, :] = res.results[c]["outT"].T
    return out


def kernel(x, norm_mix_pre, norm_mix_post, norm_ffn_pre, norm_ffn_post, w_in, conv_a,
           gate_up_fwd, gate_bias_fwd, gate_up_bwd, gate_bias_bwd, gla_head_norm,
           w_out, w_up, conv_ffn, w_down):
    nc = build_program()
    maps = _layout_inputs(x, norm_mix_pre, norm_mix_post, norm_ffn_pre, norm_ffn_post, w_in, conv_a,
                          gate_up_fwd, gate_bias_fwd, gate_up_bwd, gate_bias_bwd, gla_head_norm,
                          w_out, w_up, conv_ffn, w_down)
    res = run_bass_kernel_spmd(nc, maps, core_ids=list(range(NCORES)))
    return _gather(res)
```

```python
import numpy as np
from contextlib import ExitStack
import concourse.bass as bass
import concourse.mybir as mybir
from concourse.bass_utils import run_bass_kernel_spmd

F32 = mybir.dt.float32
BF16 = mybir.dt.bfloat16
ALU = mybir.AluOpType
AF = mybir.ActivationFunctionType
AX = mybir.AxisListType

D = 1024
DEPTH = 4
BATCH = 2
SEQ = 8192
NCORES = 8
SEG = 2048
TW = SEG + 2
KC = D // 128
EPS = 1e-6
D_IN = 3104
D_FF = 2816
NCP = D_FF // 128
HEADS = 4
NT = SEG // 128
OFF_GB, OFF_GC, OFF_GV, OFF_Q, OFF_K, OFF_V, OFF_GO, OFF_LR = 0, 512, 1024, 1536, 1792, 2048, 2560, 3072
GROUPS = [[0, 1, 2, 3], [4, 5, 6, 7]]
ARENA_WORDS = 23360
FFN_SUPER = [(0, [342, 342]), (684, [341, 341]), (1366, [341, 341])]


class Tile:
    __slots__ = ("name", "w", "r")

    def __init__(self, name):
        self.name = name
        self.w = None
        self.r = {}


class Sched:
    ENG = ("pe", "act", "dve", "pool", "sp")

    def __init__(self):
        self.ops = {e: [] for e in self.ENG}
        self.cnt = {e: 0 for e in self.ENG}
        self.known = {e: {} for e in self.ENG}
        self.dma_cnt = {}
        self.snap = {}

    def op(self, eng, fn, reads=(), writes=(), dma=None, inc=True):
        deps = {}

        def add(ev):
            if ev is None:
                return
            k, v = ev
            if deps.get(k, 0) < v:
                deps[k] = v
        for t in reads:
            add(t.w)
        for t in writes:
            add(t.w)
            for k, v in t.r.items():
                add((k, v))
        kn = self.known[eng]
        need = [(k, v) for k, v in deps.items() if not (k == eng and eng == "pe") and kn.get(k, 0) < v]
        waits = list(need)
        for d_ in need:
            if any(o is not d_ and self.snap.get(o, {}).get(d_[0], 0) >= d_[1] for o in waits):
                waits.remove(d_)
        for k, v in need:
            if kn.get(k, 0) < v:
                kn[k] = v
        for o in waits:
            for k2, v2 in self.snap.get(o, {}).items():
                if kn.get(k2, 0) < v2:
                    kn[k2] = v2
        if dma is not None:
            self.dma_cnt[dma] = self.dma_cnt.get(dma, 0) + 16
            ev = (dma, self.dma_cnt[dma])
            incv = 16
        elif inc:
            self.cnt[eng] += 1
            ev = (eng, self.cnt[eng])
            incv = 1
        else:
            ev = (eng, self.cnt[eng] + 1)
            incv = 0
        if dma is not None or inc:
            self.snap[ev] = dict(kn)
        for t in reads:
            if t.r.get(ev[0], 0) < ev[1]:
                t.r[ev[0]] = ev[1]
        for t in writes:
            t.w = ev
            t.r = {}
        self.ops[eng].append((waits, fn, ev[0], incv))

    def fence(self):
        allv = [(k, v) for k, v in self.cnt.items() if v > 0] + list(self.dma_cnt.items())
        for e in self.ENG:
            kn = self.known[e]
            waits = []
            for k, v in allv:
                if k == e and e == "pe":
                    continue
                if kn.get(k, 0) >= v:
                    continue
                kn[k] = v
                waits.append((k, v))
            if waits:
                self.ops[e].append((waits, None, None, 0))

    def sem_names(self):
        return list(self.ENG) + list(self.dma_cnt.keys())

    def emit(self, block, sems):
        engs = {"pe": block.tensor, "act": block.scalar, "dve": block.vector,
                "pool": block.gpsimd, "sp": block.sync}
        for name in self.ENG:
            ops = self.ops[name]

            def body(engine, ops=ops):
                for waits, fn, evk, incv in ops:
                    for k, v in waits:
                        engine.wait_ge(sems[k], v)
                    if fn is None:
                        continue
                    ins = fn(engine)
                    if incv:
                        ins.then_inc(sems[evk], incv)
            engs[name](body)


def MM(out, lhsT, rhs, start, stop):
    return lambda e: e.matmul(out, lhsT=lhsT, rhs=rhs, start=start, stop=stop)


def ACT(out, in_, func, **kw):
    return lambda e: e.activation(out=out, in_=in_, func=func, **kw)


def TT(out, in0, in1, op):
    return lambda e: e.tensor_tensor(out=out, in0=in0, in1=in1, op=op)


def STT(out, in0, scalar, in1, op0, op1):
    return lambda e: e.scalar_tensor_tensor(out=out, in0=in0, scalar=scalar, in1=in1, op0=op0, op1=op1)


def TS(out, in0, s1, s2, op0, op1):
    return lambda e: e.tensor_scalar(out=out, in0=in0, scalar1=s1, scalar2=s2, op0=op0, op1=op1)


def RSUM(out, in_):
    return lambda e: e.reduce_sum(out=out, in_=in_, axis=AX.X)


def DMA(out, in_):
    return lambda e: e.dma_start(out=out, in_=in_)


def AG(in_ap, out_ap):
    return lambda e: e.collective_compute("AllGather", ALU.bypass, replica_groups=GROUPS, ins=[in_ap], outs=[out_ap])


class Arena:
    def __init__(self, ap, nwords, alias0=False):
        self.ap, self.n, self.off, self.alias0 = ap, nwords, 0, alias0

    def seek(self, off):
        self.off = off

    def _take(self, words):
        words = (words + 7) // 8 * 8
        a = self.off
        self.off += words
        assert self.off <= self.n, f"arena overflow {self.off} > {self.n}"
        if self.alias0:
            return self.ap[:, 0:words]
        return self.ap[:, a:a + words]

    def f32(self, cols):
        return self._take(cols)[:, 0:cols]

    def bf16(self, cols):
        return self._take((cols + 1) // 2).bitcast(BF16)[:, 0:cols]


def _col_tiles(total, width):
    out, c = [], 0
    while c < total:
        w = min(width, total - c)
        out.append((c, w))
        c += w
    return out


class _Cut(Exception):
    pass


def build_program(n_layers=DEPTH, stop=None, cut=None):
    nc = bass.Bass("TRN2", target_bir_lowering=False)

    def din(name, shape):
        return nc.dram_tensor(name, shape, F32, kind="ExternalInput").ap()
    xT_d = din("xT", [D, TW])
    gains_d = din("gains", [128, 4 * DEPTH * KC])
    ghn_d = din("ghn", [128, DEPTH])
    cva_d = din("cva", [128, DEPTH * 12])
    cvf_d = din("cvf", [128, DEPTH * 132])
    cst_d = din("consts", [128, 4])
    msk_d = din("masks", [128, 24])
    ones_d = din("ones", [128, 128])
    onesrow_d = din("ones_row", [1, SEG])
    tri_d = din("tri", [128, 512])
    m01_d = din("m01", [128, 256])
    gup_d = din("gup", [DEPTH * HEADS * 33, 128])
    win128_d = din("w_in128", [n_layers * 16 * 128, KC * 128])
    win64_d = din("w_in64", [n_layers * HEADS * 128, KC * 64])
    winkv_d = din("w_inkv", [n_layers * HEADS * 128, KC * 192])
    win32_d = din("w_in32", [n_layers * 128, KC * 32])
    wout_d = din("w_out", [n_layers * D, D])
    wup_d = din("w_up", [n_layers * 2 * NCP * 128, KC * 128])
    wdn_d = din("w_down", [n_layers * KC * 128, NCP * 128])

    def wblk(d, idx, c):
        return d[idx * 128:(idx + 1) * 128, :].rearrange("p (c n) -> p c n", c=c)
    out_d = nc.dram_tensor("outT", [D, SEG], F32, kind="ExternalOutput").ap()
    cc_in = nc.dram_tensor("cc_in", [64, 264], F32)
    cc_out = nc.dram_tensor("cc_out", [4 * 64, 264], F32)
    cx_in = nc.dram_tensor("cx_in", [128, 16], F32)
    cx_out = nc.dram_tensor("cx_out", [4 * 128, 16], F32)

    S = Sched()

    def stage(k):
        if cut is not None and k > cut:
            raise _Cut()

    with ExitStack() as es:
        def sb(name, shape, dt):
            return es.enter_context(nc.sbuf_tensor(name, shape, dt))

        xT = sb("xT_s", [128, KC, TW], F32)
        hT = sb("hT_s", [128, KC, TW], BF16)
        gains = sb("gains_s", [128, 4 * DEPTH * KC], F32)
        ghn = sb("ghn_s", [128, DEPTH], F32)
        cva = sb("cva_s", [128, DEPTH * 12], F32)
        cvf = sb("cvf_s", [128, DEPTH * 132], F32)
        cst = sb("cst_s", [128, 4], F32)
        msk = sb("msk_s", [128, 24], F32)
        ones = sb("ones_s", [128, 128], BF16)
        tri = sb("tri_s", [128, 512], BF16)
        m01 = sb("m01_s", [128, 256], F32)
        lrT = sb("lrT_s", [33, SEG], BF16)
        sqn = [sb(f"sqn{i}", [128, 512], BF16) for i in range(2)]
        rsn = [sb(f"rsn{i}", [128, 512], F32) for i in range(2)]
        import os
        _small = os.environ.get("KDBG_SMALLARENA") == "1"
        arena_t = sb("arena", [128, 8200 if _small else ARENA_WORDS], F32)
        AR = Arena(arena_t, ARENA_WORDS, alias0=_small)
        if os.environ.get("KDBG_PSUM2") == "1":
            _p2 = [es.enter_context(nc.psum_tensor(f"bank{i}", [128, 512], F32)) for i in range(2)]
            P = [_p2[i % 2] for i in range(8)]
        else:
            P = [es.enter_context(nc.psum_tensor(f"bank{i}", [128, 512], F32)) for i in range(8)]
        tP = [Tile(f"bank{i}") for i in range(8)]
        t_x, t_h, t_setup, t_lrT = Tile("x"), Tile("h"), Tile("setup"), Tile("lrT")
        t_sqn, t_rsn = [Tile("sqn0"), Tile("sqn1")], [Tile("rsn0"), Tile("rsn1")]
        eps_ap = cst[:, 0:1]

        def gain(kind, layer, c):
            i = (kind * DEPTH + layer) * KC + c
            return gains[:, i:i + 1]

        xT_v = xT_d.rearrange("(c p) t -> p c t", p=128)
        for c in range(KC):
            S.op("sp", DMA(xT[:, c, :], xT_v[:, c, :]), writes=[t_x], dma="ldx")
        for dst, src in [(gains, gains_d), (ghn, ghn_d), (cva, cva_d), (cvf, cvf_d), (cst, cst_d), (msk, msk_d), (m01, m01_d)]:
            S.op("sp", DMA(dst[:], src), writes=[t_setup], dma="setup")
        S.op("pool", DMA(ones[:], ones_d), writes=[t_setup], dma="setup_c")
        S.op("pool", DMA(tri[:], tri_d), writes=[t_setup], dma="setup_c")
        import os
        if os.environ.get("KDBG_NOROW") != "1":
            S.op("pool", DMA(lrT[32:33, :], onesrow_d), writes=[t_lrT], dma="setup_c")
        if os.environ.get("KDBG_TOUCH") == "1":
            scr = sb("scr", [128, 64], F32)
            t_scr = Tile("scr")
            for i_, src in enumerate([win128_d, wout_d, wup_d, wdn_d, gup_d]):
                S.op("sp", DMA(scr[:, 8 * i_:8 * i_ + 8], src[0:128, 0:8]), writes=[t_scr], dma="touch")
            S.op("sp", DMA(scr[0:1, 48:56], onesrow_d[0:1, 0:8]), writes=[t_scr], dma="touch")
        S.fence()

        AR.seek(0)
        yT = AR.bf16(KC * SEG).rearrange("p (c t) -> p c t", c=KC)
        HEAD0 = AR.off
        qin = [AR.bf16(SEG) for _ in range(2)]
        kin = [AR.bf16(SEG) for _ in range(2)]
        kout = [AR.bf16(NT * 64).rearrange("p (n d) -> p n d", n=NT) for _ in range(2)]
        vtok = AR.bf16(NT * 128).rearrange("p (n e) -> p n e", n=NT)
        Sbf = [AR.bf16(NT * 128).rearrange("p (n e) -> p n e", n=NT) for _ in range(2)]
        Wq = AR.bf16(KC * 64).rearrange("p (c n) -> p c n", c=KC)
        Wkv = AR.bf16(KC * 192).rearrange("p (c n) -> p c n", c=KC)
        Wgo = AR.bf16(KC * 128).rearrange("p (c n) -> p c n", c=KC)
        Wlr = AR.bf16(KC * 32).rearrange("p (c n) -> p c n", c=KC)
        gup = AR.bf16(128)
        sp32 = [AR.f32(128) for _ in range(2)]
        sp_hi = [AR.bf16(128) for _ in range(2)]
        sp_lo = [AR.bf16(128) for _ in range(2)]
        tmpE = [[AR.f32(128) for _ in range(4)] for _ in range(2)]
        ekd = [AR.f32(128) for _ in range(2)]
        cl = [AR.f32(NT) for _ in range(2)]
        dec = [AR.f32(NT) for _ in range(2)]
        clsum = [AR.f32(1) for _ in range(2)]
        pay = AR.f32(264)
        gbuf = AR.f32(4 * 264).rearrange("p (r n) -> p r n", r=4)
        Ain = [AR.f32(128) for _ in range(2)]
        Sx = [AR.f32(128) for _ in range(2)]
        tmpKV = AR.f32(128)
        Dm = AR.f32(8)
        sTb = [[AR.bf16(128) for _ in range(2)] for _ in range(2)]
        osq = [AR.bf16(128) for _ in range(2)]
        rs_o = [AR.f32(128) for _ in range(2)]
        o_tmp = [AR.f32(128) for _ in range(2)]
        AR.seek(HEAD0)
        Wb = [AR.bf16(KC * 128).rearrange("p (c n) -> p c n", c=KC) for _ in range(2)]
        Wc = [AR.bf16(KC * 128).rearrange("p (c n) -> p c n", c=KC) for _ in range(2)]
        Wvv = [AR.bf16(KC * 128).rearrange("p (c n) -> p c n", c=KC) for _ in range(2)]
        cv = AR.f32(TW)
        acc = [AR.f32(SEG) for _ in range(2)]
        gcs = AR.f32(512)
        Wout = AR.bf16(KC * D).rearrange("p (c n) -> p c n", c=KC)
        print("arena: conv scratch + Wout end at", AR.off)
        AR.seek(HEAD0)
        zbuf = [AR.f32(KC * 256).rearrange("p (m t) -> p m t", m=KC) for _ in range(2)]
        sqz = [AR.bf16(256) for _ in range(2)]
        rs_z = [AR.f32(256) for _ in range(2)]
        z_tmp = AR.f32(256)
        payx = AR.f32(16)
        gx = AR.f32(64).rearrange("p (r n) -> p r n", r=4)
        AR.seek(0)
        aT = AR.bf16(NCP * 688).rearrange("p (c t) -> p c t", c=NCP)
        zf = AR.f32(KC * 684).rearrange("p (m t) -> p m t", m=KC)
        Wg = [AR.bf16(KC * 128).rearrange("p (c n) -> p c n", c=KC) for _ in range(2)]
        Wv = [AR.bf16(KC * 128).rearrange("p (c n) -> p c n", c=KC) for _ in range(2)]
        Wd = [AR.bf16(NCP * 128).rearrange("p (c n) -> p c n", c=NCP) for _ in range(2)]
        accg = [AR.f32(344) for _ in range(2)]
        accv = [AR.f32(344) for _ in range(2)]
        sqf = [AR.bf16(344) for _ in range(2)]
        rs_f = [AR.f32(344) for _ in range(2)]
        f_tmp = AR.f32(344)
        payx2 = AR.f32(16)
        gx2 = AR.f32(64).rearrange("p (r n) -> p r n", r=4)

        def rstd_from(ps_ap, out_ap, n, reads, wtile):
            S.op("act", ACT(out_ap, ps_ap, AF.Ln, scale=1.0 / n, bias=eps_ap), reads=list(reads) + [t_setup], writes=[wtile])
            S.op("act", ACT(out_ap, out_ap, AF.Exp, scale=-0.5), reads=[wtile], writes=[wtile])

        def rmsnorm_to_h(kind, layer):
            for ti, (c0, w) in enumerate(_col_tiles(TW, 512)):
                b = ti % 2
                bank = 6 + b
                for c in range(KC):
                    S.op("act", ACT(sqn[c % 2][:, 0:w], xT[:, c, c0:c0 + w], AF.Square), reads=[t_x], writes=[t_sqn[c % 2]])
                    S.op("pe", MM(P[bank][:, 0:w], ones[:], sqn[c % 2][:, 0:w], c == 0, c == KC - 1),
                         reads=[t_sqn[c % 2], t_setup], writes=[tP[bank]])
                rstd_from(P[bank][:, 0:w], rsn[b][:, 0:w], D, [tP[bank]], t_rsn[b])
                for c in range(KC):
                    S.op("dve", STT(hT[:, c, c0:c0 + w], xT[:, c, c0:c0 + w], gain(kind, layer, c), rsn[b][:, 0:w], ALU.mult, ALU.mult),
                         reads=[t_x, t_setup, t_rsn[b]], writes=[t_h])

        def proj_fm(bank, Wt, m, c0, w, wtile):
            for c in range(KC):
                S.op("pe", MM(P[bank][0:m, 0:w], Wt[:, c, :], hT[:, c, c0:c0 + w], c == 0, c == KC - 1),
                     reads=[wtile, t_h], writes=[tP[bank]], inc=(c == KC - 1))

        def exchange_x(pay_ap, g_ap):
            t_px, t_cxi, t_cxo, t_gx = Tile("payx"), Tile("cx_in"), Tile("cx_out"), Tile("gx")
            pv = pay_ap.rearrange("p (c two) -> p c two", two=2)
            S.op("act", ACT(pv[:, :, 0], xT[:, :, 1], AF.Copy), reads=[t_x], writes=[t_px])
            S.op("act", ACT(pv[:, :, 1], xT[:, :, SEG], AF.Copy), reads=[t_x], writes=[t_px])
            S.op("sp", DMA(cx_in.ap(), pay_ap), reads=[t_px], writes=[t_cxi], dma="xch_o")
            S.op("pool", AG(cx_in.ap().opt(), cx_out.ap().opt()), reads=[t_cxi], writes=[t_cxo])
            S.op("sp", DMA(g_ap, cx_out.ap().rearrange("(r p) n -> p r n", p=128)), reads=[t_cxo], writes=[t_gx], dma="xch_i")
            for halo_col, src_two, mbase in [(0, 1, 16), (TW - 1, 0, 20)]:
                for r in range(4):
                    src = g_ap[:, r, :].rearrange("p (c two) -> p c two", two=2)[:, :, src_two]
                    m_ap = msk[:, mbase + r:mbase + r + 1]
                    if r == 0:
                        S.op("act", ACT(xT[:, :, halo_col], src, AF.Copy, scale=m_ap), reads=[t_gx, t_setup], writes=[t_x])
                    else:
                        S.op("dve", STT(xT[:, :, halo_col], src, m_ap, xT[:, :, halo_col], ALU.mult, ALU.add),
                             reads=[t_gx, t_setup, t_x], writes=[t_x])

        def gla_head(l, h, t_y):
            t_wq, t_wkv, t_wgo, t_gup = Tile("wq"), Tile("wkv"), Tile("wgo"), Tile("gup")
            t_qin, t_kin = [Tile("qinf"), Tile("qinb")], [Tile("kinf"), Tile("kinb")]
            t_kout, t_v, t_sbf = [Tile("koutf"), Tile("koutb")], Tile("vtok"), [Tile("sbff"), Tile("sbfb")]
            t_sp2, t_hi2, t_lo2, t_ekd2 = ([Tile(f"{nm}{i}") for i in range(2)] for nm in ("sp", "sphi", "splo", "ekd"))
            t_tmpE2 = [[Tile(f"tmpE{p}{i}") for i in range(4)] for p in range(2)]
            t_cl, t_dec, t_cs, t_pay = [Tile("clf"), Tile("clb")], [Tile("decf"), Tile("decb")], [Tile("csf"), Tile("csb")], Tile("pay")
            stage(3)
            S.op("pool", DMA(Wq, wblk(win64_d, l * HEADS + h, KC)), writes=[t_wq], dma="ld_wq")
            S.op("pool", DMA(Wkv, wblk(winkv_d, l * HEADS + h, KC)), writes=[t_wkv], dma="ld_wkv")
            S.op("pool", DMA(Wgo, wblk(win128_d, l * 16 + 12 + h, KC)), writes=[t_wgo], dma="ld_wgo")
            g0 = (l * HEADS + h) * 33
            S.op("pool", DMA(gup[0:33, :], gup_d[g0:g0 + 33, :]), writes=[t_gup], dma="ld_gup")
            for tg in range(4):
                bank = tg % 2
                proj_fm(bank, Wgo, 128, 1 + 512 * tg, 512, t_wgo)
                S.op("act", ACT(yT[:, 4 + h, 512 * tg:512 * tg + 512], P[bank][:, 0:512], AF.Silu), reads=[tP[bank]], writes=[t_y])
            def gate_part(n):
                par = n % 2
                bC = 4 * par + 2
                sp32_, sp_hi_, sp_lo_ = sp32[par], sp_hi[par], sp_lo[par]
                t_sp, t_hi, t_lo = t_sp2[par], t_hi2[par], t_lo2[par]
                tl = slice(128 * n, 128 * n + 128)
                S.op("pe", MM(P[bC][:, 0:128], lrT[0:33, tl], gup[0:33, :], True, True), reads=[t_lrT, t_gup], writes=[tP[bC]])
                S.op("act", ACT(sp32_, P[bC][:, 0:128], AF.Exp, scale=-1.0), reads=[tP[bC]], writes=[t_sp])
                S.op("act", ACT(sp32_, sp32_, AF.Ln, bias=1.0), reads=[t_sp], writes=[t_sp])
                S.op("act", ACT(sp_hi_, sp32_, AF.Copy), reads=[t_sp], writes=[t_hi])
                S.op("dve", TT(sp_lo_, sp32_, sp_hi_, ALU.subtract), reads=[t_sp, t_hi], writes=[t_lo])
            gate_part(0)
            for n in range(NT):
                par = n % 2
                pb = 4 * par
                bA, bB, bC, bD = pb, pb + 1, pb + 2, pb + 3
                sp32_, sp_hi_, sp_lo_, ekd_, tmpE_ = sp32[par], sp_hi[par], sp_lo[par], ekd[par], tmpE[par]
                t_sp, t_hi, t_lo, t_ekd, t_tmpE = t_sp2[par], t_hi2[par], t_lo2[par], t_ekd2[par], t_tmpE2[par]
                c0 = 1 + 128 * n
                tl = slice(128 * n, 128 * n + 128)
                stage(4 if n == 0 else 7)
                for c in range(KC):
                    S.op("pe", MM(P[bA][0:64, 0:128], Wq[:, c, :], hT[:, c, c0:c0 + 128], c == 0, c == KC - 1),
                         reads=[t_wq, t_h], writes=[tP[bA]], inc=False)
                for c in range(KC):
                    S.op("pe", MM(P[bA][0:64, 128:256], Wkv[:, c, 0:64], hT[:, c, c0:c0 + 128], c == 0, c == KC - 1),
                         reads=[t_wkv, t_h], writes=[tP[bA]], inc=(c == KC - 1))
                for c in range(KC):
                    S.op("pe", MM(P[bB][:, 0:192], hT[:, c, c0:c0 + 128], Wkv[:, c, :], c == 0, c == KC - 1),
                         reads=[t_wkv, t_h], writes=[tP[bB]], inc=(c == KC - 1))
                stage(5 if n == 0 else 7)
                if n + 1 < NT:
                    gate_part(n + 1)
                stage(6 if n == 0 else 7)
                for d_, (lo_c, trc) in enumerate([(0, 0), (64, 128)]):
                    oc = 128 * d_
                    S.op("pe", MM(P[bD][0:64, oc:oc + 128], sp_hi_[:, lo_c:lo_c + 64], tri[:, trc:trc + 128], True, False),
                         reads=[t_hi, t_setup], writes=[tP[bD]], inc=False)
                    S.op("pe", MM(P[bD][0:64, oc:oc + 128], sp_lo_[:, lo_c:lo_c + 64], tri[:, trc:trc + 128], False, True),
                         reads=[t_lo, t_setup], writes=[tP[bD]], inc=(d_ == 1))
                for d_, (lo_c, trc) in enumerate([(0, 256), (64, 384)]):
                    oc = 128 + 64 * d_
                    S.op("pe", MM(P[bC][:, oc:oc + 64], tri[:, trc:trc + 128], sp_hi_[:, lo_c:lo_c + 64], True, False),
                         reads=[t_hi, t_setup], writes=[tP[bC]], inc=False)
                    S.op("pe", MM(P[bC][:, oc:oc + 64], tri[:, trc:trc + 128], sp_lo_[:, lo_c:lo_c + 64], False, True),
                         reads=[t_lo, t_setup], writes=[tP[bC]], inc=(d_ == 1))
                for d_ in range(2):
                    cum = P[bD][0:64, 128 * d_:128 * d_ + 128]
                    eq, ek = tmpE_[2 * d_][0:64, :], tmpE_[2 * d_ + 1][0:64, :]
                    S.op("act", ACT(eq, cum, AF.Exp), reads=[tP[bD]], writes=[t_tmpE[2 * d_]])
                    S.op("act", ACT(ek, cum, AF.Exp, scale=-1.0), reads=[tP[bD]], writes=[t_tmpE[2 * d_ + 1]])
                    S.op("dve", STT(qin[d_][0:64, tl], P[bA][0:64, 0:128], 0.125, eq, ALU.mult, ALU.mult),
                         reads=[tP[bA], t_tmpE[2 * d_]], writes=[t_qin[d_]])
                    S.op("dve", TT(kin[d_][0:64, tl], P[bA][0:64, 128:256], ek, ALU.mult),
                         reads=[tP[bA], t_tmpE[2 * d_ + 1]], writes=[t_kin[d_]])
                    last = 127 if d_ == 0 else 0
                    S.op("act", ACT(cl[d_][0:64, n:n + 1], cum[:, last:last + 1], AF.Copy), reads=[tP[bD]], writes=[t_cl[d_]])
                S.op("act", ACT(ekd_, P[bC][:, 128:256], AF.Exp), reads=[tP[bC]], writes=[t_ekd])
                for d_ in range(2):
                    S.op("dve", TT(kout[d_][:, n, :], P[bB][:, 0:64], ekd_[:, 64 * d_:64 * d_ + 64], ALU.mult),
                         reads=[tP[bB], t_ekd], writes=[t_kout[d_]])
                S.op("act", ACT(vtok[:, n, :], P[bB][:, 64:192], AF.Copy), reads=[tP[bB]], writes=[t_v])
            stage(8)
            for d_ in range(2):
                S.op("act", ACT(dec[d_][0:64, :], cl[d_][0:64, :], AF.Exp), reads=[t_cl[d_]], writes=[t_dec[d_]])
                S.op("dve", RSUM(clsum[d_][0:64, :], cl[d_][0:64, :]), reads=[t_cl[d_]], writes=[t_cs[d_]])
                dcol = 128 if d_ == 0 else 257
                S.op("act", ACT(pay[0:64, dcol:dcol + 1], clsum[d_][0:64, :], AF.Exp), reads=[t_cs[d_]], writes=[t_pay])

            def order_of(d_, i):
                return i if d_ == 0 else NT - 1 - i
            stage(9)
            t_payS = [Tile("payf"), Tile("payb")]
            S.op("act", ACT(pay[0:64, 258:264], m01[0:64, 0:6], AF.Copy), reads=[t_setup], writes=[t_pay])
            payS = [pay[0:64, 0:128], pay[0:64, 129:257]]
            for i in range(NT):
                for d_ in range(2):
                    n = order_of(d_, i)
                    bank = (2 * i + d_) % 4
                    S.op("pe", MM(P[bank][0:64, 0:128], kout[d_][:, n, :], vtok[:, n, :], True, True),
                         reads=[t_kout[d_], t_v], writes=[tP[bank]])
                    if i == 0:
                        S.op("act", ACT(payS[d_], P[bank][0:64, 0:128], AF.Copy), reads=[tP[bank]], writes=[t_payS[d_]])
                    else:
                        S.op("dve", STT(payS[d_], payS[d_], dec[d_][0:64, n:n + 1], P[bank][0:64, 0:128], ALU.mult, ALU.add),
                             reads=[t_payS[d_], t_dec[d_], tP[bank]], writes=[t_payS[d_]])
            stage(10)
            t_cci, t_cco, t_g = Tile("cc_in"), Tile("cc_out"), Tile("gbuf")
            S.op("sp", DMA(cc_in.ap(), pay[0:64, :]), reads=[t_pay, t_payS[0], t_payS[1]], writes=[t_cci], dma="gx_o")
            S.op("pool", AG(cc_in.ap().opt(), cc_out.ap().opt()), reads=[t_cci], writes=[t_cco])
            S.op("sp", DMA(gbuf[0:64, :, :], cc_out.ap().rearrange("(r p) n -> p r n", p=64)), reads=[t_cco], writes=[t_g], dma="gx_i")
            t_A, t_tkv, t_dm = [Tile("Af"), Tile("Ab")], Tile("tmpKV"), Tile("Dm")
            for d_ in range(2):
                order = [0, 1, 2, 3] if d_ == 0 else [3, 2, 1, 0]
                mb, kv0, dcol = (0, 0, 128) if d_ == 0 else (8, 129, 257)
                A = Ain[d_][0:64, :]
                for i, r in enumerate(order):
                    m_ap, om_ap = msk[0:64, mb + r:mb + r + 1], msk[0:64, mb + 4 + r:mb + 5 + r]
                    if i == 0:
                        S.op("act", ACT(A, gbuf[0:64, r, kv0:kv0 + 128], AF.Copy, scale=m_ap), reads=[t_g, t_setup], writes=[t_A[d_]])
                        continue
                    S.op("dve", TS(Dm[0:64, 0:1], gbuf[0:64, r, dcol:dcol + 1], m_ap, om_ap, ALU.mult, ALU.add),
                         reads=[t_g, t_setup], writes=[t_dm])
                    S.op("act", ACT(tmpKV[0:64, :], gbuf[0:64, r, kv0:kv0 + 128], AF.Copy, scale=m_ap), reads=[t_g, t_setup], writes=[t_tkv])
                    S.op("dve", STT(A, A, Dm[0:64, 0:1], tmpKV[0:64, :], ALU.mult, ALU.add), reads=[t_A[d_], t_dm, t_tkv], writes=[t_A[d_]])
            stage(11)
            t_Sx = [Tile("Sxf"), Tile("Sxb")]
            for i in range(NT):
                for d_ in range(2):
                    n = order_of(d_, i)
                    bufs = [(Ain[d_][0:64, :], t_A[d_]), (Sx[d_][0:64, :], t_Sx[d_])]
                    (cur, t_cur), (nxt, t_nxt) = bufs[i % 2], bufs[(i + 1) % 2]
                    S.op("act", ACT(Sbf[d_][0:64, n, :], cur, AF.Copy), reads=[t_cur], writes=[t_sbf[d_]])
                    if i == NT - 1:
                        continue
                    bank = (2 * i + d_) % 4
                    S.op("pe", MM(P[bank][0:64, 0:128], kout[d_][:, n, :], vtok[:, n, :], True, True),
                         reads=[t_kout[d_], t_v], writes=[tP[bank]])
                    S.op("dve", STT(nxt, cur, dec[d_][0:64, n:n + 1], P[bank][0:64, 0:128], ALU.mult, ALU.add),
                         reads=[t_cur, t_dec[d_], tP[bank]], writes=[t_nxt])
            stage(12)
            t_sT = [[Tile("sTf0"), Tile("sTb0")], [Tile("sTf1"), Tile("sTb1")]]
            t_osq, t_rso, t_to = [Tile("osq0"), Tile("osq1")], [Tile("rso0"), Tile("rso1")], [Tile("to0"), Tile("to1")]
            def scores_part(n):
                k = n % 2
                pb = 4 * k
                tl = slice(128 * n, 128 * n + 128)
                for d_ in range(2):
                    S.op("pe", MM(P[pb + d_][:, 0:128], kin[d_][0:64, tl], qin[d_][0:64, tl], True, True),
                         reads=[t_kin[d_], t_qin[d_]], writes=[tP[pb + d_]])
                    S.op("dve", TT(sTb[k][d_], P[pb + d_][:, 0:128], m01[:, 128 * d_:128 * d_ + 128], ALU.mult),
                         reads=[tP[pb + d_], t_setup], writes=[t_sT[k][d_]])

            def norm_part(n):
                k = n % 2
                pb = 4 * k
                ob = pb + 2
                tl = slice(128 * n, 128 * n + 128)
                S.op("pe", MM(P[pb + 3][:, 0:128], ones[:], osq[k], True, True), reads=[t_osq[k], t_setup], writes=[tP[pb + 3]])
                rstd_from(P[pb + 3][:, 0:128], rs_o[k], 128, [tP[pb + 3]], t_rso[k])
                S.op("dve", TT(o_tmp[k], P[ob][:, 0:128], rs_o[k], ALU.mult), reads=[tP[ob], t_rso[k]], writes=[t_to[k]])
                S.op("dve", STT(yT[:, 4 + h, tl], o_tmp[k], ghn[:, l:l + 1], yT[:, 4 + h, tl], ALU.mult, ALU.mult),
                     reads=[t_to[k], t_setup, t_y], writes=[t_y])
            scores_part(0)
            for n in range(NT):
                k = n % 2
                pb = 4 * k
                tl = slice(128 * n, 128 * n + 128)
                if n + 1 < NT:
                    scores_part(n + 1)
                ob = pb + 2
                S.op("pe", MM(P[ob][:, 0:128], vtok[:, n, :], sTb[k][0], True, False), reads=[t_v, t_sT[k][0]], writes=[tP[ob]], inc=False)
                S.op("pe", MM(P[ob][:, 0:128], vtok[:, n, :], sTb[k][1], False, False), reads=[t_v, t_sT[k][1]], writes=[tP[ob]], inc=False)
                S.op("pe", MM(P[ob][:, 0:128], Sbf[0][0:64, n, :], qin[0][0:64, tl], False, False), reads=[t_sbf[0], t_qin[0]], writes=[tP[ob]], inc=False)
                S.op("pe", MM(P[ob][:, 0:128], Sbf[1][0:64, n, :], qin[1][0:64, tl], False, True), reads=[t_sbf[1], t_qin[1]], writes=[tP[ob]])
                S.op("act", ACT(osq[k], P[ob][:, 0:128], AF.Square), reads=[tP[ob]], writes=[t_osq[k]])
                if n >= 1:
                    norm_part(n - 1)
            norm_part(NT - 1)

        def mixer(l):
            S.fence()
            stage(1)
            rmsnorm_to_h(0, l)
            stage(2)
            t_wlr, t_y = Tile("wlr"), Tile("yT")
            S.op("pool", DMA(Wlr, wblk(win32_d, l, KC)), writes=[t_wlr], dma="ld_wlr")
            for tg in range(4):
                bank = tg % 2
                proj_fm(bank, Wlr, 32, 1 + 512 * tg, 512, t_wlr)
                S.op("act", ACT(lrT[0:32, 512 * tg:512 * tg + 512], P[bank][0:32, 0:512], AF.Copy), reads=[tP[bank]], writes=[t_lrT])
            for h in range(HEADS):
                gla_head(l, h, t_y)
            S.fence()
            stage(13)
            t_wb2, t_wc2, t_wv2 = ([Tile(f"{nm}{i}") for i in range(2)] for nm in ("wb", "wc", "wvv"))
            t_cv, t_acc2, t_gcs = Tile("cv"), [Tile("acc0"), Tile("acc1")], Tile("gcs")
            t_wo = Tile("wout")
            wo = wout_d[l * D:(l + 1) * D, :].rearrange("(c p) n -> p c n", p=128)

            def load_conv_b(cc):
                q_ = cc % 2
                S.op("pool", DMA(Wb[q_], wblk(win128_d, l * 16 + cc, KC)), writes=[t_wb2[q_]], dma=f"ld_wb{q_}")

            def load_conv_w(cc):
                q_ = cc % 2
                S.op("pool", DMA(Wc[q_], wblk(win128_d, l * 16 + 4 + cc, KC)), writes=[t_wc2[q_]], dma=f"ld_wc{q_}")
                S.op("pool", DMA(Wvv[q_], wblk(win128_d, l * 16 + 8 + cc, KC)), writes=[t_wv2[q_]], dma=f"ld_wv{q_}")
            load_conv_w(0)
            load_conv_b(0)
            load_conv_w(1)
            load_conv_b(1)

            def wb_block(cc):
                q_ = cc % 2
                for tg in range(4):
                    bank = 4 + tg % 2
                    proj_fm(bank, Wb[q_], 128, 1 + 512 * tg, 512, t_wb2[q_])
                    S.op("dve", TT(yT[:, cc, 512 * tg:512 * tg + 512], P[bank][:, 0:512], acc[q_][:, 512 * tg:512 * tg + 512], ALU.mult),
                         reads=[tP[bank], t_acc2[q_]], writes=[t_y])
            for c in range(KC):
                S.op("pool", DMA(Wout[:, c, :], wo[:, c, :]), writes=[t_wo], dma="ld_wo")
            for cc in range(4):
                q_ = cc % 2
                Wb_, Wc_, Wvv_, t_wb, t_wc, t_wv = Wb[q_], Wc[q_], Wvv[q_], t_wb2[q_], t_wc2[q_], t_wv2[q_]
                for ti, (c0, w) in enumerate(_col_tiles(TW, 512)):
                    b0 = 2 * (ti % 2)
                    proj_fm(b0, Wc_, 128, c0, w, t_wc)
                    proj_fm(b0 + 1, Wvv_, 128, c0, w, t_wv)
                    S.op("act", ACT(gcs[:, 0:w], P[b0][:, 0:w], AF.Copy), reads=[tP[b0]], writes=[t_gcs])
                    S.op("dve", TT(cv[:, c0:c0 + w], P[b0 + 1][:, 0:w], gcs[:, 0:w], ALU.mult), reads=[tP[b0 + 1], t_gcs], writes=[t_cv])
                wi = (l * 4 + cc) * 3
                acc_, t_acc = acc[q_], t_acc2[q_]
                S.op("act", ACT(acc_, cv[:, 1:1 + SEG], AF.Copy, scale=cva[:, wi + 1:wi + 2]), reads=[t_cv, t_setup], writes=[t_acc])
                S.op("dve", STT(acc_, cv[:, 0:SEG], cva[:, wi:wi + 1], acc_, ALU.mult, ALU.add), reads=[t_cv, t_setup, t_acc], writes=[t_acc])
                S.op("dve", STT(acc_, cv[:, 2:2 + SEG], cva[:, wi + 2:wi + 3], acc_, ALU.mult, ALU.add), reads=[t_cv, t_setup, t_acc], writes=[t_acc])
                if cc + 2 < 4:
                    load_conv_w(cc + 2)
                if cc >= 1:
                    wb_block(cc - 1)
                    if cc + 1 < 4:
                        load_conv_b(cc + 1)
            wb_block(3)
            S.fence()
            stage(14)
            t_zb2, t_sqz, t_rsz2, t_zt = [Tile("zbuf0"), Tile("zbuf1")], [Tile("sqz0"), Tile("sqz1")], [Tile("rsz0"), Tile("rsz1")], Tile("ztmp")
            for tg in range(8):
                zb, t_zb, rsz, t_rsz = zbuf[tg % 2], t_zb2[tg % 2], rs_z[tg % 2], t_rsz2[tg % 2]
                sbank = 6 + tg % 2
                cs = slice(256 * tg, 256 * tg + 256)
                xs = slice(1 + 256 * tg, 1 + 256 * tg + 256)
                for m in range(KC):
                    bank = m % 6
                    for c in range(KC):
                        S.op("pe", MM(P[bank][:, 0:256], Wout[:, c, 128 * m:128 * m + 128], yT[:, c, cs], c == 0, c == KC - 1),
                             reads=[t_wo, t_y], writes=[tP[bank]], inc=(c == KC - 1))
                    S.op("act", ACT(zb[:, m, :], P[bank][:, 0:256], AF.Copy), reads=[tP[bank]], writes=[t_zb])
                    S.op("act", ACT(sqz[m % 2], P[bank][:, 0:256], AF.Square), reads=[tP[bank]], writes=[t_sqz[m % 2]])
                    for j in ([m - 1] if m >= 1 else []) + ([m] if m == KC - 1 else []):
                        S.op("pe", MM(P[sbank][:, 0:256], ones[:], sqz[j % 2], j == 0, j == KC - 1),
                             reads=[t_sqz[j % 2], t_setup], writes=[tP[sbank]], inc=(j == KC - 1))
                rstd_from(P[sbank][:, 0:256], rsz, D, [tP[sbank]], t_rsz)
                for m in range(KC):
                    S.op("dve", STT(z_tmp, zb[:, m, :], gain(1, l, m), rsz, ALU.mult, ALU.mult), reads=[t_zb, t_setup, t_rsz], writes=[t_zt])
                    S.op("dve", TT(xT[:, m, xs], xT[:, m, xs], z_tmp, ALU.add), reads=[t_zt, t_x], writes=[t_x])
            S.fence()
            stage(15)
            exchange_x(payx, gx)

        def ffn(l, last):
            S.fence()
            rmsnorm_to_h(2, l)
            t_wg, t_wv = [Tile("wg0"), Tile("wg1")], [Tile("wv0"), Tile("wv1")]
            t_wd = [Tile("wd0"), Tile("wd1")]
            t_ag, t_av = [Tile("accg0"), Tile("accg1")], [Tile("accv0"), Tile("accv1")]
            t_a, t_zf, t_sqf, t_rsf, t_ft = Tile("aT"), Tile("zf"), [Tile("sqf0"), Tile("sqf1")], [Tile("rsf0"), Tile("rsf1")], Tile("ftmp")
            it = 0
            for (tok0, subs) in FFN_SUPER:
                for cp in range(NCP):
                    sl = cp % 2
                    S.op("pool", DMA(Wg[sl], wblk(wup_d, l * 2 * NCP + cp, KC)), writes=[t_wg[sl]], dma=f"ld_wg{sl}")
                    S.op("pool", DMA(Wv[sl], wblk(wup_d, l * 2 * NCP + NCP + cp, KC)), writes=[t_wv[sl]], dma=f"ld_wv{sl}")
                    a0 = 0
                    for si, wo_ in enumerate(subs):
                        t0 = tok0 + a0
                        u0, uw = t0, wo_ + 2
                        pb = 2 * (it % 4)
                        par = it % 2
                        it += 1
                        for c in range(KC):
                            S.op("pe", MM(P[pb][:, 0:uw], Wg[sl][:, c, :], hT[:, c, u0:u0 + uw], c == 0, c == KC - 1),
                                 reads=[t_wg[sl], t_h], writes=[tP[pb]], inc=(c == KC - 1))
                        for c in range(KC):
                            S.op("pe", MM(P[pb + 1][:, 0:uw], Wv[sl][:, c, :], hT[:, c, u0:u0 + uw], c == 0, c == KC - 1),
                                 reads=[t_wv[sl], t_h], writes=[tP[pb + 1]], inc=(c == KC - 1))
                        for (bank, accb, t_acc_, chunk) in [(pb, accg[par], t_ag[par], cp), (pb + 1, accv[par], t_av[par], NCP + cp)]:
                            wi = (l * 2 * NCP + chunk) * 3
                            ao = accb[:, 0:wo_]
                            S.op("act", ACT(ao, P[bank][:, 1:1 + wo_], AF.Copy, scale=cvf[:, wi + 1:wi + 2]), reads=[tP[bank], t_setup], writes=[t_acc_])
                            S.op("dve", STT(ao, P[bank][:, 0:wo_], cvf[:, wi:wi + 1], ao, ALU.mult, ALU.add), reads=[tP[bank], t_setup, t_acc_], writes=[t_acc_])
                            S.op("dve", STT(ao, P[bank][:, 2:2 + wo_], cvf[:, wi + 2:wi + 3], ao, ALU.mult, ALU.add), reads=[tP[bank], t_setup, t_acc_], writes=[t_acc_])
                        S.op("act", ACT(accg[par][:, 0:wo_], accg[par][:, 0:wo_], AF.Silu), reads=[t_ag[par]], writes=[t_ag[par]])
                        S.op("dve", TT(aT[:, cp, a0:a0 + wo_], accg[par][:, 0:wo_], accv[par][:, 0:wo_], ALU.mult),
                             reads=[t_ag[par], t_av[par]], writes=[t_a])
                        a0 += wo_
                pend = None

                def stat_mm(m, si, wo_):
                    S.op("pe", MM(P[6 + si][:, 0:wo_], ones[:], sqf[si][:, 0:wo_], m == 0, m == KC - 1),
                         reads=[t_sqf[si], t_setup], writes=[tP[6 + si]], inc=(m == KC - 1))
                for m in range(KC):
                    sl = m % 2
                    S.op("pool", DMA(Wd[sl], wblk(wdn_d, l * KC + m, NCP)), writes=[t_wd[sl]], dma=f"ld_wd{sl}")
                    a0 = 0
                    for si, wo_ in enumerate(subs):
                        bank = 4 + (2 * m + si) % 2
                        for cp in range(NCP):
                            S.op("pe", MM(P[bank][:, 0:wo_], Wd[sl][:, cp, :], aT[:, cp, a0:a0 + wo_], cp == 0, cp == NCP - 1),
                                 reads=[t_wd[sl], t_a], writes=[tP[bank]], inc=(cp == NCP - 1))
                        S.op("act", ACT(zf[:, m, a0:a0 + wo_], P[bank][:, 0:wo_], AF.Copy), reads=[tP[bank]], writes=[t_zf])
                        S.op("act", ACT(sqf[si][:, 0:wo_], P[bank][:, 0:wo_], AF.Square), reads=[tP[bank]], writes=[t_sqf[si]])
                        if pend is not None:
                            stat_mm(*pend)
                        pend = (m, si, wo_)
                        a0 += wo_
                stat_mm(*pend)
                a0 = 0
                for si, wo_ in enumerate(subs):
                    rstd_from(P[6 + si][:, 0:wo_], rs_f[si][:, 0:wo_], D, [tP[6 + si]], t_rsf[si])
                    xs = slice(1 + tok0 + a0, 1 + tok0 + a0 + wo_)
                    for m in range(KC):
                        S.op("dve", STT(f_tmp[:, 0:wo_], zf[:, m, a0:a0 + wo_], gain(3, l, m), rs_f[si][:, 0:wo_], ALU.mult, ALU.mult),
                             reads=[t_zf, t_setup, t_rsf[si]], writes=[t_ft])
                        S.op("dve", TT(xT[:, m, xs], xT[:, m, xs], f_tmp[:, 0:wo_], ALU.add), reads=[t_ft, t_x], writes=[t_x])
                    a0 += wo_
            S.fence()
            if not last:
                exchange_x(payx2, gx2)

        try:
            for l in range(n_layers):
                mixer(l)
                if stop == "mixer" and l == n_layers - 1:
                    break
                ffn(l, last=(l == n_layers - 1))
        except _Cut:
            pass

        S.fence()
        out_v = out_d.rearrange("(c p) t -> p c t", p=128)
        for c in range(KC):
            S.op("sp", DMA(out_v[:, c, :], xT[:, c, 1:1 + SEG]), reads=[t_x], dma="sto")
        S.fence()
        sems = {n: es.enter_context(nc.semaphore(n)) for n in S.sem_names()}
        with nc.Block() as block:
            S.emit(block, sems)
    return nc


def _layout_inputs(x, norm_mix_pre, norm_mix_post, norm_ffn_pre, norm_ffn_post, w_in, conv_a,
                   gate_up_fwd, gate_bias_fwd, gate_up_bwd, gate_bias_bwd, gla_head_norm,
                   w_out, w_up, conv_ffn, w_down, n_layers=DEPTH):
    f = lambda a: np.ascontiguousarray(np.asarray(a, np.float32))
    x = f(x)
    gains = np.stack([f(norm_mix_pre), f(norm_mix_post), f(norm_ffn_pre), f(norm_ffn_post)])
    gains = f(gains.reshape(4 * DEPTH * KC, 128).T)
    ghn = f(f(gla_head_norm).T)
    cva = f(f(conv_a).reshape(DEPTH, 3, 4, 128).transpose(3, 0, 2, 1).reshape(128, DEPTH * 12))
    cvf = f(f(conv_ffn).reshape(DEPTH, 3, 2 * NCP, 128).transpose(3, 0, 2, 1).reshape(128, DEPTH * 132))
    consts = np.zeros((128, 4), np.float32)
    consts[:, 0] = EPS
    ones = np.ones((128, 128), np.float32)
    ones_row = np.ones((1, SEG), np.float32)
    j = np.arange(128)[:, None]
    i = np.arange(128)[None, :]
    sc = np.float32(-1.0 / 16.0)
    tri = np.concatenate([(j <= i) * sc, (j >= i) * sc, (j > i) * sc, (j < i) * sc], axis=1).astype(np.float32)
    m01 = np.concatenate([(j <= i), (j >= i)], axis=1).astype(np.float32)
    gup = np.zeros((DEPTH, HEADS, 33, 128), np.float32)
    guf, gub, gbf, gbb = f(gate_up_fwd), f(gate_up_bwd), f(gate_bias_fwd), f(gate_bias_bwd)
    for l in range(DEPTH):
        for h in range(HEADS):
            hs = slice(64 * h, 64 * h + 64)
            gup[l, h, 0:16, 0:64] = guf[l][:, hs]
            gup[l, h, 16:32, 64:128] = gub[l][:, hs]
            gup[l, h, 32, 0:64] = gbf[l][hs]
            gup[l, h, 32, 64:128] = gbb[l][hs]
    gup = gup.reshape(DEPTH * HEADS * 33, 128)
    L_ = n_layers
    wi = f(w_in)[:L_]

    def blocks(w, nb, bw):
        return f(w.reshape(L_, KC, 128, nb, bw).transpose(0, 3, 2, 1, 4).reshape(L_ * nb * 128, KC * bw))
    w_in128 = blocks(np.concatenate([wi[:, :, 0:OFF_Q], wi[:, :, OFF_GO:OFF_LR]], axis=2), 16, 128)
    w_in64 = blocks(wi[:, :, OFF_Q:OFF_K], HEADS, 64)
    wkv = np.concatenate([wi[:, :, OFF_K:OFF_V].reshape(L_, D, HEADS, 64),
                          wi[:, :, OFF_V:OFF_GO].reshape(L_, D, HEADS, 128)], axis=3).reshape(L_, D, HEADS * 192)
    w_inkv = blocks(wkv, HEADS, 192)
    w_in32 = blocks(wi[:, :, OFF_LR:OFF_LR + 32], 1, 32)
    w_out2 = f(w_out)[:L_].reshape(L_ * D, D)
    w_up2 = blocks(f(w_up)[:L_], 2 * NCP, 128)
    w_dn2 = f(f(w_down)[:L_].reshape(L_, NCP, 128, KC, 128).transpose(0, 3, 2, 1, 4).reshape(L_ * KC * 128, NCP * 128))
    maps = []
    for c in range(NCORES):
        b, s = c // 4, c % 4
        xt = np.zeros((D, TW), np.float32)
        lo, hi = s * SEG - 1, (s + 1) * SEG + 1
        clo, chi = max(lo, 0), min(hi, SEQ)
        xt[:, clo - lo:clo - lo + (chi - clo)] = x[b, clo:chi, :].T
        mk = np.zeros((128, 24), np.float32)
        for r in range(4):
            mk[:, 0 + r] = 1.0 if r < s else 0.0
            mk[:, 4 + r] = 1.0 - mk[:, 0 + r]
            mk[:, 8 + r] = 1.0 if r > s else 0.0
            mk[:, 12 + r] = 1.0 - mk[:, 8 + r]
            mk[:, 16 + r] = 1.0 if r == s - 1 else 0.0
            mk[:, 20 + r] = 1.0 if r == s + 1 else 0.0
        maps.append({"xT": xt, "gains": gains, "ghn": ghn, "cva": cva, "cvf": cvf, "consts": consts, "masks": mk,
                     "ones": ones, "ones_row": ones_row, "tri": tri, "m01": m01, "gup": gup,
                     "w_in128": w_in128, "w_in64": w_in64, "w_inkv": w_inkv, "w_in32": w_in32,
                     "w_out": w_out2, "w_up": w_up2, "w_down": w_dn2})
    return maps


def _gather(res):
    out = np.empty((BATCH, SEQ, D), np.float32)
    for c in range(NCORES):
        b, s = c // 4, c % 4
        out[b, s * SEG:(s + 1) * SEG, :] = res.results[c]["outT"].T
    return out


def kernel(x, norm_mix_pre, norm_mix_post, norm_ffn_pre, norm_ffn_post, w_in, conv_a,
           gate_up_fwd, gate_bias_fwd, gate_up_bwd, gate_bias_bwd, gla_head_norm,
           w_out, w_up, conv_ffn, w_down):
    nc = build_program()
    maps = _layout_inputs(x, norm_mix_pre, norm_mix_post, norm_ffn_pre, norm_ffn_post, w_in, conv_a,
                          gate_up_fwd, gate_bias_fwd, gate_up_bwd, gate_bias_bwd, gla_head_norm,
                          w_out, w_up, conv_ffn, w_down)
    res = run_bass_kernel_spmd(nc, maps, core_ids=list(range(NCORES)))
    return _gather(res)
```
